# Optimizing a Trainium2 kernel written in Bass

```python
import jax, jax.numpy as jnp
from jax import lax
import numpy as np

D_MODEL = 1024
BATCH = 16
SEQ = 4096
DEPTH = 4

N_MIXERS = 4
EPS = 1e-6
GLA_HEADS = 4
GLA_DK = D_MODEL // 2 // GLA_HEADS
GLA_DV = D_MODEL // GLA_HEADS
GLA_GATE_RANK = 16
GLA_GATE_NORM = 16.0
GLA_CHUNK = 64
CONV_WIDTH = 31
SGU_CHUNK = 128
SGU_GROUPS = 8
SGU_DIM = D_MODEL
HGRN_EXPAND = 128
HGRN_HEADS = D_MODEL // HGRN_EXPAND
HGRN_CHUNK = 64
FFN_HIDDEN = 2816
FFN_CONV_WIDTH = 3

kernel_name = "hybrid_gla_conformer_sgu_hgrn2_trunk"


def n_uses(m):
    return (DEPTH - m + N_MIXERS - 1) // N_MIXERS


def rms_norm(x, g):
    x32 = x.astype(jnp.float32)
    y = x32 * lax.rsqrt(jnp.mean(x32 * x32, axis=-1, keepdims=True) + EPS)
    return (y * g.astype(jnp.float32)).astype(x.dtype)


def layer_norm(x, g, b):
    x32 = x.astype(jnp.float32)
    mu = jnp.mean(x32, axis=-1, keepdims=True)
    xc = x32 - mu
    y = xc * lax.rsqrt(jnp.mean(xc * xc, axis=-1, keepdims=True) + EPS)
    return (y * g.astype(jnp.float32) + b.astype(jnp.float32)).astype(x.dtype)


def causal_depthwise_conv(x, w):
    k_width, ch = w.shape
    return lax.conv_general_dilated(
        x, w.astype(x.dtype)[:, None, :], window_strides=(1,),
        padding=[(k_width - 1, 0)], dimension_numbers=("NWC", "WIO", "NWC"),
        feature_group_count=ch)


def chunk_gated_linear_attention(q, k, v, log_g, chunk):
    f32 = jnp.float32
    bsz, seq, heads, dk = q.shape
    dv = v.shape[-1]
    n = seq // chunk
    qc = q.astype(f32).reshape(bsz, n, chunk, heads, dk)
    kc = k.astype(f32).reshape(bsz, n, chunk, heads, dk)
    vc = v.astype(f32).reshape(bsz, n, chunk, heads, dv)
    b = jnp.cumsum(log_g.astype(f32).reshape(bsz, n, chunk, heads, dk), axis=2)
    b_mid = b[:, :, chunk // 2:chunk // 2 + 1]
    b_last = b[:, :, -1:]
    scores = jnp.einsum('bnihd,bnjhd->bnhij', qc * jnp.exp(b - b_mid), kc * jnp.exp(b_mid - b))
    causal = jnp.tril(jnp.ones((chunk, chunk), dtype=bool))
    scores = jnp.where(causal, scores, 0.0)
    o_intra = jnp.einsum('bnhij,bnjhe->bnihe', scores, vc)
    q_dec = qc * jnp.exp(b)
    k_dec = kc * jnp.exp(b_last - b)
    g_chunk = jnp.exp(b_last[:, :, 0])

    def step(state, xs):
        q_n, k_n, v_n, g_n = xs
        o_n = jnp.einsum('bihd,bhde->bihe', q_n, state)
        state = g_n[..., None] * state + jnp.einsum('bjhd,bjhe->bhde', k_n, v_n)
        return state, o_n

    s0 = jnp.zeros((bsz, heads, dk, dv), f32)
    _, o_inter = lax.scan(step, s0, (jnp.moveaxis(q_dec, 1, 0), jnp.moveaxis(k_dec, 1, 0),
                                     jnp.moveaxis(vc, 1, 0), jnp.moveaxis(g_chunk, 1, 0)))
    o = o_intra + jnp.moveaxis(o_inter, 0, 1)
    return o.reshape(bsz, seq, heads, dv)


def gla_mixer(h, w_in, w_g2, b_g2, norm_g, w_out):
    bsz, seq, _ = h.shape
    dk_t, dv_t = GLA_HEADS * GLA_DK, GLA_HEADS * GLA_DV
    proj = h @ w_in
    q, k, v, r, g_lr = jnp.split(proj, [dk_t, 2 * dk_t, 2 * dk_t + dv_t, 2 * dk_t + 2 * dv_t], axis=-1)
    log_g = jax.nn.log_sigmoid((g_lr @ w_g2 + b_g2).astype(jnp.float32)) / GLA_GATE_NORM
    o = chunk_gated_linear_attention(
        (q * GLA_DK ** -0.5).reshape(bsz, seq, GLA_HEADS, GLA_DK),
        k.reshape(bsz, seq, GLA_HEADS, GLA_DK),
        v.reshape(bsz, seq, GLA_HEADS, GLA_DV),
        log_g.reshape(bsz, seq, GLA_HEADS, GLA_DK), GLA_CHUNK)
    o = rms_norm(o, norm_g).reshape(bsz, seq, dv_t) * jax.nn.silu(r.astype(jnp.float32))
    return o.astype(h.dtype) @ w_out


def conformer_conv_mixer(h, w_in, b_in, w_dw, b_dw, ln_g, ln_b, w_out, b_out):
    a, gate = jnp.split(h @ w_in + b_in, 2, axis=-1)
    y = a * jax.nn.sigmoid(gate)
    y = causal_depthwise_conv(y, w_dw) + b_dw
    y = jax.nn.silu(layer_norm(y, ln_g, ln_b))
    return y @ w_out + b_out


def sgu_mixer(h, w_in, b_in, ln_g, ln_b, w_s, b_s, w_out, b_out):
    bsz, seq, _ = h.shape
    u, v = jnp.split(jax.nn.gelu(h @ w_in + b_in), 2, axis=-1)
    v = layer_norm(v, ln_g, ln_b)
    n = seq // SGU_CHUNK
    v = v.reshape(bsz, n, SGU_CHUNK, SGU_GROUPS, SGU_DIM // SGU_GROUPS)
    causal = jnp.tril(jnp.ones((SGU_CHUNK, SGU_CHUNK), dtype=bool))
    w_causal = jnp.where(causal, w_s, 0.0).astype(v.dtype)
    s = jnp.einsum('gij,bnjgc->bnigc', w_causal, v) + b_s.T.astype(v.dtype)[:, :, None]
    return (u * s.reshape(bsz, seq, SGU_DIM)) @ w_out + b_out


def hgrn2_mixer(h, w_in, lb, norm_g, w_out):
    bsz, seq, _ = h.shape
    q, f, i, g = jnp.split(h @ w_in, 4, axis=-1)
    lb = lb.astype(jnp.float32)
    log_f = jnp.logaddexp(jnp.log(lb), jnp.log1p(-lb) + jax.nn.log_sigmoid(f.astype(jnp.float32)))
    k = -jnp.expm1(log_f)
    shp = (bsz, seq, HGRN_HEADS, HGRN_EXPAND)
    o = chunk_gated_linear_attention(jax.nn.silu(q).reshape(shp), k.reshape(shp),
                                     i.reshape(shp), log_f.reshape(shp), HGRN_CHUNK)
    o = rms_norm(o, norm_g).reshape(bsz, seq, D_MODEL) * jax.nn.silu(g.astype(jnp.float32))
    return o.astype(h.dtype) @ w_out


def conv_ffn(h, w_up, w_dw, w_down):
    z = causal_depthwise_conv(h @ w_up, w_dw)
    gate, val = jnp.split(z, 2, axis=-1)
    return (jax.nn.silu(gate) * val) @ w_down


def setup_inputs(seed: int = 0) -> dict:
    key = jax.random.key(seed)
    ks = iter(jax.random.split(key, 48))
    D = D_MODEL
    na, nb, nc, nd = n_uses(0), n_uses(1), n_uses(2), n_uses(3)

    def nrm(shape, scale):
        return scale * jax.random.normal(next(ks), shape, jnp.float32)

    def gain(shape):
        return 1.0 + nrm(shape, 0.02)

    gla_in = 2 * GLA_HEADS * GLA_DK + 2 * GLA_HEADS * GLA_DV + GLA_GATE_RANK
    return {
        "x": nrm((BATCH, SEQ, D), 1.0),
        "norm_mix": gain((DEPTH, D)),
        "norm_ffn": gain((DEPTH, D)),
        "norm_final": gain((D,)),
        "gla_w_in": nrm((na, D, gla_in), D ** -0.5),
        "gla_w_g2": nrm((na, GLA_GATE_RANK, GLA_HEADS * GLA_DK), GLA_GATE_RANK ** -0.5),
        "gla_b_g2": nrm((na, GLA_HEADS * GLA_DK), 0.01),
        "gla_norm": gain((na, GLA_DV)),
        "gla_w_out": nrm((na, GLA_HEADS * GLA_DV, D), (GLA_HEADS * GLA_DV) ** -0.5),
        "cv_w_in": nrm((nb, D, 2 * D), D ** -0.5),
        "cv_b_in": nrm((nb, 2 * D), 0.01),
        "cv_w_dw": nrm((nb, CONV_WIDTH, D), CONV_WIDTH ** -0.5),
        "cv_b_dw": nrm((nb, D), 0.01),
        "cv_ln_g": gain((nb, D)),
        "cv_ln_b": nrm((nb, D), 0.01),
        "cv_w_out": nrm((nb, D, D), D ** -0.5),
        "cv_b_out": nrm((nb, D), 0.01),
        "sg_w_in": nrm((nc, D, 2 * SGU_DIM), D ** -0.5),
        "sg_b_in": nrm((nc, 2 * SGU_DIM), 0.01),
        "sg_ln_g": gain((nc, SGU_DIM)),
        "sg_ln_b": nrm((nc, SGU_DIM), 0.01),
        "sg_w_s": nrm((nc, SGU_GROUPS, SGU_CHUNK, SGU_CHUNK), SGU_CHUNK ** -0.5),
        "sg_b_s": gain((nc, SGU_GROUPS, SGU_CHUNK)),
        "sg_w_out": nrm((nc, SGU_DIM, D), SGU_DIM ** -0.5),
        "sg_b_out": nrm((nc, D), 0.01),
        "hg_w_in": nrm((nd, D, 4 * D), D ** -0.5),
        "hg_lb_table": nrm((DEPTH, D), 0.1),
        "hg_norm": gain((nd, HGRN_EXPAND)),
        "hg_w_out": nrm((nd, D, D), D ** -0.5),
        "ffn_w_up": nrm((DEPTH, D, 2 * FFN_HIDDEN), D ** -0.5),
        "ffn_w_dw": nrm((DEPTH, FFN_CONV_WIDTH, 2 * FFN_HIDDEN), FFN_CONV_WIDTH ** -0.5),
        "ffn_w_down": nrm((DEPTH, FFN_HIDDEN, D), FFN_HIDDEN ** -0.5),
    }


def reference(x, norm_mix, norm_ffn, norm_final,
              gla_w_in, gla_w_g2, gla_b_g2, gla_norm, gla_w_out,
              cv_w_in, cv_b_in, cv_w_dw, cv_b_dw, cv_ln_g, cv_ln_b, cv_w_out, cv_b_out,
              sg_w_in, sg_b_in, sg_ln_g, sg_ln_b, sg_w_s, sg_b_s, sg_w_out, sg_b_out,
              hg_w_in, hg_lb_table, hg_norm, hg_w_out,
              ffn_w_up, ffn_w_dw, ffn_w_down):
    lb_cum = jnp.cumsum(jax.nn.softmax(hg_lb_table.astype(jnp.float32), axis=0), axis=0)
    lower_bounds = lb_cum - lb_cum[0]
    for layer in range(DEPTH):
        m, j = layer % N_MIXERS, layer // N_MIXERS
        h = rms_norm(x, norm_mix[layer])
        if m == 0:
            y = gla_mixer(h, gla_w_in[j], gla_w_g2[j], gla_b_g2[j], gla_norm[j], gla_w_out[j])
        elif m == 1:
            y = conformer_conv_mixer(h, cv_w_in[j], cv_b_in[j], cv_w_dw[j], cv_b_dw[j],
                                     cv_ln_g[j], cv_ln_b[j], cv_w_out[j], cv_b_out[j])
        elif m == 2:
            y = sgu_mixer(h, sg_w_in[j], sg_b_in[j], sg_ln_g[j], sg_ln_b[j],
                          sg_w_s[j], sg_b_s[j], sg_w_out[j], sg_b_out[j])
        else:
            y = hgrn2_mixer(h, hg_w_in[j], lower_bounds[layer], hg_norm[j], hg_w_out[j])
        x = x + y
        x = x + conv_ffn(rms_norm(x, norm_ffn[layer]), ffn_w_up[layer], ffn_w_dw[layer], ffn_w_down[layer])
    return rms_norm(x, norm_final)
```

```python
import numpy as np
from contextlib import ExitStack
import concourse.bass as bass
import concourse.mybir as mybir
from concourse.bass_utils import run_bass_kernel_spmd

F32 = mybir.dt.float32
BF16 = mybir.dt.bfloat16
AF = mybir.ActivationFunctionType
ALU = mybir.AluOpType

LINV = 0
LINSTOP = 9
LINX = 0
D = 1024
TT = 512
EPS = 1e-6
FH = 2816
NCORES = 8

INPUT_SHAPES = [
    ("norm_mix", [4, 1024]), ("norm_ffn", [4, 1024]), ("norm_final", [1024]),
    ("gla_w_in", [1, 1024, 3088]), ("gla_w_g2", [1, 16, 512]), ("gla_b_g2", [1, 512]),
    ("gla_norm", [1, 256]), ("gla_w_out", [1, 1024, 1024]),
    ("cv_w_in", [1, 1024, 2048]), ("cv_b_in", [1, 2048]), ("cv_w_dw", [1, 31, 1024]),
    ("cv_b_dw", [1, 1024]), ("cv_ln_g", [1, 1024]), ("cv_ln_b", [1, 1024]),
    ("cv_w_out", [1, 1024, 1024]), ("cv_b_out", [1, 1024]),
    ("sg_w_in", [1, 1024, 2048]), ("sg_b_in", [1, 2048]), ("sg_ln_g", [1, 1024]),
    ("sg_ln_b", [1, 1024]), ("sg_w_s", [1, 8, 128, 128]), ("sg_b_s", [1, 8, 128]),
    ("sg_w_out", [1, 1024, 1024]), ("sg_b_out", [1, 1024]),
    ("hg_w_in", [1, 1024, 4096]), ("hg_lb_table", [4, 1024]), ("hg_norm", [1, 128]),
    ("hg_w_out", [1, 1024, 1024]),
    ("ffn_w_up", [4, 1024, 5632]), ("ffn_w_dw", [4, 3, 5632]), ("ffn_w_down", [4, 2816, 1024]),
]


class Buf:
    __slots__ = ("w", "r")

    def __init__(self):
        self.w = None
        self.r = {}


class Prog:
    ENG = ("pe", "dve", "act", "pool", "sp")

    def __init__(self, nc, stack):
        self.nc = nc
        self.stack = stack
        self.ops = {e: [] for e in self.ENG}
        self.semh = {}
        self.cnt = {}
        for e in ("pe", "dve", "act", "pool"):
            self.semh[e] = stack.enter_context(nc.semaphore("s_" + e))
            self.cnt[e] = 0
        self.seen = {e: {} for e in self.ENG}
        self.ninstr = 0

    def dma_sem(self, name):
        self.semh[name] = self.stack.enter_context(self.nc.semaphore(name))
        self.cnt[name] = 0
        return name

    def _waits(self, eng, reads, writes):
        deps = {}
        for b in reads:
            if b.w is not None:
                k, v = b.w
                if deps.get(k, 0) < v:
                    deps[k] = v
        for b in writes:
            if b.w is not None:
                k, v = b.w
                if deps.get(k, 0) < v:
                    deps[k] = v
            for k, v in b.r.items():
                if deps.get(k, 0) < v:
                    deps[k] = v
        seen = self.seen[eng]
        for k, v in deps.items():
            if k == eng:
                if eng == "pe":
                    continue
                if v <= self.cnt[eng] - 3:
                    continue
            if seen.get(k, 0) >= v:
                continue
            seen[k] = v
            sem = self.semh[k]
            self.ninstr += 1
            self.ops[eng].append(lambda e, sem=sem, v=v: e.wait_ge(sem, v))

    def _mark(self, ev, reads, writes):
        k, v = ev
        for b in reads:
            if b.r.get(k, 0) < v:
                b.r[k] = v
        for b in writes:
            b.w = ev
            b.r = {}

    def op(self, eng, fn, reads=(), writes=()):
        self._waits(eng, reads, writes)
        self.cnt[eng] += 1
        sem = self.semh[eng]
        self.ninstr += 1
        self.ops[eng].append(lambda e, fn=fn, sem=sem: fn(e).then_inc(sem, 1))
        self._mark((eng, self.cnt[eng]), reads, writes)

    def dma(self, pairs, sem, reads=(), writes=()):
        for q in sorted(set(p[0] for p in pairs)):
            self._waits(q, reads, writes)
            if self.cnt[sem] and self.seen[q].get(sem, 0) < self.cnt[sem]:
                self.seen[q][sem] = self.cnt[sem]
                self.ops[q].append(lambda e, s_=self.semh[sem], v=self.cnt[sem]: e.wait_ge(s_, v))
        h = self.semh[sem]
        for (q, o, i) in pairs:
            self.cnt[sem] += 16
            self.ninstr += 1
            self.ops[q].append(lambda e, o=o, i=i, h=h: e.dma_start(out=o, in_=i).then_inc(h, 16))
        self._mark((sem, self.cnt[sem]), reads, writes)

    def barrier(self):
        for eng in self.ENG:
            seen = self.seen[eng]
            for k, v in self.cnt.items():
                if v == 0 or seen.get(k, 0) >= v:
                    continue
                if k == eng and eng == "pe":
                    continue
                seen[k] = v
                sem = self.semh[k]
                self.ops[eng].append(lambda e, sem=sem, v=v: e.wait_ge(sem, v))

    def flush(self):
        ops = self.ops
        with self.nc.Block() as block:
            @block.tensor
            def _(e):
                for c in ops["pe"]:
                    c(e)

            @block.vector
            def _(e):
                for c in ops["dve"]:
                    c(e)

            @block.scalar
            def _(e):
                for c in ops["act"]:
                    c(e)

            @block.gpsimd
            def _(e):
                for c in ops["pool"]:
                    c(e)

            @block.sync
            def _(e):
                for c in ops["sp"]:
                    c(e)
        self.ops = {e: [] for e in self.ENG}


class Ctx:
    pass


class LazyInputs(dict):
    def __init__(self, nc):
        super().__init__()
        self.nc = nc
        self.shapes = dict(INPUT_SHAPES)

    def __missing__(self, name):
        ap = self.nc.dram_tensor(name, self.shapes[name], F32, kind="ExternalInput").ap()
        self[name] = ap
        return ap


def mm(K, out, lhsT, rhs, start, stop, reads, wb):
    K.P.op("pe", lambda e: e.matmul(out, lhsT=lhsT, rhs=rhs, start=start, stop=stop), reads=reads, writes=[wb])


def tr(K, out, in_, ident, reads, wb):
    K.P.op("pe", lambda e: e.transpose(out=out, in_=in_, identity=ident), reads=list(reads) + [K.cst_b], writes=[wb])


def act(K, out, in_, func, reads, writes, **kw):
    K.P.op("act", lambda e: e.activation(out=out, in_=in_, func=func, **kw), reads=reads, writes=writes)


def tt(K, eng, out, in0, in1, op, reads, writes):
    K.P.op(eng, lambda e: e.tensor_tensor(out=out, in0=in0, in1=in1, op=op), reads=reads, writes=writes)


def stt(K, out, in0, scalar, in1, op0, op1, reads, writes):
    K.P.op("dve", lambda e: e.scalar_tensor_tensor(out=out, in0=in0, scalar=scalar, in1=in1, op0=op0, op1=op1),
           reads=reads, writes=writes)


def ts(K, eng, out, in0, s1, s2, op0, op1, reads, writes):
    if s2 is None:
        K.P.op(eng, lambda e: e.tensor_scalar(out=out, in0=in0, scalar1=s1, scalar2=None, op0=op0), reads=reads, writes=writes)
    else:
        K.P.op(eng, lambda e: e.tensor_scalar(out=out, in0=in0, scalar1=s1, scalar2=s2, op0=op0, op1=op1), reads=reads, writes=writes)


def cp(K, eng, out, in_, reads, writes):
    if eng == "act":
        act(K, out, in_, AF.Copy, reads, writes)
    else:
        K.P.op(eng, lambda e: e.tensor_copy(out=out, in_=in_), reads=reads, writes=writes)


def memset(K, eng, ap, val, writes):
    K.P.op(eng, lambda e: e.memset(ap, val), writes=writes)


def bank(K, n=None):
    n = n or K.nb
    b = K.rr % n
    K.rr += 1
    return b


class Phase:
    def __init__(self, K, name):
        self.K = K
        self.name = name
        self.st = ExitStack()
        self.n = 0

    def sb(self, shape, dt, name=None):
        self.n += 1
        t = self.st.enter_context(self.K.nc.sbuf_tensor(f"{self.name}_{name or self.n}", list(shape), dt))
        return t, Buf()

    def close(self):
        K = self.K
        K.P.barrier()
        K.P.flush()
        self.st.close()


def load_cols(K, ph, specs):
    P = K.P
    tot = sum(ap.shape[0] for _, ap in specs)
    G = (tot + 127) // 128
    stg, stg_b = ph.sb([128, G, 128], F32, "colstg")
    cols, cols_b = ph.sb([128, G * 128], F32, "cols")
    off = {}
    pairs = []
    r = 0
    for key, ap in specs:
        off[key] = r
        R = ap.shape[0]
        s = 0
        while s < R:
            n = min(R - s, 128 - (r % 128))
            pairs.append(("sp", stg[r % 128:r % 128 + n, r // 128, :], ap[s:s + n, :]))
            s += n
            r += n
    P.dma(pairs, K.pld, writes=[stg_b])
    for g in range(G):
        rows = min(128, tot - g * 128)
        pb = bank(K)
        tr(K, K.ps[pb][:, 0:rows], stg[0:rows, g, :], K.cstf[0:rows, 0:rows], [stg_b], K.psb[pb])
        cp(K, "act", cols[:, g * 128:g * 128 + rows], K.ps[pb][:, 0:rows], [K.psb[pb]], [cols_b])
    return cols, cols_b, off


def load_bcast(K, ph, specs):
    P = K.P
    tot = sum(ap.shape[1] for _, ap in specs)
    row, row_b = ph.sb([1, tot], F32, "bcrow")
    bc, bc_b = ph.sb([128, tot], F32, "bc")
    off = {}
    pairs = []
    c = 0
    for key, ap in specs:
        off[key] = c
        pairs.append(("sp", row[0:1, c:c + ap.shape[1]], ap))
        c += ap.shape[1]
    P.dma(pairs, K.pld, writes=[row_b])
    for s in range(0, tot, 512):
        n = min(512, tot - s)
        pb = bank(K)
        mm(K, K.ps[pb][:, 0:n], K.cstf[0:1, 896:1024], row[0:1, s:s + n], True, True, [row_b, K.cst_b], K.psb[pb])
        cp(K, "act", bc[:, s:s + n], K.ps[pb][:, 0:n], [K.psb[pb]], [bc_b])
    return bc, bc_b, off


def load_x(K, t, xt, xt_b):
    K.P.dma([("sp", xt[:, :, :], K.xT[:, t * TT:(t + 1) * TT].rearrange("(c p) t -> p c t", p=128))],
            K.ld, writes=[xt_b])


def store_x(K, t, xt, xt_b):
    K.P.dma([("sp", K.xT[:, t * TT:(t + 1) * TT].rearrange("(c p) t -> p c t", p=128), xt[:, :, :])],
            K.stx, reads=[xt_b])


def bcast_stat(K, srcs, src_bufs, out, out_b, scale, func=None, bias=EPS):
    pb = bank(K)
    n = len(srcs)
    for i, s in enumerate(srcs):
        mm(K, K.ps[pb][:, 0:out.shape[-1]], K.onesb[:, :], s, i == 0, i == n - 1, list(src_bufs) + [K.cst_b], K.psb[pb])
    if func is None:
        act(K, out, K.ps[pb][:, 0:out.shape[-1]], AF.Ln, [K.psb[pb]], [out_b], scale=scale, bias=bias)
        act(K, out, out, AF.Exp, [out_b], [out_b], scale=-0.5)
    else:
        act(K, out, K.ps[pb][:, 0:out.shape[-1]], func, [K.psb[pb]], [out_b], scale=scale, bias=bias)
    return pb


def rmsnorm_h(K, xt, xt_b, g, g_b, sq, sq_b, rstd, rstd_b, h, h_b):
    act(K, sq[:, 0:8, :], xt[:, :, :], AF.Square, [xt_b], [sq_b])
    bcast_stat(K, [sq[:, c, :] for c in range(8)], [sq_b], rstd[:, :], rstd_b, 1.0 / D)
    for c in range(8):
        stt(K, h[:, c, :], xt[:, c, :], g[:, c:c + 1], rstd[:, :], ALU.mult, ALU.mult, [xt_b, g_b, rstd_b], [h_b])


class XPipe:
    def __init__(self, K, ph, g, g_b, xt, xt_b, h, h_b, sq, sq_b, rstd, rstd_b, nring=4):
        self.K, self.g, self.g_b = K, g, g_b
        self.xt, self.xt_b, self.h, self.h_b = xt, xt_b, h, h_b
        self.sq, self.sq_b, self.rstd, self.rstd_b = sq, sq_b, rstd, rstd_b
        self.ring = [ph.sb([128, TT], F32, f"xr{i}") for i in range(nring)]
        self.r = 0

    def load(self, t):
        load_x(self.K, t, self.xt, self.xt_b)

    def square(self):
        act(self.K, self.sq[:, 0:8, :], self.xt[:, :, :], AF.Square, [self.xt_b], [self.sq_b])

    def make_h(self):
        K = self.K
        bcast_stat(K, [self.sq[:, c, :] for c in range(8)], [self.sq_b], self.rstd[:, :], self.rstd_b, 1.0 / D)
        for c in range(8):
            stt(K, self.h[:, c, :], self.xt[:, c, :], self.g[:, c:c + 1], self.rstd[:, :], ALU.mult, ALU.mult,
                [self.xt_b, self.g_b, self.rstd_b], [self.h_b])

    def res_load(self, t, c):
        K = self.K
        i = self.r % len(self.ring)
        self.r += 1
        xr, xr_b = self.ring[i]
        K.P.dma([("sp", xr[:, :], K.xT[c * 128:(c + 1) * 128, t * TT:(t + 1) * TT])], K.rl[i], writes=[xr_b])
        return i

    def res_add_store(self, t, c, i, pb, bias=None, bias_b=None):
        K = self.K
        xr, xr_b = self.ring[i]
        if bias is None:
            tt(K, "dve", xr[:, :], xr[:, :], K.ps[pb][:, :], ALU.add, [xr_b, K.psb[pb]], [xr_b])
        else:
            stt(K, xr[:, :], K.ps[pb][:, :], bias, xr[:, :], ALU.add, ALU.add, [K.psb[pb], bias_b, xr_b], [xr_b])
        K.P.dma([("sp", K.xT[c * 128:(c + 1) * 128, t * TT:(t + 1) * TT], xr[:, :])], K.rs[i], reads=[xr_b])

    def out_proj(self, t, wout, wout_b, src, src_b, bias_fn=None, bias_b=None):
        K = self.K
        for half in range(2):
            slots = [self.res_load(t, c) for c in range(half * 4, half * 4 + 4)]
            for c in range(half * 4, half * 4 + 4):
                pb = bank(K)
                for d in range(8):
                    mm(K, K.ps[pb][:, :], wout[:, d, c * 128:(c + 1) * 128], src[:, d, :], d == 0, d == 7, [wout_b, src_b], K.psb[pb])
                self.res_add_store(t, c, slots[c - half * 4], pb, None if bias_fn is None else bias_fn(c), bias_b)


def phase_tin(K):
    ph = Phase(K, "tin")
    xin, xin_b = ph.sb([128, 4, D], F32)
    xt, xt_b = ph.sb([128, 8, TT], F32)
    for t in range(K.NT):
        K.P.dma([("sp", xin[:, :, :], K.x[t * TT:(t + 1) * TT, :].rearrange("(g p) d -> p g d", p=128))], K.ld, writes=[xin_b])
        for c in range(8):
            pb = bank(K)
            for g in range(4):
                tr(K, K.ps[pb][:, g * 128:(g + 1) * 128], xin[:, g, c * 128:(c + 1) * 128], K.cstf[:, 0:128], [xin_b], K.psb[pb])
            cp(K, "act" if c % 2 else "dve", xt[:, c, :], K.ps[pb][:, :], [K.psb[pb]], [xt_b])
        store_x(K, t, xt, xt_b)
    ph.close()


def phase_fin(K, norm):
    ph = Phase(K, "fin")
    xt, xt_b = ph.sb([128, 8, TT], F32)
    yo, yo_b = ph.sb([128, 4, D], F32)
    if norm:
        cols, cols_b, off = load_cols(K, ph, [("g", K.d["norm_final"].rearrange("(c p) -> c p", p=128))])
        sq, sq_b = ph.sb([128, 8, TT], BF16)
        rstd, rstd_b = ph.sb([128, TT], F32)
    for t in range(K.NT):
        load_x(K, t, xt, xt_b)
        if norm:
            act(K, sq[:, :, :], xt[:, :, :], AF.Square, [xt_b], [sq_b])
            bcast_stat(K, [sq[:, c, :] for c in range(8)], [sq_b], rstd[:, :], rstd_b, 1.0 / D)
            for c in range(8):
                stt(K, xt[:, c, :], xt[:, c, :], cols[:, off["g"] + c:off["g"] + c + 1], rstd[:, :], ALU.mult, ALU.mult,
                    [xt_b, cols_b, rstd_b], [xt_b])
        for g in range(4):
            for half in range(2):
                pb = bank(K)
                for c4 in range(4):
                    c = half * 4 + c4
                    tr(K, K.ps[pb][:, c4 * 128:(c4 + 1) * 128], xt[:, c, g * 128:(g + 1) * 128], K.cstf[:, 0:128], [xt_b], K.psb[pb])
                cp(K, "act" if half else "dve", yo[:, g, half * 512:(half + 1) * 512], K.ps[pb][:, :], [K.psb[pb]], [yo_b])
        K.P.dma([("sp", K.out[t * TT:(t + 1) * TT, :].rearrange("(g p) d -> p g d", p=128), yo[:, :, :])], K.stx, reads=[yo_b])
    ph.close()


def phase_ffn(K, li):
    P = K.P
    ph = Phase(K, f"ffn{li}")
    K.nb = 4
    wup, wup_b = ph.sb([128, 8, 2 * FH], BF16, "wup")
    wdn, wdn_b = ph.sb([128, 22, D], BF16, "wdn")
    pairs = [("pool", wup[:, c, :], K.d["ffn_w_up"][li, c * 128:(c + 1) * 128, :]) for c in range(8)]
    pairs += [("pool", wdn[:, j, :], K.d["ffn_w_down"][li, j * 128:(j + 1) * 128, :]) for j in range(22)]
    P.dma(pairs, K.wld, writes=[wup_b, wdn_b])
    cols, cols_b, off = load_cols(K, ph, [
        ("g", K.d["norm_ffn"][li].rearrange("(c p) -> c p", p=128)),
        ("dw", K.d["ffn_w_dw"][li].rearrange("k (c p) -> (k c) p", p=128)),
    ])
    xt, xt_b = ph.sb([128, 8, TT], F32, "xt")
    h, h_b = ph.sb([128, 8, TT], BF16, "h")
    u, u_b = ph.sb([128, 22, TT], BF16, "u")
    rstd, rstd_b = ph.sb([128, TT], F32, "rstd")
    NZ = 2
    zs = [ph.sb([128, TT + 2], F32, f"zs{i}") for i in range(NZ)]
    acc = [ph.sb([128, TT], F32, f"acc{i}") for i in range(4)]
    ztail, ztail_b = ph.sb([128, 44, 2], F32, "ztail")
    NR = 4
    ring = [ph.sb([128, TT], F32, f"xr{i}") for i in range(NR)]
    g = cols[:, off["g"]:off["g"] + 8]
    dw0 = off["dw"]
    ACC = [4, 5, 6, 7]
    st = {"k": 0, "r": 0}

    def norm_a(t):
        load_x(K, t, xt, xt_b)
        act(K, h[:, :, :], xt[:, :, :], AF.Square, [xt_b], [h_b])

    def norm_b():
        bcast_stat(K, [h[:, c, :] for c in range(8)], [h_b], rstd[:, :], rstd_b, 1.0 / D)
        for c in range(8):
            stt(K, h[:, c, :], xt[:, c, :], g[:, c:c + 1], rstd[:, :], ALU.mult, ALU.mult, [xt_b, cols_b, rstd_b], [h_b])

    def res_load(t, c):
        i = st["r"] % NR
        st["r"] += 1
        xr, xr_b = ring[i]
        P.dma([("sp", xr[:, :], K.xT[c * 128:(c + 1) * 128, t * TT:(t + 1) * TT])], K.rl[i], writes=[xr_b])
        return i

    def res_add_store(t, c, i, pb):
        xr, xr_b = ring[i]
        tt(K, "dve", xr[:, :], xr[:, :], K.ps[pb][:, :], ALU.add, [xr_b, K.psb[pb]], [xr_b])
        P.dma([("sp", K.xT[c * 128:(c + 1) * 128, t * TT:(t + 1) * TT], xr[:, :])], K.rs[i], reads=[xr_b])

    LAG = 3

    def down1(j):
        for c in range(4):
            mm(K, K.ps[ACC[c]][:, :], wdn[:, j, c * 128:(c + 1) * 128], u[:, j, :], j == 0, j == 21, [wdn_b, u_b], K.psb[ACC[c]])

    def down2(t, cs):
        slots = [res_load(t, c) for c in cs]
        for i, c in enumerate(cs):
            pb = bank(K)
            for j in range(22):
                mm(K, K.ps[pb][:, :], wdn[:, j, c * 128:(c + 1) * 128], u[:, j, :], j == 0, j == 21, [wdn_b, u_b], K.psb[pb])
            res_add_store(t, c, slots[i], pb)

    norm_a(0)
    norm_b()
    for t in range(K.NT):
        if t % K.TPS == 0:
            memset(K, "pool", ztail[:, :, :], 0.0, [ztail_b])
        slots = [res_load(t, c) for c in range(4)]
        for j in range(22):
            a2 = []
            for which, fc in ((0, j), (1, 22 + j)):
                pb = bank(K)
                for d in range(8):
                    mm(K, K.ps[pb][:, :], wup[:, d, fc * 128:(fc + 1) * 128], h[:, d, :], d == 0, d == 7, [wup_b, h_b], K.psb[pb])
                z, z_b = zs[st["k"] % NZ]
                a, a_b = acc[(2 * (j % 2)) + which]
                st["k"] += 1
                cp(K, "pool", z[:, 0:2], ztail[:, fc, :], [ztail_b], [z_b])
                act(K, z[:, 2:TT + 2], K.ps[pb][:, :], AF.Copy, [K.psb[pb]], [z_b])
                cp(K, "pool", ztail[:, fc, :], z[:, TT:TT + 2], [z_b], [ztail_b])
                act(K, a[:, :], z[:, 0:TT], AF.Copy, [z_b, cols_b], [a_b], scale=cols[:, dw0 + fc:dw0 + fc + 1])
                stt(K, a[:, :], z[:, 1:TT + 1], cols[:, dw0 + 44 + fc:dw0 + 44 + fc + 1], a[:, :], ALU.mult, ALU.add, [z_b, cols_b, a_b], [a_b])
                stt(K, a[:, :], z[:, 2:TT + 2], cols[:, dw0 + 88 + fc:dw0 + 88 + fc + 1], a[:, :], ALU.mult, ALU.add, [z_b, cols_b, a_b], [a_b])
                a2.append((a, a_b))
            (ag, ag_b), (av, av_b) = a2
            act(K, ag[:, :], ag[:, :], AF.Silu, [ag_b], [ag_b])
            tt(K, "dve", u[:, j, :], ag[:, :], av[:, :], ALU.mult, [ag_b, av_b], [u_b])
            if j >= LAG:
                down1(j - LAG)
        nxt = t + 1 < K.NT
        if nxt:
            norm_a(t + 1)
        for j in range(22 - LAG, 22):
            down1(j)
        for c in range(4):
            res_add_store(t, c, slots[c], ACC[c])
        down2(t, [4, 5])
        if nxt:
            norm_b()
        down2(t, [6, 7])
    K.nb = 6
    ph.close()


def phase_conf(K):
    P = K.P
    ph = Phase(K, "conf")
    win, win_b = ph.sb([128, 8, 2048], BF16, "win")
    wout, wout_b = ph.sb([128, 8, D], BF16, "wout")
    pairs = [("pool", win[:, c, :], K.d["cv_w_in"][0, c * 128:(c + 1) * 128, :]) for c in range(8)]
    pairs += [("pool", wout[:, c, :], K.d["cv_w_out"][0, c * 128:(c + 1) * 128, :]) for c in range(8)]
    P.dma(pairs, K.wld, writes=[win_b, wout_b])
    v2 = lambda a: a.rearrange("(c p) -> c p", p=128)
    cols, cols_b, off = load_cols(K, ph, [
        ("g", v2(K.d["norm_mix"][1])), ("bin", v2(K.d["cv_b_in"][0])),
        ("dw", K.d["cv_w_dw"][0].rearrange("k (c p) -> (k c) p", p=128)),
        ("bdw", v2(K.d["cv_b_dw"][0])), ("lng", v2(K.d["cv_ln_g"][0])), ("lnb", v2(K.d["cv_ln_b"][0])),
        ("bout", v2(K.d["cv_b_out"][0])),
    ])
    col = lambda key, i: cols[:, off[key] + i:off[key] + i + 1]
    xt, xt_b = ph.sb([128, 8, TT], F32, "xt")
    h, h_b = ph.sb([128, 8, TT], BF16, "h")
    csq, csq_b = ph.sb([128, 8, TT], BF16, "csq")
    sq, sq_b = csq, csq_b
    rstd, rstd_b = ph.sb([128, TT], F32, "rstd")
    glu = [ph.sb([128, TT + 30], BF16, f"glu{c}") for c in range(8)]
    cacc = [ph.sb([128, TT], F32, f"cacc{c}") for c in range(8)]
    sg = [ph.sb([128, TT], F32, f"sg{i}") for i in range(2)]
    cb, cb_b = ph.sb([128, 8, TT], BF16, "cb")
    sl, sl_b = cb, cb_b
    mean, mean_b = ph.sb([128, TT], F32, "mean")
    msq, msq_b = ph.sb([128, TT], F32, "msq")
    lrs, lrs_b = ph.sb([128, TT], F32, "lrs")
    tmp = [ph.sb([128, TT], F32, f"tmp{i}") for i in range(2)]
    dg, dg_b = ph.sb([128, 31 * 8, 128], BF16, "dg")
    for i in range(31 * 8):
        ts(K, "dve", dg[:, i, :], K.cstf[:, 0:128], col("dw", i), None, ALU.mult, None, [K.cst_b, cols_b], [dg_b])
    xp = XPipe(K, ph, cols[:, off["g"]:off["g"] + 8], cols_b, xt, xt_b, h, h_b, sq, sq_b, rstd, rstd_b)
    xp.load(0)
    xp.square()
    xp.make_h()
    for t in range(K.NT):
        nxt = t + 1 < K.NT
        if nxt:
            xp.load(t + 1)
        for c in range(8):
            gl, gl_b = glu[c]
            if t % K.TPS == 0:
                memset(K, "pool", gl[:, 0:30], 0.0, [gl_b])
            else:
                cp(K, "pool", gl[:, 0:30], gl[:, TT:TT + 30], [gl_b], [gl_b])
            pa = bank(K)
            for d in range(8):
                mm(K, K.ps[pa][:, :], win[:, d, c * 128:(c + 1) * 128], h[:, d, :], d == 0, d == 7, [win_b, h_b], K.psb[pa])
            pg = bank(K)
            for d in range(8):
                mm(K, K.ps[pg][:, :], win[:, d, D + c * 128:D + (c + 1) * 128], h[:, d, :], d == 0, d == 7, [win_b, h_b], K.psb[pg])
            s, s_b = sg[c % 2]
            act(K, s[:, :], K.ps[pg][:, :], AF.Sigmoid, [K.psb[pg], cols_b], [s_b], bias=col("bin", 8 + c))
            stt(K, gl[:, 30:TT + 30], K.ps[pa][:, :], col("bin", c), s[:, :], ALU.add, ALU.mult, [K.psb[pa], cols_b, s_b], [gl_b])
        if nxt:
            xp.square()
            xp.make_h()
        for c in range(8):
            gl, gl_b = glu[c]
            pc = bank(K)
            for kk in range(31):
                mm(K, K.ps[pc][:, :], dg[:, kk * 8 + c, :], gl[:, kk:kk + TT], kk == 0, kk == 30, [dg_b, gl_b], K.psb[pc])
            ca, ca_b = cacc[c]
            act(K, ca[:, :], K.ps[pc][:, :], AF.Identity, [K.psb[pc], cols_b], [ca_b], bias=col("bdw", c))
            cp(K, "dve", cb[:, c, :], ca[:, :], [ca_b], [cb_b])
            act(K, csq[:, c, :], ca[:, :], AF.Square, [ca_b], [csq_b])
        bcast_stat(K, [cb[:, c, :] for c in range(8)], [cb_b], mean[:, :], mean_b, 1.0 / D, func=AF.Copy, bias=0.0)
        tt(K, "dve", msq[:, :], mean[:, :], mean[:, :], ALU.mult, [mean_b], [msq_b])
        pb = bank(K)
        for c in range(8):
            mm(K, K.ps[pb][:, :], K.onesb[:, :], csq[:, c, :], c == 0, c == 7, [csq_b, K.cst_b], K.psb[pb])
        stt(K, lrs[:, :], K.ps[pb][:, :], 1.0 / D, msq[:, :], ALU.mult, ALU.subtract, [K.psb[pb], msq_b], [lrs_b])
        act(K, lrs[:, :], lrs[:, :], AF.Ln, [lrs_b], [lrs_b], bias=EPS)
        act(K, lrs[:, :], lrs[:, :], AF.Exp, [lrs_b], [lrs_b], scale=-0.5)
        for c in range(8):
            ca, ca_b = cacc[c]
            tm, tm_b = tmp[c % 2]
            tt(K, "dve", tm[:, :], ca[:, :], mean[:, :], ALU.subtract, [ca_b, mean_b], [tm_b])
            tt(K, "dve", tm[:, :], tm[:, :], lrs[:, :], ALU.mult, [tm_b, lrs_b], [tm_b])
            act(K, sl[:, c, :], tm[:, :], AF.Silu, [tm_b, cols_b], [sl_b], scale=col("lng", c), bias=col("lnb", c))
        xp.out_proj(t, wout, wout_b, sl, sl_b, lambda c: col("bout", c), cols_b)
    ph.close()


def phase_sgu(K):
    P = K.P
    ph = Phase(K, "sgu")
    win, win_b = ph.sb([128, 8, 2048], BF16, "win")
    wout, wout_b = ph.sb([128, 8, D], BF16, "wout")
    wsn, wsn_b = ph.sb([128, 8, 128], F32, "wsn")
    pairs = [("pool", win[:, c, :], K.d["sg_w_in"][0, c * 128:(c + 1) * 128, :]) for c in range(8)]
    pairs += [("pool", wout[:, c, :], K.d["sg_w_out"][0, c * 128:(c + 1) * 128, :]) for c in range(8)]
    P.dma(pairs, K.wld, writes=[win_b, wout_b])
    P.dma([("sp", wsn[:, g, :], K.d["sg_w_s"][0, g, :, :]) for g in range(8)], K.pld, writes=[wsn_b])
    v2 = lambda a: a.rearrange("(c p) -> c p", p=128)
    cols, cols_b, off = load_cols(K, ph, [
        ("g", v2(K.d["norm_mix"][2])), ("binu", v2(K.d["sg_b_in"][0, 0:D])), ("bout", v2(K.d["sg_b_out"][0])),
    ])
    col = lambda key, i: cols[:, off[key] + i:off[key] + i + 1]
    bc, bc_b, boff = load_bcast(K, ph, [
        ("binv", K.d["sg_b_in"][0:1, D:2 * D]), ("lng", K.d["sg_ln_g"][0:1, :]), ("lnb", K.d["sg_ln_b"][0:1, :]),
        ("bs", K.d["sg_b_s"][0:1, :, :].rearrange("o g i -> o (g i)")),
    ])
    wct, wct_b = ph.sb([128, 8, 128], BF16, "wct")
    for g in range(8):
        pb = bank(K)
        tr(K, K.ps[pb][:, 0:128], wsn[:, g, :], K.cstf[:, 0:128], [wsn_b], K.psb[pb])
        tt(K, "dve", wct[:, g, :], K.ps[pb][:, 0:128], K.cstf[:, 128:256], ALU.mult, [K.psb[pb], K.cst_b], [wct_b])
    bs4, bs4_b = ph.sb([128, 8, TT], F32, "bs4")
    for tg in range(4):
        cp(K, "pool", bs4[:, :, tg * 128:(tg + 1) * 128], bc[:, boff["bs"]:boff["bs"] + 1024].rearrange("p (g i) -> p g i", i=128), [bc_b], [bs4_b])
    xt, xt_b = ph.sb([128, 8, TT], F32, "xt")
    h, h_b = ph.sb([128, 8, TT], BF16, "h")
    sq, sq_b = ph.sb([128, 8, TT], BF16, "sq")
    rstd, rstd_b = ph.sb([128, TT], F32, "rstd")
    ug, ug_b = ph.sb([128, 8, TT], F32, "ug")
    vx = [ph.sb([128, D], F32, f"vx{i}") for i in range(2)]
    vnb, vnb_b = ph.sb([128, 4, D], BF16, "vnb")
    stats, stats_b = ph.sb([128, 2, 6], F32, "stats")
    mv, mv_b = ph.sb([128, 2], F32, "mv")
    tmp = [ph.sb([128, TT], F32, f"tmp{i}") for i in range(2)]
    gated, gated_b = ph.sb([128, 8, TT], BF16, "gated")
    xp = XPipe(K, ph, cols[:, off["g"]:off["g"] + 8], cols_b, xt, xt_b, h, h_b, sq, sq_b, rstd, rstd_b)
    xp.load(0)
    xp.square()
    xp.make_h()
    for t in range(K.NT):
        nxt = t + 1 < K.NT
        if nxt:
            xp.load(t + 1)
        for tg in range(4):
            vxt, vx_b = vx[tg % 2]
            for half in range(2):
                pb = bank(K)
                for d in range(8):
                    mm(K, K.ps[pb][:, :], h[:, d, tg * 128:(tg + 1) * 128], win[:, d, D + half * 512:D + (half + 1) * 512], d == 0, d == 7,
                       [win_b, h_b], K.psb[pb])
                tt(K, "dve", vxt[:, half * 512:(half + 1) * 512], K.ps[pb][:, :], bc[:, boff["binv"] + half * 512:boff["binv"] + (half + 1) * 512],
                   ALU.add, [K.psb[pb], bc_b], [vx_b])
            act(K, vxt[:, :], vxt[:, :], AF.Gelu_apprx_tanh, [vx_b], [vx_b])
            for half in range(2):
                K.P.op("dve", lambda e, half=half, vxt=vxt: e.bn_stats(out=stats[:, half, :], in_=vxt[:, half * 512:(half + 1) * 512]),
                       reads=[vx_b], writes=[stats_b])
            K.P.op("dve", lambda e: e.bn_aggr(out=mv[:, :], in_=stats[:, :, :].rearrange("p a b -> p (a b)")), reads=[stats_b], writes=[mv_b])
            act(K, mv[:, 1:2], mv[:, 1:2], AF.Ln, [mv_b], [mv_b], bias=EPS)
            act(K, mv[:, 1:2], mv[:, 1:2], AF.Exp, [mv_b], [mv_b], scale=-0.5)
            ts(K, "dve", vxt[:, :], vxt[:, :], mv[:, 0:1], mv[:, 1:2], ALU.subtract, ALU.mult, [vx_b, mv_b], [vx_b])
            tt(K, "pool", vxt[:, :], vxt[:, :], bc[:, boff["lng"]:boff["lng"] + D], ALU.mult, [vx_b, bc_b], [vx_b])
            tt(K, "dve", vnb[:, tg, :], vxt[:, :], bc[:, boff["lnb"]:boff["lnb"] + D], ALU.add, [vx_b, bc_b], [vnb_b])
        for c in range(8):
            pb = bank(K)
            for d in range(8):
                mm(K, K.ps[pb][:, :], win[:, d, c * 128:(c + 1) * 128], h[:, d, :], d == 0, d == 7, [win_b, h_b], K.psb[pb])
            act(K, ug[:, c, :], K.ps[pb][:, :], AF.Gelu_apprx_tanh, [K.psb[pb], cols_b], [ug_b], bias=col("binu", c))
        if nxt:
            xp.square()
            xp.make_h()
        for g in range(8):
            pb = bank(K)
            for tg in range(4):
                mm(K, K.ps[pb][:, tg * 128:(tg + 1) * 128], vnb[:, tg, g * 128:(g + 1) * 128], wct[:, g, :], True, True, [vnb_b, wct_b], K.psb[pb])
            tm, tm_b = tmp[g % 2]
            tt(K, "dve", tm[:, :], K.ps[pb][:, :], bs4[:, g, :], ALU.add, [K.psb[pb], bs4_b], [tm_b])
            tt(K, "pool", gated[:, g, :], tm[:, :], ug[:, g, :], ALU.mult, [tm_b, ug_b], [gated_b])
        xp.out_proj(t, wout, wout_b, gated, gated_b, lambda c: col("bout", c), cols_b)
    ph.close()


def phase_lin(K, kind):
    P = K.P
    gla = kind == "gla"
    ph = Phase(K, kind)
    NH, DV = (4, 256) if gla else (8, 128)
    DVC = DV // 128
    WIN = 3088 if gla else 4096
    wkey = "gla_w_in" if gla else "hg_w_in"
    okey = "gla_w_out" if gla else "hg_w_out"
    layer = 0 if gla else 3
    win, win_b = ph.sb([128, 8, WIN], BF16, "win")
    wout, wout_b = ph.sb([128, 8, D], BF16, "wout")
    pairs = [("pool", win[:, c, :], K.d[wkey][0, c * 128:(c + 1) * 128, :]) for c in range(8)]
    pairs += [("pool", wout[:, c, :], K.d[okey][0, c * 128:(c + 1) * 128, :]) for c in range(8)]
    wbufs = [win_b, wout_b]
    if gla:
        wg2, wg2_b = ph.sb([16, 512], BF16, "wg2")
        pairs.append(("pool", wg2[:, :], K.d["gla_w_g2"][0]))
        wbufs.append(wg2_b)
    P.dma(pairs, K.wld, writes=wbufs)
    v2 = lambda a: a.rearrange("(c p) -> c p", p=128)
    specs = [("g", v2(K.d["norm_mix"][layer]))]
    if gla:
        specs += [("bg2", v2(K.d["gla_b_g2"][0])), ("gn", v2(K.d["gla_norm"][0]))]
    else:
        specs += [("gn", v2(K.d["hg_norm"][0])), ("lbt", K.d["hg_lb_table"].rearrange("l (c p) -> (l c) p", p=128))]
    cols, cols_b, off = load_cols(K, ph, specs)
    col = lambda key, i: cols[:, off[key] + i:off[key] + i + 1]
    par, par_b = ph.sb([128, 16], F32, "par")
    if gla:
        ts(K, "dve", par[:, 0:4], cols[:, off["bg2"]:off["bg2"] + 4], -1.0, None, ALU.mult, None, [cols_b], [par_b])
        dsc = 1.0 / 16.0
        qscale = 128.0 ** -0.5
    else:
        lt = cols[:, off["lbt"]:off["lbt"] + 32].rearrange("p (l c) -> p l c", c=8)
        ex, ex_b = ph.sb([128, 4, 8], F32, "lbex")
        mx, mx_b = ph.sb([128, 8], F32, "lbmx")
        tt(K, "dve", mx[:, :], lt[:, 0, :], lt[:, 1, :], ALU.max, [cols_b], [mx_b])
        tt(K, "dve", mx[:, :], mx[:, :], lt[:, 2, :], ALU.max, [cols_b, mx_b], [mx_b])
        tt(K, "dve", mx[:, :], mx[:, :], lt[:, 3, :], ALU.max, [cols_b, mx_b], [mx_b])
        for l in range(4):
            tt(K, "dve", ex[:, l, :], lt[:, l, :], mx[:, :], ALU.subtract, [cols_b, mx_b], [ex_b])
        act(K, ex[:, :, :], ex[:, :, :], AF.Exp, [ex_b], [ex_b])
        tt(K, "dve", par[:, 8:16], ex[:, 1, :], ex[:, 2, :], ALU.add, [ex_b], [par_b])
        tt(K, "dve", par[:, 8:16], par[:, 8:16], ex[:, 3, :], ALU.add, [ex_b, par_b], [par_b])
        tt(K, "dve", mx[:, :], par[:, 8:16], ex[:, 0, :], ALU.add, [ex_b, par_b], [mx_b])
        K.P.op("dve", lambda e: e.reciprocal(out=mx[:, :], in_=mx[:, :]), reads=[mx_b], writes=[mx_b])
        tt(K, "dve", par[:, 0:8], par[:, 8:16], mx[:, :], ALU.mult, [par_b, mx_b], [par_b])
        dsc = 1.0
        qscale = 1.0
    xt, xt_b = ph.sb([128, 8, TT], F32, "xt")
    h, h_b = ph.sb([128, 8, TT], BF16, "h")
    rstd, rstd_b = ph.sb([128, TT], F32, "rstd")
    vt, vt_b = ph.sb([128, 4, D], BF16, "vt")
    gate, gate_b = ph.sb([128, 8, TT], F32, "gate")
    sq = gate[:, 0:4, :].bitcast(BF16).rearrange("p a (b t) -> p (a b) t", t=TT)
    sq_b = gate_b
    og, og_b = ph.sb([128, 8, TT], BF16, "og")
    f32t = lambda n: ph.sb([128, TT], F32, n)
    bft = lambda n: ph.sb([128, TT], BF16, n)
    (l1, l1_b), (l2, l2_b), (cl, cl_b) = f32t("l1"), f32t("l2"), f32t("cl")
    (dA, dA_b), (dD, dD_b) = (l1, l1_b), (l2, l2_b)
    (eA, eA_b), (eB, eB_b), (eC, eC_b), (eD, eD_b) = f32t("eA"), f32t("eB"), f32t("eC"), f32t("eD")
    (qs, qs_b), (kk, kk_b) = f32t("qs"), f32t("kk")
    qtS, ktS, qdS, kdS = [[bft(f"{n}{i}") for i in range(2)] for n in ("qt", "kt", "qd", "kd")]
    eCS = [(eC, eC_b), f32t("eC1")]
    stmS = [ph.sb([128, 4, 128], BF16, f"stm{i}") for i in range(2)]
    kdt, kdt_b = ph.sb([128, 4, 128], BF16, "kdt")
    Sall, Sall_b = ph.sb([128, 7, DV], F32, "Sall")
    Sb8S = [ph.sb([128, 8, DV], BF16, f"Sb8_{i}") for i in range(2)]
    sqo, sqo_b = ph.sb([128, DVC, TT], BF16, "sqo")
    rs, rs_b = f32t("rs")
    tmpo, tmpo_b = f32t("tmpo")
    if gla:
        glr, glr_b = ph.sb([16, TT], BF16, "glr")
    S32 = [ph.sb([128, DV], F32, f"S32_{i}") for i in range(NH)]
    NB = 2
    K.nb = NB
    PS_S, PS_T, PS_KV = 2, 3, [4, 5]
    POS = [[6, 7], [6, 7]] if gla else [[6], [7]]
    bd4 = K.cstf[:, 256:384].rearrange("p (o i) -> p o i", o=1).to_broadcast([128, 4, 128])
    msk = K.cstf[:, 384:896]
    bd = K.cstf[:, 256:384]
    c3 = lambda a: a[:, :].rearrange("p (c t) -> p c t", t=64)
    QOFF, KOFF = (0, 512) if gla else (0, 1024)
    VOFF = 1024 if gla else 2048
    GOFF = 2048 if gla else 3072

    def pre(hd):
        s_ = hd % 2
        (qt, qt_b), (kt, kt_b), (qd, qd_b), (kd, kd_b) = qtS[s_], ktS[s_], qdS[s_], kdS[s_]
        eC_, eC_b_ = eCS[s_]
        if gla:
            pq = bank(K, NB)
            for d in range(8):
                mm(K, K.ps[pq][:, :], win[:, d, QOFF + hd * 128:QOFF + (hd + 1) * 128], h[:, d, :], d == 0, d == 7, [win_b, h_b], K.psb[pq])
            cp(K, "act", qs[:, :], K.ps[pq][:, :], [K.psb[pq]], [qs_b])
            pk = bank(K, NB)
            for d in range(8):
                mm(K, K.ps[pk][:, :], win[:, d, KOFF + hd * 128:KOFF + (hd + 1) * 128], h[:, d, :], d == 0, d == 7, [win_b, h_b], K.psb[pk])
            cp(K, "act", kk[:, :], K.ps[pk][:, :], [K.psb[pk]], [kk_b])
            pg = bank(K, NB)
            mm(K, K.ps[pg][:, :], wg2[0:16, hd * 128:(hd + 1) * 128], glr[0:16, :], True, True, [wg2_b, glr_b], K.psb[pg])
            act(K, l1[:, :], K.ps[pg][:, :], AF.Exp, [K.psb[pg], par_b], [l1_b], scale=-1.0, bias=par[:, hd:hd + 1])
            act(K, l2[:, :], l1[:, :], AF.Ln, [l1_b], [l2_b], bias=1.0)
            yield
        else:
            pq = bank(K, NB)
            for d in range(8):
                mm(K, K.ps[pq][:, :], win[:, d, QOFF + hd * 128:QOFF + (hd + 1) * 128], h[:, d, :], d == 0, d == 7, [win_b, h_b], K.psb[pq])
            pk = bank(K, NB)
            for d in range(8):
                mm(K, K.ps[pk][:, :], win[:, d, KOFF + hd * 128:KOFF + (hd + 1) * 128], h[:, d, :], d == 0, d == 7, [win_b, h_b], K.psb[pk])
            act(K, l1[:, :], K.ps[pk][:, :], AF.Exp, [K.psb[pk]], [l1_b], scale=-1.0)
            act(K, l2[:, :], l1[:, :], AF.Ln, [l1_b], [l2_b], bias=1.0)
            act(K, l1[:, :], l1[:, :], AF.Ln, [l1_b, par_b], [l1_b], scale=par[:, hd:hd + 1], bias=1.0)
            act(K, qs[:, :], K.ps[pq][:, :], AF.Silu, [K.psb[pq]], [qs_b])
            yield
            tt(K, "dve", l2[:, :], l2[:, :], l1[:, :], ALU.subtract, [l1_b, l2_b], [l2_b])
            act(K, kk[:, :], l2[:, :], AF.Exp, [l2_b], [kk_b], scale=-1.0)
            ts(K, "dve", kk[:, :], kk[:, :], -1.0, 1.0, ALU.mult, ALU.add, [kk_b], [kk_b])
        K.P.op("dve", lambda e: e.tensor_tensor_scan(out=cl[:, :], data0=msk, data1=l2[:, :], initial=0.0, op0=ALU.mult, op1=ALU.add),
               reads=[l2_b, K.cst_b], writes=[cl_b])
        tt(K, "dve", c3(dA), c3(cl), c3(cl)[:, :, 32:33].to_broadcast([128, 8, 64]), ALU.subtract, [cl_b], [dA_b])
        tt(K, "dve", c3(dD), c3(cl)[:, :, 63:64].to_broadcast([128, 8, 64]), c3(cl), ALU.subtract, [cl_b], [dD_b])
        yield
        act(K, eA[:, :], dA[:, :], AF.Exp, [dA_b], [eA_b], scale=-dsc)
        act(K, eB[:, :], dA[:, :], AF.Exp, [dA_b], [eB_b], scale=dsc)
        act(K, eC_[:, :], cl[:, :], AF.Exp, [cl_b], [eC_b_], scale=-dsc)
        act(K, eD[:, :], dD[:, :], AF.Exp, [dD_b], [eD_b], scale=-dsc)
        yield
        stt(K, qt[:, :], qs[:, :], qscale, eA[:, :], ALU.mult, ALU.mult, [qs_b, eA_b], [qt_b])
        stt(K, qd[:, :], qs[:, :], qscale, eC_[:, :], ALU.mult, ALU.mult, [qs_b, eC_b_], [qd_b])
        tt(K, "dve", kt[:, :], kk[:, :], eB[:, :], ALU.mult, [kk_b, eB_b], [kt_b])
        tt(K, "dve", kd[:, :], kk[:, :], eD[:, :], ALU.mult, [kk_b, eD_b], [kd_b])
        yield

    def stageB(hd, nxt):
        s_ = hd % 2
        (qt, qt_b), (kt, kt_b), (kd, kd_b) = qtS[s_], ktS[s_], kdS[s_]
        eC_, eC_b_ = eCS[s_]
        stm, stm_b = stmS[s_]
        Sb8, Sb8_b = Sb8S[s_]
        S, S_b = S32[hd]
        step = lambda: next(nxt, None) if nxt is not None else None
        for tg in range(4):
            sl_ = slice(tg * 128, (tg + 1) * 128)
            mm(K, K.ps[PS_S][:, sl_], kt[:, sl_], qt[:, sl_], True, True, [kt_b, qt_b], K.psb[PS_S])
        tt(K, "dve", stm[:, :, :], K.ps[PS_S][:, :].rearrange("p (g i) -> p g i", i=128), bd4, ALU.mult, [K.psb[PS_S], K.cst_b], [stm_b])
        ptv = K.ps[PS_T][:, 0:256].bitcast(BF16)
        for tg in range(4):
            tr(K, ptv[:, tg * 128:(tg + 1) * 128], kd[:, tg * 128:(tg + 1) * 128], K.identb[:, :], [kd_b], K.psb[PS_T])
        cp(K, "act", kdt[:, :, :], ptv.rearrange("p (g d) -> p g d", d=128), [K.psb[PS_T]], [kdt_b])
        cp(K, "act", Sb8[:, 0, :], S[:, :], [S_b], [Sb8_b])
        CPB = 512 // DV
        NHALF = 8 // (2 * CPB)

        def kvloc(c):
            cc = c % (2 * CPB)
            return PS_KV[cc % 2], slice((cc // 2) * DV, (cc // 2 + 1) * DV)

        for hf in range(NHALF):
            cr = range(hf * 2 * CPB, (hf + 1) * 2 * CPB)
            for c in cr:
                tg, ci = c // 2, c % 2
                rows = slice(ci * 64, (ci + 1) * 64)
                kb, ks = kvloc(c)
                mm(K, K.ps[kb][:, ks], kdt[rows, tg, :], vt[rows, tg, hd * DV:(hd + 1) * DV], True, True, [kdt_b, vt_b], K.psb[kb])
            if hf == 0:
                step()
            for c in cr:
                kb, ks = kvloc(c)
                src, src_b = (S[:, :], S_b) if c == 0 else (Sall[:, c - 1, :], Sall_b)
                dst, dst_b = (S[:, :], S_b) if c == 7 else (Sall[:, c, :], Sall_b)
                stt(K, dst, src, eC_[:, c * 64 + 63:c * 64 + 64], K.ps[kb][:, ks], ALU.mult, ALU.add,
                    [src_b, eC_b_, K.psb[kb]], [dst_b])
        cp(K, "act", Sb8[:, 1:8, :], Sall[:, :, :], [Sall_b], [Sb8_b])
        step()
        step()

    def stageC(hd):
        s_ = hd % 2
        qd, qd_b = qdS[s_]
        stm, stm_b = stmS[s_]
        Sb8, Sb8_b = Sb8S[s_]
        PO = POS[s_]
        for tg in range(4):
            sl_ = slice(tg * 128, (tg + 1) * 128)
            for ec in range(DVC):
                mm(K, K.ps[PO[ec]][:, sl_], vt[:, tg, hd * DV + ec * 128:hd * DV + (ec + 1) * 128], stm[:, tg, :], True, False,
                   [vt_b, stm_b], K.psb[PO[ec]])
            for ci in range(2):
                c = tg * 2 + ci
                cs = slice(c * 64, (c + 1) * 64)
                for ec in range(DVC):
                    mm(K, K.ps[PO[ec]][:, cs], Sb8[:, c, ec * 128:(ec + 1) * 128], qd[:, cs], False, ci == 1, [Sb8_b, qd_b], K.psb[PO[ec]])
        for ec in range(DVC):
            act(K, sqo[:, ec, :], K.ps[PO[ec]][:, :], AF.Square, [K.psb[PO[ec]]], [sqo_b])
        bcast_stat(K, [sqo[:, ec, :] for ec in range(DVC)], [sqo_b], rs[:, :], rs_b, 1.0 / DV)
        for ec in range(DVC):
            stt(K, tmpo[:, :], K.ps[PO[ec]][:, :], col("gn", ec), rs[:, :], ALU.mult, ALU.mult, [K.psb[PO[ec]], cols_b, rs_b], [tmpo_b])
            tt(K, "dve", og[:, hd * DVC + ec, :], tmpo[:, :], gate[:, hd * DVC + ec, :], ALU.mult, [tmpo_b, gate_b], [og_b])

    xp = XPipe(K, ph, cols[:, off["g"]:off["g"] + 8], cols_b, xt, xt_b, h, h_b, sq, sq_b, rstd, rstd_b)
    xp.load(0)
    xp.square()
    xp.make_h()
    for t in range(K.NT):
        nxt = t + 1 < K.NT
        if nxt:
            xp.load(t + 1)
        if t % K.TPS == 0:
            for hd in range(NH):
                memset(K, "pool", S32[hd][0][:, :], 0.0, [S32[hd][1]])
        if gla:
            pb = bank(K, NB)
            for d in range(8):
                mm(K, K.ps[pb][0:16, :], win[:, d, 3072:3088], h[:, d, :], d == 0, d == 7, [win_b, h_b], K.psb[pb])
            cp(K, "act", glr[:, :], K.ps[pb][0:16, :], [K.psb[pb]], [glr_b])
        g0 = pre(0)
        next(g0)
        for tg in range(4):
            for half in range(2):
                pb = bank(K, NB)
                for d in range(8):
                    mm(K, K.ps[pb][:, :], h[:, d, tg * 128:(tg + 1) * 128], win[:, d, VOFF + half * 512:VOFF + (half + 1) * 512], d == 0, d == 7,
                       [win_b, h_b], K.psb[pb])
                cp(K, "act" if half else "dve", vt[:, tg, half * 512:(half + 1) * 512], K.ps[pb][:, :], [K.psb[pb]], [vt_b])
            next(g0, None)
        for _ in g0:
            pass
        for c in range(8):
            pb = bank(K, NB)
            for d in range(8):
                mm(K, K.ps[pb][:, :], win[:, d, GOFF + c * 128:GOFF + (c + 1) * 128], h[:, d, :], d == 0, d == 7, [win_b, h_b], K.psb[pb])
            act(K, gate[:, c, :], K.ps[pb][:, :], AF.Silu, [K.psb[pb]], [gate_b])
        pend = pre(1)
        stageB(0, pend)
        for hd in range(NH):
            if hd + 1 < NH:
                for _ in pend:
                    pass
            stageC(hd)
            if hd + 1 < NH:
                pend = pre(hd + 2) if hd + 2 < NH else None
                stageB(hd + 1, pend)
        if nxt:
            xp.square()
            xp.make_h()
        xp.out_proj(t, wout, wout_b, og, og_b)
    K.nb = 6
    ph.close()


PHASES = {
    "tin": phase_tin,
    "tout": lambda K: phase_fin(K, False),
    "fin": lambda K: phase_fin(K, True),
    "gla": lambda K: phase_lin(K, "gla"),
    "hg": lambda K: phase_lin(K, "hg"),
    "conf": phase_conf,
    "sgu": phase_sgu,
    "ffn0": lambda K: phase_ffn(K, 0),
    "ffn1": lambda K: phase_ffn(K, 1),
    "ffn2": lambda K: phase_ffn(K, 2),
    "ffn3": lambda K: phase_ffn(K, 3),
}
ALL_PHASES = ["tin", "gla", "ffn0", "conf", "ffn1", "sgu", "ffn2", "hg", "ffn3", "fin"]


def make_consts():
    c = np.zeros((128, 1024), np.float32)
    j = np.arange(128)[:, None]
    i = np.arange(128)[None, :]
    c[:, 0:128] = (j == i)
    c[:, 128:256] = (j <= i)
    c[:, 256:384] = (j <= i) & ((j // 64) == (i // 64))
    t = np.arange(512)[None, :]
    c[:, 384:896] = (t % 64 != 0)
    c[:, 896:1024] = 1.0
    return c


def build(phases, NSEQ, SEQ):
    NTOK = NSEQ * SEQ
    nc = bass.Bass("TRN2", target_bir_lowering=False)
    K = Ctx()
    K.nc = nc
    K.d = LazyInputs(nc)
    K.x = nc.dram_tensor("x", [NTOK, D], F32, kind="ExternalInput").ap()
    cst = nc.dram_tensor("cst", [128, 1024], F32, kind="ExternalInput").ap()
    K.out = nc.dram_tensor("out", [NTOK, D], F32, kind="ExternalOutput").ap()
    K.xT = nc.dram_tensor("xT", [D, NTOK], F32, kind="Internal").ap()
    K.NT = NTOK // TT
    K.TPS = SEQ // TT
    K.rr = 0
    K.nb = 6
    with ExitStack() as st:
        P = Prog(nc, st)
        K.P = P
        K.ld = P.dma_sem("ld")
        K.stx = P.dma_sem("stx")
        K.wld = P.dma_sem("wld")
        K.pld = P.dma_sem("pld")
        K.rl = [P.dma_sem(f"rl{i}") for i in range(4)]
        K.rs = [P.dma_sem(f"rs{i}") for i in range(4)]
        K.cstf = st.enter_context(nc.sbuf_tensor("cstf", [128, 1024], F32))
        K.identb = st.enter_context(nc.sbuf_tensor("identb", [128, 128], BF16))
        K.onesb = st.enter_context(nc.sbuf_tensor("onesb", [128, 128], BF16))
        K.cst_b = Buf()
        K.ps = [st.enter_context(nc.psum_tensor(f"ps{i}", [128, 512], F32)) for i in range(8)]
        K.psb = [Buf() for _ in range(8)]
        P.dma([("sp", K.cstf[:, :], cst)], K.pld, writes=[K.cst_b])
        cp(K, "dve", K.identb[:, :], K.cstf[:, 0:128], [K.cst_b], [K.cst_b])
        cp(K, "dve", K.onesb[:, :], K.cstf[:, 896:1024], [K.cst_b], [K.cst_b])
        for p in phases:
            PHASES[p](K)
        for k, v in P.cnt.items():
            if v:
                P.ops["sp"].append(lambda e, sem=P.semh[k], v=v: e.wait_ge(sem, v))
        P.flush()
        K.ninstr = P.ninstr
    return nc, K


def run(phases, x, inputs, NSEQ, SEQ, ncores, trace=False):
    nc, K = build(phases, NSEQ, SEQ)
    cst = make_consts()
    in_maps = []
    for i in range(ncores):
        m = {name: np.ascontiguousarray(inputs[name], dtype=np.float32) for name in K.d.keys()}
        m["x"] = np.ascontiguousarray(x[i])
        m["cst"] = cst
        in_maps.append(m)
    res = run_bass_kernel_spmd(nc, in_maps, core_ids=list(range(ncores)), trace=trace)
    return np.stack([res.results[i]["out"] for i in range(ncores)]), res


def kernel(**inputs):
    x = np.asarray(inputs["x"], dtype=np.float32)
    B, S, _ = x.shape
    nseq = B // NCORES
    xs = x.reshape(NCORES, nseq * S, D)
    out, _ = run(ALL_PHASES, xs, inputs, nseq, S, NCORES)
    return out.reshape(B, S, D).astype(np.float32)
```

```python
import numpy as np
from contextlib import ExitStack
import concourse.bass as bass
import concourse.mybir as mybir
from concourse.bass_utils import run_bass_kernel_spmd

F32 = mybir.dt.float32
BF16 = mybir.dt.bfloat16
AF = mybir.ActivationFunctionType
ALU = mybir.AluOpType

LINV = 0
LINSTOP = 9
LINX = 0
D = 1024
TT = 512
EPS = 1e-6
FH = 2816
NCORES = 8

INPUT_SHAPES = [
    ("norm_mix", [4, 1024]), ("norm_ffn", [4, 1024]), ("norm_final", [1024]),
    ("gla_w_in", [1, 1024, 3088]), ("gla_w_g2", [1, 16, 512]), ("gla_b_g2", [1, 512]),
    ("gla_norm", [1, 256]), ("gla_w_out", [1, 1024, 1024]),
    ("cv_w_in", [1, 1024, 2048]), ("cv_b_in", [1, 2048]), ("cv_w_dw", [1, 31, 1024]),
    ("cv_b_dw", [1, 1024]), ("cv_ln_g", [1, 1024]), ("cv_ln_b", [1, 1024]),
    ("cv_w_out", [1, 1024, 1024]), ("cv_b_out", [1, 1024]),
    ("sg_w_in", [1, 1024, 2048]), ("sg_b_in", [1, 2048]), ("sg_ln_g", [1, 1024]),
    ("sg_ln_b", [1, 1024]), ("sg_w_s", [1, 8, 128, 128]), ("sg_b_s", [1, 8, 128]),
    ("sg_w_out", [1, 1024, 1024]), ("sg_b_out", [1, 1024]),
    ("hg_w_in", [1, 1024, 4096]), ("hg_lb_table", [4, 1024]), ("hg_norm", [1, 128]),
    ("hg_w_out", [1, 1024, 1024]),
    ("ffn_w_up", [4, 1024, 5632]), ("ffn_w_dw", [4, 3, 5632]), ("ffn_w_down", [4, 2816, 1024]),
]


class Buf:
    __slots__ = ("w", "r")

    def __init__(self):
        self.w = None
        self.r = {}


class Prog:
    ENG = ("pe", "dve", "act", "pool", "sp")

    def __init__(self, nc, stack):
        self.nc = nc
        self.stack = stack
        self.ops = {e: [] for e in self.ENG}
        self.semh = {}
        self.cnt = {}
        for e in ("pe", "dve", "act", "pool"):
            self.semh[e] = stack.enter_context(nc.semaphore("s_" + e))
            self.cnt[e] = 0
        self.seen = {e: {} for e in self.ENG}
        self.ninstr = 0

    def dma_sem(self, name):
        self.semh[name] = self.stack.enter_context(self.nc.semaphore(name))
        self.cnt[name] = 0
        return name

    def _waits(self, eng, reads, writes):
        deps = {}
        for b in reads:
            if b.w is not None:
                k, v = b.w
                if deps.get(k, 0) < v:
                    deps[k] = v
        for b in writes:
            if b.w is not None:
                k, v = b.w
                if deps.get(k, 0) < v:
                    deps[k] = v
            for k, v in b.r.items():
                if deps.get(k, 0) < v:
                    deps[k] = v
        seen = self.seen[eng]
        for k, v in deps.items():
            if k == eng:
                if eng == "pe":
                    continue
                if v <= self.cnt[eng] - 3:
                    continue
            if seen.get(k, 0) >= v:
                continue
            seen[k] = v
            sem = self.semh[k]
            self.ninstr += 1
            self.ops[eng].append(lambda e, sem=sem, v=v: e.wait_ge(sem, v))

    def _mark(self, ev, reads, writes):
        k, v = ev
        for b in reads:
            if b.r.get(k, 0) < v:
                b.r[k] = v
        for b in writes:
            b.w = ev
            b.r = {}

    def op(self, eng, fn, reads=(), writes=()):
        self._waits(eng, reads, writes)
        self.cnt[eng] += 1
        sem = self.semh[eng]
        self.ninstr += 1
        self.ops[eng].append(lambda e, fn=fn, sem=sem: fn(e).then_inc(sem, 1))
        self._mark((eng, self.cnt[eng]), reads, writes)

    def dma(self, pairs, sem, reads=(), writes=()):
        for q in sorted(set(p[0] for p in pairs)):
            self._waits(q, reads, writes)
            if self.cnt[sem] and self.seen[q].get(sem, 0) < self.cnt[sem]:
                self.seen[q][sem] = self.cnt[sem]
                self.ops[q].append(lambda e, s_=self.semh[sem], v=self.cnt[sem]: e.wait_ge(s_, v))
        h = self.semh[sem]
        for (q, o, i) in pairs:
            self.cnt[sem] += 16
            self.ninstr += 1
            self.ops[q].append(lambda e, o=o, i=i, h=h: e.dma_start(out=o, in_=i).then_inc(h, 16))
        self._mark((sem, self.cnt[sem]), reads, writes)

    def barrier(self):
        for eng in self.ENG:
            seen = self.seen[eng]
            for k, v in self.cnt.items():
                if v == 0 or seen.get(k, 0) >= v:
                    continue
                if k == eng and eng == "pe":
                    continue
                seen[k] = v
                sem = self.semh[k]
                self.ops[eng].append(lambda e, sem=sem, v=v: e.wait_ge(sem, v))

    def flush(self):
        ops = self.ops
        with self.nc.Block() as block:
            @block.tensor
            def _(e):
                for c in ops["pe"]:
                    c(e)

            @block.vector
            def _(e):
                for c in ops["dve"]:
                    c(e)

            @block.scalar
            def _(e):
                for c in ops["act"]:
                    c(e)

            @block.gpsimd
            def _(e):
                for c in ops["pool"]:
                    c(e)

            @block.sync
            def _(e):
                for c in ops["sp"]:
                    c(e)
        self.ops = {e: [] for e in self.ENG}


class Ctx:
    pass


class LazyInputs(dict):
    def __init__(self, nc):
        super().__init__()
        self.nc = nc
        self.shapes = dict(INPUT_SHAPES)

    def __missing__(self, name):
        ap = self.nc.dram_tensor(name, self.shapes[name], F32, kind="ExternalInput").ap()
        self[name] = ap
        return ap


def mm(K, out, lhsT, rhs, start, stop, reads, wb):
    K.P.op("pe", lambda e: e.matmul(out, lhsT=lhsT, rhs=rhs, start=start, stop=stop), reads=reads, writes=[wb])


def tr(K, out, in_, ident, reads, wb):
    K.P.op("pe", lambda e: e.transpose(out=out, in_=in_, identity=ident), reads=list(reads) + [K.cst_b], writes=[wb])


def act(K, out, in_, func, reads, writes, **kw):
    K.P.op("act", lambda e: e.activation(out=out, in_=in_, func=func, **kw), reads=reads, writes=writes)


def tt(K, eng, out, in0, in1, op, reads, writes):
    K.P.op(eng, lambda e: e.tensor_tensor(out=out, in0=in0, in1=in1, op=op), reads=reads, writes=writes)


def stt(K, out, in0, scalar, in1, op0, op1, reads, writes):
    K.P.op("dve", lambda e: e.scalar_tensor_tensor(out=out, in0=in0, scalar=scalar, in1=in1, op0=op0, op1=op1),
           reads=reads, writes=writes)


def ts(K, eng, out, in0, s1, s2, op0, op1, reads, writes):
    if s2 is None:
        K.P.op(eng, lambda e: e.tensor_scalar(out=out, in0=in0, scalar1=s1, scalar2=None, op0=op0), reads=reads, writes=writes)
    else:
        K.P.op(eng, lambda e: e.tensor_scalar(out=out, in0=in0, scalar1=s1, scalar2=s2, op0=op0, op1=op1), reads=reads, writes=writes)


def cp(K, eng, out, in_, reads, writes):
    if eng == "act":
        act(K, out, in_, AF.Copy, reads, writes)
    else:
        K.P.op(eng, lambda e: e.tensor_copy(out=out, in_=in_), reads=reads, writes=writes)


def memset(K, eng, ap, val, writes):
    K.P.op(eng, lambda e: e.memset(ap, val), writes=writes)


def bank(K, n=None):
    n = n or K.nb
    b = K.rr % n
    K.rr += 1
    return b


class Phase:
    def __init__(self, K, name):
        self.K = K
        self.name = name
        self.st = ExitStack()
        self.n = 0

    def sb(self, shape, dt, name=None):
        self.n += 1
        t = self.st.enter_context(self.K.nc.sbuf_tensor(f"{self.name}_{name or self.n}", list(shape), dt))
        return t, Buf()

    def close(self):
        K = self.K
        K.P.barrier()
        K.P.flush()
        self.st.close()


def load_cols(K, ph, specs):
    P = K.P
    tot = sum(ap.shape[0] for _, ap in specs)
    G = (tot + 127) // 128
    stg, stg_b = ph.sb([128, G, 128], F32, "colstg")
    cols, cols_b = ph.sb([128, G * 128], F32, "cols")
    off = {}
    pairs = []
    r = 0
    for key, ap in specs:
        off[key] = r
        R = ap.shape[0]
        s = 0
        while s < R:
            n = min(R - s, 128 - (r % 128))
            pairs.append(("sp", stg[r % 128:r % 128 + n, r // 128, :], ap[s:s + n, :]))
            s += n
            r += n
    P.dma(pairs, K.pld, writes=[stg_b])
    for g in range(G):
        rows = min(128, tot - g * 128)
        pb = bank(K)
        tr(K, K.ps[pb][:, 0:rows], stg[0:rows, g, :], K.cstf[0:rows, 0:rows], [stg_b], K.psb[pb])
        cp(K, "act", cols[:, g * 128:g * 128 + rows], K.ps[pb][:, 0:rows], [K.psb[pb]], [cols_b])
    return cols, cols_b, off


def load_bcast(K, ph, specs):
    P = K.P
    tot = sum(ap.shape[1] for _, ap in specs)
    row, row_b = ph.sb([1, tot], F32, "bcrow")
    bc, bc_b = ph.sb([128, tot], F32, "bc")
    off = {}
    pairs = []
    c = 0
    for key, ap in specs:
        off[key] = c
        pairs.append(("sp", row[0:1, c:c + ap.shape[1]], ap))
        c += ap.shape[1]
    P.dma(pairs, K.pld, writes=[row_b])
    for s in range(0, tot, 512):
        n = min(512, tot - s)
        pb = bank(K)
        mm(K, K.ps[pb][:, 0:n], K.cstf[0:1, 896:1024], row[0:1, s:s + n], True, True, [row_b, K.cst_b], K.psb[pb])
        cp(K, "act", bc[:, s:s + n], K.ps[pb][:, 0:n], [K.psb[pb]], [bc_b])
    return bc, bc_b, off


def load_x(K, t, xt, xt_b):
    K.P.dma([("sp", xt[:, :, :], K.xT[:, t * TT:(t + 1) * TT].rearrange("(c p) t -> p c t", p=128))],
            K.ld, writes=[xt_b])


def store_x(K, t, xt, xt_b):
    K.P.dma([("sp", K.xT[:, t * TT:(t + 1) * TT].rearrange("(c p) t -> p c t", p=128), xt[:, :, :])],
            K.stx, reads=[xt_b])


def bcast_stat(K, srcs, src_bufs, out, out_b, scale, func=None, bias=EPS):
    pb = bank(K)
    n = len(srcs)
    for i, s in enumerate(srcs):
        mm(K, K.ps[pb][:, 0:out.shape[-1]], K.onesb[:, :], s, i == 0, i == n - 1, list(src_bufs) + [K.cst_b], K.psb[pb])
    if func is None:
        act(K, out, K.ps[pb][:, 0:out.shape[-1]], AF.Ln, [K.psb[pb]], [out_b], scale=scale, bias=bias)
        act(K, out, out, AF.Exp, [out_b], [out_b], scale=-0.5)
    else:
        act(K, out, K.ps[pb][:, 0:out.shape[-1]], func, [K.psb[pb]], [out_b], scale=scale, bias=bias)
    return pb


def rmsnorm_h(K, xt, xt_b, g, g_b, sq, sq_b, rstd, rstd_b, h, h_b):
    act(K, sq[:, 0:8, :], xt[:, :, :], AF.Square, [xt_b], [sq_b])
    bcast_stat(K, [sq[:, c, :] for c in range(8)], [sq_b], rstd[:, :], rstd_b, 1.0 / D)
    for c in range(8):
        stt(K, h[:, c, :], xt[:, c, :], g[:, c:c + 1], rstd[:, :], ALU.mult, ALU.mult, [xt_b, g_b, rstd_b], [h_b])


class XPipe:
    def __init__(self, K, ph, g, g_b, xt, xt_b, h, h_b, sq, sq_b, rstd, rstd_b, nring=4):
        self.K, self.g, self.g_b = K, g, g_b
        self.xt, self.xt_b, self.h, self.h_b = xt, xt_b, h, h_b
        self.sq, self.sq_b, self.rstd, self.rstd_b = sq, sq_b, rstd, rstd_b
        self.ring = [ph.sb([128, TT], F32, f"xr{i}") for i in range(nring)]
        self.r = 0

    def load(self, t):
        load_x(self.K, t, self.xt, self.xt_b)

    def square(self):
        act(self.K, self.sq[:, 0:8, :], self.xt[:, :, :], AF.Square, [self.xt_b], [self.sq_b])

    def make_h(self):
        K = self.K
        bcast_stat(K, [self.sq[:, c, :] for c in range(8)], [self.sq_b], self.rstd[:, :], self.rstd_b, 1.0 / D)
        for c in range(8):
            stt(K, self.h[:, c, :], self.xt[:, c, :], self.g[:, c:c + 1], self.rstd[:, :], ALU.mult, ALU.mult,
                [self.xt_b, self.g_b, self.rstd_b], [self.h_b])

    def res_load(self, t, c):
        K = self.K
        i = self.r % len(self.ring)
        self.r += 1
        xr, xr_b = self.ring[i]
        K.P.dma([("sp", xr[:, :], K.xT[c * 128:(c + 1) * 128, t * TT:(t + 1) * TT])], K.rl[i], writes=[xr_b])
        return i

    def res_add_store(self, t, c, i, pb, bias=None, bias_b=None):
        K = self.K
        xr, xr_b = self.ring[i]
        if bias is None:
            tt(K, "dve", xr[:, :], xr[:, :], K.ps[pb][:, :], ALU.add, [xr_b, K.psb[pb]], [xr_b])
        else:
            stt(K, xr[:, :], K.ps[pb][:, :], bias, xr[:, :], ALU.add, ALU.add, [K.psb[pb], bias_b, xr_b], [xr_b])
        K.P.dma([("sp", K.xT[c * 128:(c + 1) * 128, t * TT:(t + 1) * TT], xr[:, :])], K.rs[i], reads=[xr_b])

    def out_proj(self, t, wout, wout_b, src, src_b, bias_fn=None, bias_b=None):
        K = self.K
        for half in range(2):
            slots = [self.res_load(t, c) for c in range(half * 4, half * 4 + 4)]
            for c in range(half * 4, half * 4 + 4):
                pb = bank(K)
                for d in range(8):
                    mm(K, K.ps[pb][:, :], wout[:, d, c * 128:(c + 1) * 128], src[:, d, :], d == 0, d == 7, [wout_b, src_b], K.psb[pb])
                self.res_add_store(t, c, slots[c - half * 4], pb, None if bias_fn is None else bias_fn(c), bias_b)


def phase_tin(K):
    ph = Phase(K, "tin")
    xins = [ph.sb([128, 4, D], F32) for _ in range(2)]
    xts = [ph.sb([128, 8, TT], F32) for _ in range(2)]
    for t in range(K.NT):
        xin, xin_b = xins[t % 2]
        xt, xt_b = xts[t % 2]
        K.P.dma([("sp", xin[:, :, :], K.x[t * TT:(t + 1) * TT, :].rearrange("(g p) d -> p g d", p=128))], K.rl[t % 2], writes=[xin_b])
        for c in range(8):
            pb = bank(K)
            for g in range(4):
                tr(K, K.ps[pb][:, g * 128:(g + 1) * 128], xin[:, g, c * 128:(c + 1) * 128], K.cstf[:, 0:128], [xin_b], K.psb[pb])
            cp(K, "act" if c % 2 else "dve", xt[:, c, :], K.ps[pb][:, :], [K.psb[pb]], [xt_b])
        K.P.dma([("sp", K.xT[:, t * TT:(t + 1) * TT].rearrange("(c p) t -> p c t", p=128), xt[:, :, :])], K.rs[t % 2], reads=[xt_b])
    ph.close()


def phase_fin(K, norm):
    ph = Phase(K, "fin")
    xts = [ph.sb([128, 8, TT], F32) for _ in range(2)]
    yos = [ph.sb([128, 4, D], F32) for _ in range(2)]
    if norm:
        cols, cols_b, off = load_cols(K, ph, [("g", K.d["norm_final"].rearrange("(c p) -> c p", p=128))])
        sq, sq_b = ph.sb([128, 8, TT], BF16)
        rstd, rstd_b = ph.sb([128, TT], F32)
    for t in range(K.NT):
        xt, xt_b = xts[t % 2]
        yo, yo_b = yos[t % 2]
        K.P.dma([("sp", xt[:, :, :], K.xT[:, t * TT:(t + 1) * TT].rearrange("(c p) t -> p c t", p=128))], K.rl[t % 2], writes=[xt_b])
        if norm:
            act(K, sq[:, :, :], xt[:, :, :], AF.Square, [xt_b], [sq_b])
            bcast_stat(K, [sq[:, c, :] for c in range(8)], [sq_b], rstd[:, :], rstd_b, 1.0 / D)
            for c in range(8):
                stt(K, xt[:, c, :], xt[:, c, :], cols[:, off["g"] + c:off["g"] + c + 1], rstd[:, :], ALU.mult, ALU.mult,
                    [xt_b, cols_b, rstd_b], [xt_b])
        for g in range(4):
            for half in range(2):
                pb = bank(K)
                for c4 in range(4):
                    c = half * 4 + c4
                    tr(K, K.ps[pb][:, c4 * 128:(c4 + 1) * 128], xt[:, c, g * 128:(g + 1) * 128], K.cstf[:, 0:128], [xt_b], K.psb[pb])
                cp(K, "act" if half else "dve", yo[:, g, half * 512:(half + 1) * 512], K.ps[pb][:, :], [K.psb[pb]], [yo_b])
        K.P.dma([("sp", K.out[t * TT:(t + 1) * TT, :].rearrange("(g p) d -> p g d", p=128), yo[:, :, :])], K.rs[t % 2], reads=[yo_b])
    ph.close()


def phase_ffn(K, li):
    P = K.P
    ph = Phase(K, f"ffn{li}")
    K.nb = 4
    wup, wup_b = ph.sb([128, 8, 2 * FH], BF16, "wup")
    wdn, wdn_b = ph.sb([128, 22, D], BF16, "wdn")
    pairs = [("pool", wup[:, c, :], K.d["ffn_w_up"][li, c * 128:(c + 1) * 128, :]) for c in range(8)]
    pairs += [("pool", wdn[:, j, :], K.d["ffn_w_down"][li, j * 128:(j + 1) * 128, :]) for j in range(22)]
    P.dma(pairs, K.wld, writes=[wup_b, wdn_b])
    cols, cols_b, off = load_cols(K, ph, [
        ("g", K.d["norm_ffn"][li].rearrange("(c p) -> c p", p=128)),
        ("dw", K.d["ffn_w_dw"][li].rearrange("k (c p) -> (k c) p", p=128)),
    ])
    xt, xt_b = ph.sb([128, 8, TT], F32, "xt")
    h, h_b = ph.sb([128, 8, TT], BF16, "h")
    u, _ = ph.sb([128, 22, TT], BF16, "u")
    u_bs = [Buf() for _ in range(22)]
    rstd, rstd_b = ph.sb([128, TT], F32, "rstd")
    NZ = 2
    zs = [ph.sb([128, TT + 2], F32, f"zs{i}") for i in range(NZ)]
    acc = [ph.sb([128, TT], F32, f"acc{i}") for i in range(4)]
    ztail, ztail_b = ph.sb([128, 44, 2], F32, "ztail")
    NR = 4
    ring = [ph.sb([128, TT], F32, f"xr{i}") for i in range(NR)]
    g = cols[:, off["g"]:off["g"] + 8]
    dw0 = off["dw"]
    ACC = [4, 5, 6, 7]
    st = {"k": 0, "r": 0}

    def norm_a(t):
        load_x(K, t, xt, xt_b)
        act(K, h[:, :, :], xt[:, :, :], AF.Square, [xt_b], [h_b])

    def norm_b():
        bcast_stat(K, [h[:, c, :] for c in range(8)], [h_b], rstd[:, :], rstd_b, 1.0 / D)
        for c in range(8):
            stt(K, h[:, c, :], xt[:, c, :], g[:, c:c + 1], rstd[:, :], ALU.mult, ALU.mult, [xt_b, cols_b, rstd_b], [h_b])

    def res_load(t, c):
        i = st["r"] % NR
        st["r"] += 1
        xr, xr_b = ring[i]
        P.dma([("sp", xr[:, :], K.xT[c * 128:(c + 1) * 128, t * TT:(t + 1) * TT])], K.rl[i], writes=[xr_b])
        return i

    def res_add_store(t, c, i, pb):
        xr, xr_b = ring[i]
        tt(K, "dve", xr[:, :], xr[:, :], K.ps[pb][:, :], ALU.add, [xr_b, K.psb[pb]], [xr_b])
        P.dma([("sp", K.xT[c * 128:(c + 1) * 128, t * TT:(t + 1) * TT], xr[:, :])], K.rs[i], reads=[xr_b])

    norm_a(0)
    norm_b()
    for t in range(K.NT):
        if t % K.TPS == 0:
            memset(K, "pool", ztail[:, :, :], 0.0, [ztail_b])
        for j in range(22):
            a2 = []
            for which, fc in ((0, j), (1, 22 + j)):
                pb = bank(K)
                for d in range(8):
                    mm(K, K.ps[pb][:, :], wup[:, d, fc * 128:(fc + 1) * 128], h[:, d, :], d == 0, d == 7, [wup_b, h_b], K.psb[pb])
                z, z_b = zs[st["k"] % NZ]
                a, a_b = acc[(2 * (j % 2)) + which]
                st["k"] += 1
                cp(K, "pool", z[:, 0:2], ztail[:, fc, :], [ztail_b], [z_b])
                act(K, z[:, 2:TT + 2], K.ps[pb][:, :], AF.Copy, [K.psb[pb]], [z_b])
                cp(K, "pool", ztail[:, fc, :], z[:, TT:TT + 2], [z_b], [ztail_b])
                act(K, a[:, :], z[:, 0:TT], AF.Copy, [z_b, cols_b], [a_b], scale=cols[:, dw0 + fc:dw0 + fc + 1])
                stt(K, a[:, :], z[:, 1:TT + 1], cols[:, dw0 + 44 + fc:dw0 + 44 + fc + 1], a[:, :], ALU.mult, ALU.add, [z_b, cols_b, a_b], [a_b])
                stt(K, a[:, :], z[:, 2:TT + 2], cols[:, dw0 + 88 + fc:dw0 + 88 + fc + 1], a[:, :], ALU.mult, ALU.add, [z_b, cols_b, a_b], [a_b])
                a2.append((a, a_b))
            (ag, ag_b), (av, av_b) = a2
            act(K, ag[:, :], ag[:, :], AF.Silu, [ag_b], [ag_b])
            tt(K, "dve", u[:, j, :], ag[:, :], av[:, :], ALU.mult, [ag_b, av_b], [u_bs[j]])
        nxt = t + 1 < K.NT
        if nxt:
            norm_a(t + 1)
        slots = [res_load(t, c) for c in range(4)]
        for c in range(4):
            for j in range(20):
                mm(K, K.ps[ACC[c]][:, :], wdn[:, j, c * 128:(c + 1) * 128], u[:, j, :], j == 0, False, [wdn_b, u_bs[j]], K.psb[ACC[c]])
        for c in range(4):
            for j in (20, 21):
                mm(K, K.ps[ACC[c]][:, :], wdn[:, j, c * 128:(c + 1) * 128], u[:, j, :], False, j == 21, [wdn_b, u_bs[j]], K.psb[ACC[c]])
        for c in range(4):
            res_add_store(t, c, slots[c], ACC[c])
        if nxt:
            norm_b()
        slots = [res_load(t, c) for c in range(4, 8)]
        for c in range(4, 8):
            pb = bank(K)
            for j in range(22):
                mm(K, K.ps[pb][:, :], wdn[:, j, c * 128:(c + 1) * 128], u[:, j, :], j == 0, j == 21, [wdn_b, u_bs[j]], K.psb[pb])
            res_add_store(t, c, slots[c - 4], pb)
    K.nb = 6
    ph.close()


def phase_conf(K):
    P = K.P
    ph = Phase(K, "conf")
    win, win_b = ph.sb([128, 8, 2048], BF16, "win")
    wout, wout_b = ph.sb([128, 8, D], BF16, "wout")
    pairs = [("pool", win[:, c, :], K.d["cv_w_in"][0, c * 128:(c + 1) * 128, :]) for c in range(8)]
    pairs += [("pool", wout[:, c, :], K.d["cv_w_out"][0, c * 128:(c + 1) * 128, :]) for c in range(8)]
    P.dma(pairs, K.wld, writes=[win_b, wout_b])
    v2 = lambda a: a.rearrange("(c p) -> c p", p=128)
    cols, cols_b, off = load_cols(K, ph, [
        ("g", v2(K.d["norm_mix"][1])), ("bin", v2(K.d["cv_b_in"][0])),
        ("dw", K.d["cv_w_dw"][0].rearrange("k (c p) -> (k c) p", p=128)),
        ("bdw", v2(K.d["cv_b_dw"][0])), ("lng", v2(K.d["cv_ln_g"][0])), ("lnb", v2(K.d["cv_ln_b"][0])),
        ("bout", v2(K.d["cv_b_out"][0])),
    ])
    col = lambda key, i: cols[:, off[key] + i:off[key] + i + 1]
    xt, xt_b = ph.sb([128, 8, TT], F32, "xt")
    h, h_b = ph.sb([128, 8, TT], BF16, "h")
    csq, csq_b = ph.sb([128, 8, TT], BF16, "csq")
    sq, sq_b = csq, csq_b
    rstd, rstd_b = ph.sb([128, TT], F32, "rstd")
    glu = [ph.sb([128, TT + 30], BF16, f"glu{c}") for c in range(8)]
    cacc = [ph.sb([128, TT], F32, f"cacc{c}") for c in range(8)]
    sg = [ph.sb([128, TT], F32, f"sg{i}") for i in range(2)]
    cb, cb_b = ph.sb([128, 8, TT], BF16, "cb")
    sl, sl_b = cb, cb_b
    mean, mean_b = ph.sb([128, TT], F32, "mean")
    msq, msq_b = ph.sb([128, TT], F32, "msq")
    lrs, lrs_b = ph.sb([128, TT], F32, "lrs")
    tmp = [ph.sb([128, TT], F32, f"tmp{i}") for i in range(2)]
    dg, dg_b = ph.sb([128, 31 * 8, 128], BF16, "dg")
    for i in range(31 * 8):
        ts(K, "dve", dg[:, i, :], K.cstf[:, 0:128], col("dw", i), None, ALU.mult, None, [K.cst_b, cols_b], [dg_b])
    xp = XPipe(K, ph, cols[:, off["g"]:off["g"] + 8], cols_b, xt, xt_b, h, h_b, sq, sq_b, rstd, rstd_b)
    xp.load(0)
    xp.square()
    xp.make_h()
    for t in range(K.NT):
        nxt = t + 1 < K.NT
        if nxt:
            xp.load(t + 1)
        for c in range(8):
            gl, gl_b = glu[c]
            if t % K.TPS == 0:
                memset(K, "pool", gl[:, 0:30], 0.0, [gl_b])
            else:
                cp(K, "pool", gl[:, 0:30], gl[:, TT:TT + 30], [gl_b], [gl_b])
            pa = bank(K)
            for d in range(8):
                mm(K, K.ps[pa][:, :], win[:, d, c * 128:(c + 1) * 128], h[:, d, :], d == 0, d == 7, [win_b, h_b], K.psb[pa])
            pg = bank(K)
            for d in range(8):
                mm(K, K.ps[pg][:, :], win[:, d, D + c * 128:D + (c + 1) * 128], h[:, d, :], d == 0, d == 7, [win_b, h_b], K.psb[pg])
            s, s_b = sg[c % 2]
            act(K, s[:, :], K.ps[pg][:, :], AF.Sigmoid, [K.psb[pg], cols_b], [s_b], bias=col("bin", 8 + c))
            stt(K, gl[:, 30:TT + 30], K.ps[pa][:, :], col("bin", c), s[:, :], ALU.add, ALU.mult, [K.psb[pa], cols_b, s_b], [gl_b])
        if nxt:
            xp.square()
            xp.make_h()
        for c in range(8):
            gl, gl_b = glu[c]
            pc = bank(K)
            for kk in range(31):
                mm(K, K.ps[pc][:, :], dg[:, kk * 8 + c, :], gl[:, kk:kk + TT], kk == 0, kk == 30, [dg_b, gl_b], K.psb[pc])
            ca, ca_b = cacc[c]
            act(K, ca[:, :], K.ps[pc][:, :], AF.Identity, [K.psb[pc], cols_b], [ca_b], bias=col("bdw", c))
            cp(K, "dve", cb[:, c, :], ca[:, :], [ca_b], [cb_b])
            act(K, csq[:, c, :], ca[:, :], AF.Square, [ca_b], [csq_b])
        bcast_stat(K, [cb[:, c, :] for c in range(8)], [cb_b], mean[:, :], mean_b, 1.0 / D, func=AF.Copy, bias=0.0)
        tt(K, "dve", msq[:, :], mean[:, :], mean[:, :], ALU.mult, [mean_b], [msq_b])
        pb = bank(K)
        for c in range(8):
            mm(K, K.ps[pb][:, :], K.onesb[:, :], csq[:, c, :], c == 0, c == 7, [csq_b, K.cst_b], K.psb[pb])
        stt(K, lrs[:, :], K.ps[pb][:, :], 1.0 / D, msq[:, :], ALU.mult, ALU.subtract, [K.psb[pb], msq_b], [lrs_b])
        act(K, lrs[:, :], lrs[:, :], AF.Ln, [lrs_b], [lrs_b], bias=EPS)
        act(K, lrs[:, :], lrs[:, :], AF.Exp, [lrs_b], [lrs_b], scale=-0.5)
        for c in range(8):
            ca, ca_b = cacc[c]
            tm, tm_b = tmp[c % 2]
            tt(K, "dve", tm[:, :], ca[:, :], mean[:, :], ALU.subtract, [ca_b, mean_b], [tm_b])
            tt(K, "dve", tm[:, :], tm[:, :], lrs[:, :], ALU.mult, [tm_b, lrs_b], [tm_b])
            act(K, sl[:, c, :], tm[:, :], AF.Silu, [tm_b, cols_b], [sl_b], scale=col("lng", c), bias=col("lnb", c))
        xp.out_proj(t, wout, wout_b, sl, sl_b, lambda c: col("bout", c), cols_b)
    ph.close()


def phase_sgu(K):
    P = K.P
    ph = Phase(K, "sgu")
    win, win_b = ph.sb([128, 8, 2048], BF16, "win")
    wout, wout_b = ph.sb([128, 8, D], BF16, "wout")
    wsn, wsn_b = ph.sb([128, 8, 128], F32, "wsn")
    pairs = [("pool", win[:, c, :], K.d["sg_w_in"][0, c * 128:(c + 1) * 128, :]) for c in range(8)]
    pairs += [("pool", wout[:, c, :], K.d["sg_w_out"][0, c * 128:(c + 1) * 128, :]) for c in range(8)]
    P.dma(pairs, K.wld, writes=[win_b, wout_b])
    P.dma([("sp", wsn[:, g, :], K.d["sg_w_s"][0, g, :, :]) for g in range(8)], K.pld, writes=[wsn_b])
    v2 = lambda a: a.rearrange("(c p) -> c p", p=128)
    cols, cols_b, off = load_cols(K, ph, [
        ("g", v2(K.d["norm_mix"][2])), ("binu", v2(K.d["sg_b_in"][0, 0:D])), ("bout", v2(K.d["sg_b_out"][0])),
    ])
    col = lambda key, i: cols[:, off[key] + i:off[key] + i + 1]
    bc, bc_b, boff = load_bcast(K, ph, [
        ("binv", K.d["sg_b_in"][0:1, D:2 * D]), ("lng", K.d["sg_ln_g"][0:1, :]), ("lnb", K.d["sg_ln_b"][0:1, :]),
        ("bs", K.d["sg_b_s"][0:1, :, :].rearrange("o g i -> o (g i)")),
    ])
    wct, wct_b = ph.sb([128, 8, 128], BF16, "wct")
    for g in range(8):
        pb = bank(K)
        tr(K, K.ps[pb][:, 0:128], wsn[:, g, :], K.cstf[:, 0:128], [wsn_b], K.psb[pb])
        tt(K, "dve", wct[:, g, :], K.ps[pb][:, 0:128], K.cstf[:, 128:256], ALU.mult, [K.psb[pb], K.cst_b], [wct_b])
    bs4, bs4_b = ph.sb([128, 8, TT], F32, "bs4")
    for tg in range(4):
        cp(K, "pool", bs4[:, :, tg * 128:(tg + 1) * 128], bc[:, boff["bs"]:boff["bs"] + 1024].rearrange("p (g i) -> p g i", i=128), [bc_b], [bs4_b])
    xt, xt_b = ph.sb([128, 8, TT], F32, "xt")
    h, h_b = ph.sb([128, 8, TT], BF16, "h")
    sq, sq_b = ph.sb([128, 8, TT], BF16, "sq")
    rstd, rstd_b = ph.sb([128, TT], F32, "rstd")
    ug, ug_b = ph.sb([128, 8, TT], F32, "ug")
    vx = [ph.sb([128, D], F32, f"vx{i}") for i in range(2)]
    vnb, vnb_b = ph.sb([128, 4, D], BF16, "vnb")
    stats, stats_b = ph.sb([128, 2, 6], F32, "stats")
    mv, mv_b = ph.sb([128, 2], F32, "mv")
    tmp = [ph.sb([128, TT], F32, f"tmp{i}") for i in range(2)]
    gated, gated_b = ph.sb([128, 8, TT], BF16, "gated")
    xp = XPipe(K, ph, cols[:, off["g"]:off["g"] + 8], cols_b, xt, xt_b, h, h_b, sq, sq_b, rstd, rstd_b)
    xp.load(0)
    xp.square()
    xp.make_h()
    for t in range(K.NT):
        nxt = t + 1 < K.NT
        if nxt:
            xp.load(t + 1)
        for tg in range(4):
            vxt, vx_b = vx[tg % 2]
            for half in range(2):
                pb = bank(K)
                for d in range(8):
                    mm(K, K.ps[pb][:, :], h[:, d, tg * 128:(tg + 1) * 128], win[:, d, D + half * 512:D + (half + 1) * 512], d == 0, d == 7,
                       [win_b, h_b], K.psb[pb])
                tt(K, "dve", vxt[:, half * 512:(half + 1) * 512], K.ps[pb][:, :], bc[:, boff["binv"] + half * 512:boff["binv"] + (half + 1) * 512],
                   ALU.add, [K.psb[pb], bc_b], [vx_b])
            act(K, vxt[:, :], vxt[:, :], AF.Gelu_apprx_tanh, [vx_b], [vx_b])
            for half in range(2):
                K.P.op("dve", lambda e, half=half, vxt=vxt: e.bn_stats(out=stats[:, half, :], in_=vxt[:, half * 512:(half + 1) * 512]),
                       reads=[vx_b], writes=[stats_b])
            K.P.op("dve", lambda e: e.bn_aggr(out=mv[:, :], in_=stats[:, :, :].rearrange("p a b -> p (a b)")), reads=[stats_b], writes=[mv_b])
            act(K, mv[:, 1:2], mv[:, 1:2], AF.Ln, [mv_b], [mv_b], bias=EPS)
            act(K, mv[:, 1:2], mv[:, 1:2], AF.Exp, [mv_b], [mv_b], scale=-0.5)
            ts(K, "dve", vxt[:, :], vxt[:, :], mv[:, 0:1], mv[:, 1:2], ALU.subtract, ALU.mult, [vx_b, mv_b], [vx_b])
            tt(K, "pool", vxt[:, :], vxt[:, :], bc[:, boff["lng"]:boff["lng"] + D], ALU.mult, [vx_b, bc_b], [vx_b])
            tt(K, "dve", vnb[:, tg, :], vxt[:, :], bc[:, boff["lnb"]:boff["lnb"] + D], ALU.add, [vx_b, bc_b], [vnb_b])
        for c in range(8):
            pb = bank(K)
            for d in range(8):
                mm(K, K.ps[pb][:, :], win[:, d, c * 128:(c + 1) * 128], h[:, d, :], d == 0, d == 7, [win_b, h_b], K.psb[pb])
            act(K, ug[:, c, :], K.ps[pb][:, :], AF.Gelu_apprx_tanh, [K.psb[pb], cols_b], [ug_b], bias=col("binu", c))
        if nxt:
            xp.square()
            xp.make_h()
        for g in range(8):
            pb = bank(K)
            for tg in range(4):
                mm(K, K.ps[pb][:, tg * 128:(tg + 1) * 128], vnb[:, tg, g * 128:(g + 1) * 128], wct[:, g, :], True, True, [vnb_b, wct_b], K.psb[pb])
            tm, tm_b = tmp[g % 2]
            tt(K, "dve", tm[:, :], K.ps[pb][:, :], bs4[:, g, :], ALU.add, [K.psb[pb], bs4_b], [tm_b])
            tt(K, "pool", gated[:, g, :], tm[:, :], ug[:, g, :], ALU.mult, [tm_b, ug_b], [gated_b])
        xp.out_proj(t, wout, wout_b, gated, gated_b, lambda c: col("bout", c), cols_b)
    ph.close()


def phase_lin(K, kind):
    P = K.P
    gla = kind == "gla"
    ph = Phase(K, kind)
    NH, DV = (4, 256) if gla else (8, 128)
    DVC = DV // 128
    WIN = 3088 if gla else 4096
    wkey = "gla_w_in" if gla else "hg_w_in"
    okey = "gla_w_out" if gla else "hg_w_out"
    layer = 0 if gla else 3
    win, win_b = ph.sb([128, 8, WIN], BF16, "win")
    wout, wout_b = ph.sb([128, 8, D], BF16, "wout")
    pairs = [("pool", win[:, c, :], K.d[wkey][0, c * 128:(c + 1) * 128, :]) for c in range(8)]
    pairs += [("pool", wout[:, c, :], K.d[okey][0, c * 128:(c + 1) * 128, :]) for c in range(8)]
    wbufs = [win_b, wout_b]
    if gla:
        wg2, wg2_b = ph.sb([16, 512], BF16, "wg2")
        pairs.append(("pool", wg2[:, :], K.d["gla_w_g2"][0]))
        wbufs.append(wg2_b)
    P.dma(pairs, K.wld, writes=wbufs)
    v2 = lambda a: a.rearrange("(c p) -> c p", p=128)
    specs = [("g", v2(K.d["norm_mix"][layer]))]
    if gla:
        specs += [("bg2", v2(K.d["gla_b_g2"][0])), ("gn", v2(K.d["gla_norm"][0]))]
    else:
        specs += [("gn", v2(K.d["hg_norm"][0])), ("lbt", K.d["hg_lb_table"].rearrange("l (c p) -> (l c) p", p=128))]
    cols, cols_b, off = load_cols(K, ph, specs)
    col = lambda key, i: cols[:, off[key] + i:off[key] + i + 1]
    par, par_b = ph.sb([128, 16], F32, "par")
    if gla:
        ts(K, "dve", par[:, 0:4], cols[:, off["bg2"]:off["bg2"] + 4], -1.0, None, ALU.mult, None, [cols_b], [par_b])
        dsc = 1.0 / 16.0
        qscale = 128.0 ** -0.5
    else:
        lt = cols[:, off["lbt"]:off["lbt"] + 32].rearrange("p (l c) -> p l c", c=8)
        ex, ex_b = ph.sb([128, 4, 8], F32, "lbex")
        mx, mx_b = ph.sb([128, 8], F32, "lbmx")
        tt(K, "dve", mx[:, :], lt[:, 0, :], lt[:, 1, :], ALU.max, [cols_b], [mx_b])
        tt(K, "dve", mx[:, :], mx[:, :], lt[:, 2, :], ALU.max, [cols_b, mx_b], [mx_b])
        tt(K, "dve", mx[:, :], mx[:, :], lt[:, 3, :], ALU.max, [cols_b, mx_b], [mx_b])
        for l in range(4):
            tt(K, "dve", ex[:, l, :], lt[:, l, :], mx[:, :], ALU.subtract, [cols_b, mx_b], [ex_b])
        act(K, ex[:, :, :], ex[:, :, :], AF.Exp, [ex_b], [ex_b])
        tt(K, "dve", par[:, 8:16], ex[:, 1, :], ex[:, 2, :], ALU.add, [ex_b], [par_b])
        tt(K, "dve", par[:, 8:16], par[:, 8:16], ex[:, 3, :], ALU.add, [ex_b, par_b], [par_b])
        tt(K, "dve", mx[:, :], par[:, 8:16], ex[:, 0, :], ALU.add, [ex_b, par_b], [mx_b])
        K.P.op("dve", lambda e: e.reciprocal(out=mx[:, :], in_=mx[:, :]), reads=[mx_b], writes=[mx_b])
        tt(K, "dve", par[:, 0:8], par[:, 8:16], mx[:, :], ALU.mult, [par_b, mx_b], [par_b])
        dsc = 1.0
        qscale = 1.0
    xt, xt_b = ph.sb([128, 8, TT], F32, "xt")
    h, h_b = ph.sb([128, 8, TT], BF16, "h")
    rstd, rstd_b = ph.sb([128, TT], F32, "rstd")
    vt, vt_b = ph.sb([128, 4, D], BF16, "vt")
    gate, gate_b = ph.sb([128, 8, TT], BF16, "gate")
    sq, sq_b = gate, gate_b
    if not gla:
        qsall, qsall_b = ph.sb([128, NH, TT], BF16, "qsall")
    og, og_b = ph.sb([128, 8, TT], BF16, "og")
    f32t = lambda n: ph.sb([128, TT], F32, n)
    bft = lambda n: ph.sb([128, TT], BF16, n)
    (l1, l1_b), (l2, l2_b), (cl, cl_b) = f32t("l1"), f32t("l2"), f32t("cl")
    (dA, dA_b), (dD, dD_b) = (l1, l1_b), (l2, l2_b)
    (eA, eA_b), (eB, eB_b), (eC, eC_b), (eD, eD_b) = f32t("eA"), f32t("eB"), f32t("eC"), f32t("eD")
    (qs, qs_b), (kk, kk_b) = f32t("qs"), f32t("kk")
    qtS, ktS, qdS, kdS = [[bft(f"{n}{i}") for i in range(2)] for n in ("qt", "kt", "qd", "kd")]
    eCS = [(eC, eC_b), f32t("eC1")]
    stmS = [ph.sb([128, 4, 128], BF16, f"stm{i}") for i in range(2)]
    kdt, kdt_b = ph.sb([128, 4, 128], BF16, "kdt")
    Sall, Sall_b = ph.sb([128, 7, DV], F32, "Sall")
    Sb8S = [ph.sb([128, 8, DV], BF16, f"Sb8_{i}") for i in range(2)]
    sqo, sqo_b = ph.sb([128, DVC, TT], BF16, "sqo")
    rs, rs_b = f32t("rs")
    tmpo, tmpo_b = f32t("tmpo")
    if gla:
        glr, glr_b = ph.sb([16, TT], BF16, "glr")
    S32 = [ph.sb([128, DV], F32, f"S32_{i}") for i in range(NH)]
    NB = 2
    K.nb = NB
    PS_S, PS_T, PS_KV = 2, 3, [4, 5]
    POS = [[6, 7], [6, 7]] if gla else [[6], [7]]
    bd4 = K.cstf[:, 256:384].rearrange("p (o i) -> p o i", o=1).to_broadcast([128, 4, 128])
    msk = K.cstf[:, 384:896]
    bd = K.cstf[:, 256:384]
    c3 = lambda a: a[:, :].rearrange("p (c t) -> p c t", t=64)
    QOFF, KOFF = (0, 512) if gla else (0, 1024)
    VOFF = 1024 if gla else 2048
    GOFF = 2048 if gla else 3072

    def pre(hd):
        s_ = hd % 2
        (qt, qt_b), (kt, kt_b), (qd, qd_b), (kd, kd_b) = qtS[s_], ktS[s_], qdS[s_], kdS[s_]
        eC_, eC_b_ = eCS[s_]
        if gla:
            pq = bank(K, NB)
            for d in range(8):
                mm(K, K.ps[pq][:, :], win[:, d, QOFF + hd * 128:QOFF + (hd + 1) * 128], h[:, d, :], d == 0, d == 7, [win_b, h_b], K.psb[pq])
            cp(K, "act", qs[:, :], K.ps[pq][:, :], [K.psb[pq]], [qs_b])
            pk = bank(K, NB)
            for d in range(8):
                mm(K, K.ps[pk][:, :], win[:, d, KOFF + hd * 128:KOFF + (hd + 1) * 128], h[:, d, :], d == 0, d == 7, [win_b, h_b], K.psb[pk])
            cp(K, "act", kk[:, :], K.ps[pk][:, :], [K.psb[pk]], [kk_b])
            pg = bank(K, NB)
            mm(K, K.ps[pg][:, :], wg2[0:16, hd * 128:(hd + 1) * 128], glr[0:16, :], True, True, [wg2_b, glr_b], K.psb[pg])
            act(K, l1[:, :], K.ps[pg][:, :], AF.Exp, [K.psb[pg], par_b], [l1_b], scale=-1.0, bias=par[:, hd:hd + 1])
            act(K, l2[:, :], l1[:, :], AF.Ln, [l1_b], [l2_b], bias=1.0)
            yield
        else:
            pk = bank(K, NB)
            for d in range(8):
                mm(K, K.ps[pk][:, :], win[:, d, KOFF + hd * 128:KOFF + (hd + 1) * 128], h[:, d, :], d == 0, d == 7, [win_b, h_b], K.psb[pk])
            act(K, l1[:, :], K.ps[pk][:, :], AF.Exp, [K.psb[pk]], [l1_b], scale=-1.0)
            act(K, l2[:, :], l1[:, :], AF.Ln, [l1_b], [l2_b], bias=1.0)
            act(K, l1[:, :], l1[:, :], AF.Ln, [l1_b, par_b], [l1_b], scale=par[:, hd:hd + 1], bias=1.0)
            yield
            tt(K, "dve", l2[:, :], l2[:, :], l1[:, :], ALU.subtract, [l1_b, l2_b], [l2_b])
            act(K, kk[:, :], l2[:, :], AF.Exp, [l2_b], [kk_b], scale=-1.0)
            act(K, kk[:, :], kk[:, :], AF.Identity, [kk_b], [kk_b], scale=-1.0, bias=1.0)
        K.P.op("dve", lambda e: e.tensor_tensor_scan(out=cl[:, :], data0=msk, data1=l2[:, :], initial=0.0, op0=ALU.mult, op1=ALU.add),
               reads=[l2_b, K.cst_b], writes=[cl_b])
        tt(K, "dve", c3(dA), c3(cl), c3(cl)[:, :, 32:33].to_broadcast([128, 8, 64]), ALU.subtract, [cl_b], [dA_b])
        tt(K, "dve", c3(dD), c3(cl)[:, :, 63:64].to_broadcast([128, 8, 64]), c3(cl), ALU.subtract, [cl_b], [dD_b])
        yield
        act(K, eA[:, :], dA[:, :], AF.Exp, [dA_b], [eA_b], scale=-dsc)
        act(K, eB[:, :], dA[:, :], AF.Exp, [dA_b], [eB_b], scale=dsc)
        act(K, eC_[:, :], cl[:, :], AF.Exp, [cl_b], [eC_b_], scale=-dsc)
        act(K, eD[:, :], dD[:, :], AF.Exp, [dD_b], [eD_b], scale=-dsc)
        yield
        qsrc, qsrc_b = (qs[:, :], qs_b) if gla else (qsall[:, hd, :], qsall_b)
        stt(K, qt[:, :], qsrc, qscale, eA[:, :], ALU.mult, ALU.mult, [qsrc_b, eA_b], [qt_b])
        stt(K, qd[:, :], qsrc, qscale, eC_[:, :], ALU.mult, ALU.mult, [qsrc_b, eC_b_], [qd_b])
        tt(K, "dve", kt[:, :], kk[:, :], eB[:, :], ALU.mult, [kk_b, eB_b], [kt_b])
        tt(K, "dve", kd[:, :], kk[:, :], eD[:, :], ALU.mult, [kk_b, eD_b], [kd_b])
        yield

    def stageB(hd, nxt):
        s_ = hd % 2
        (qt, qt_b), (kt, kt_b), (kd, kd_b) = qtS[s_], ktS[s_], kdS[s_]
        eC_, eC_b_ = eCS[s_]
        stm, stm_b = stmS[s_]
        Sb8, Sb8_b = Sb8S[s_]
        S, S_b = S32[hd]
        step = lambda: next(nxt, None) if nxt is not None else None
        for tg in range(4):
            sl_ = slice(tg * 128, (tg + 1) * 128)
            mm(K, K.ps[PS_S][:, sl_], kt[:, sl_], qt[:, sl_], True, True, [kt_b, qt_b], K.psb[PS_S])
        tt(K, "dve", stm[:, :, :], K.ps[PS_S][:, :].rearrange("p (g i) -> p g i", i=128), bd4, ALU.mult, [K.psb[PS_S], K.cst_b], [stm_b])
        ptv = K.ps[PS_T][:, 0:256].bitcast(BF16)
        for tg in range(4):
            tr(K, ptv[:, tg * 128:(tg + 1) * 128], kd[:, tg * 128:(tg + 1) * 128], K.identb[:, :], [kd_b], K.psb[PS_T])
        cp(K, "act", kdt[:, :, :], ptv.rearrange("p (g d) -> p g d", d=128), [K.psb[PS_T]], [kdt_b])
        cp(K, "act", Sb8[:, 0, :], S[:, :], [S_b], [Sb8_b])
        CPB = 512 // DV
        NHALF = 8 // (2 * CPB)

        def kvloc(c):
            cc = c % (2 * CPB)
            return PS_KV[cc % 2], slice((cc // 2) * DV, (cc // 2 + 1) * DV)

        for hf in range(NHALF):
            cr = range(hf * 2 * CPB, (hf + 1) * 2 * CPB)
            for c in cr:
                tg, ci = c // 2, c % 2
                rows = slice(ci * 64, (ci + 1) * 64)
                kb, ks = kvloc(c)
                mm(K, K.ps[kb][:, ks], kdt[rows, tg, :], vt[rows, tg, hd * DV:(hd + 1) * DV], True, True, [kdt_b, vt_b], K.psb[kb])
            if hf == 0:
                step()
            for c in cr:
                kb, ks = kvloc(c)
                src, src_b = (S[:, :], S_b) if c == 0 else (Sall[:, c - 1, :], Sall_b)
                dst, dst_b = (S[:, :], S_b) if c == 7 else (Sall[:, c, :], Sall_b)
                stt(K, dst, src, eC_[:, c * 64 + 63:c * 64 + 64], K.ps[kb][:, ks], ALU.mult, ALU.add,
                    [src_b, eC_b_, K.psb[kb]], [dst_b])
        cp(K, "act", Sb8[:, 1:8, :], Sall[:, :, :], [Sall_b], [Sb8_b])
        step()
        step()

    def stageC(hd):
        s_ = hd % 2
        qd, qd_b = qdS[s_]
        stm, stm_b = stmS[s_]
        Sb8, Sb8_b = Sb8S[s_]
        PO = POS[s_]
        for tg in range(4):
            sl_ = slice(tg * 128, (tg + 1) * 128)
            for ec in range(DVC):
                mm(K, K.ps[PO[ec]][:, sl_], vt[:, tg, hd * DV + ec * 128:hd * DV + (ec + 1) * 128], stm[:, tg, :], True, False,
                   [vt_b, stm_b], K.psb[PO[ec]])
            for ci in range(2):
                c = tg * 2 + ci
                cs = slice(c * 64, (c + 1) * 64)
                for ec in range(DVC):
                    mm(K, K.ps[PO[ec]][:, cs], Sb8[:, c, ec * 128:(ec + 1) * 128], qd[:, cs], False, ci == 1, [Sb8_b, qd_b], K.psb[PO[ec]])
        for ec in range(DVC):
            act(K, sqo[:, ec, :], K.ps[PO[ec]][:, :], AF.Square, [K.psb[PO[ec]]], [sqo_b])
        bcast_stat(K, [sqo[:, ec, :] for ec in range(DVC)], [sqo_b], rs[:, :], rs_b, 1.0 / DV)
        for ec in range(DVC):
            stt(K, tmpo[:, :], K.ps[PO[ec]][:, :], col("gn", ec), rs[:, :], ALU.mult, ALU.mult, [K.psb[PO[ec]], cols_b, rs_b], [tmpo_b])
            tt(K, "dve", og[:, hd * DVC + ec, :], tmpo[:, :], gate[:, hd * DVC + ec, :], ALU.mult, [tmpo_b, gate_b], [og_b])

    xp = XPipe(K, ph, cols[:, off["g"]:off["g"] + 8], cols_b, xt, xt_b, h, h_b, sq, sq_b, rstd, rstd_b)
    xp.load(0)
    xp.square()
    xp.make_h()
    for t in range(K.NT):
        nxt = t + 1 < K.NT
        if nxt:
            xp.load(t + 1)
        if t % K.TPS == 0:
            for hd in range(NH):
                memset(K, "pool", S32[hd][0][:, :], 0.0, [S32[hd][1]])
        if gla:
            pb = bank(K, NB)
            for d in range(8):
                mm(K, K.ps[pb][0:16, :], win[:, d, 3072:3088], h[:, d, :], d == 0, d == 7, [win_b, h_b], K.psb[pb])
            cp(K, "act", glr[:, :], K.ps[pb][0:16, :], [K.psb[pb]], [glr_b])
        if not gla:
            for hd in range(NH):
                pb = bank(K, NB)
                for d in range(8):
                    mm(K, K.ps[pb][:, :], win[:, d, QOFF + hd * 128:QOFF + (hd + 1) * 128], h[:, d, :], d == 0, d == 7, [win_b, h_b], K.psb[pb])
                act(K, qsall[:, hd, :], K.ps[pb][:, :], AF.Silu, [K.psb[pb]], [qsall_b])
        for c in range(8):
            pb = bank(K, NB)
            for d in range(8):
                mm(K, K.ps[pb][:, :], win[:, d, GOFF + c * 128:GOFF + (c + 1) * 128], h[:, d, :], d == 0, d == 7, [win_b, h_b], K.psb[pb])
            act(K, gate[:, c, :], K.ps[pb][:, :], AF.Silu, [K.psb[pb]], [gate_b])
        g0 = pre(0)
        next(g0)
        for tg in range(4):
            for half in range(2):
                pb = bank(K, NB)
                for d in range(8):
                    mm(K, K.ps[pb][:, :], h[:, d, tg * 128:(tg + 1) * 128], win[:, d, VOFF + half * 512:VOFF + (half + 1) * 512], d == 0, d == 7,
                       [win_b, h_b], K.psb[pb])
                cp(K, "act" if half else "dve", vt[:, tg, half * 512:(half + 1) * 512], K.ps[pb][:, :], [K.psb[pb]], [vt_b])
            next(g0, None)
        for _ in g0:
            pass
        pend = pre(1)
        stageB(0, pend)
        for hd in range(NH):
            if hd + 1 < NH:
                for _ in pend:
                    pass
            stageC(hd)
            if hd + 1 < NH:
                pend = pre(hd + 2) if hd + 2 < NH else None
                stageB(hd + 1, pend)
        if nxt:
            xp.square()
            xp.make_h()
        xp.out_proj(t, wout, wout_b, og, og_b)
    K.nb = 6
    ph.close()


PHASES = {
    "tin": phase_tin,
    "tout": lambda K: phase_fin(K, False),
    "fin": lambda K: phase_fin(K, True),
    "gla": lambda K: phase_lin(K, "gla"),
    "hg": lambda K: phase_lin(K, "hg"),
    "conf": phase_conf,
    "sgu": phase_sgu,
    "ffn0": lambda K: phase_ffn(K, 0),
    "ffn1": lambda K: phase_ffn(K, 1),
    "ffn2": lambda K: phase_ffn(K, 2),
    "ffn3": lambda K: phase_ffn(K, 3),
}
ALL_PHASES = ["tin", "gla", "ffn0", "conf", "ffn1", "sgu", "ffn2", "hg", "ffn3", "fin"]


def make_consts():
    c = np.zeros((128, 1024), np.float32)
    j = np.arange(128)[:, None]
    i = np.arange(128)[None, :]
    c[:, 0:128] = (j == i)
    c[:, 128:256] = (j <= i)
    c[:, 256:384] = (j <= i) & ((j // 64) == (i // 64))
    t = np.arange(512)[None, :]
    c[:, 384:896] = (t % 64 != 0)
    c[:, 896:1024] = 1.0
    return c


def build(phases, NSEQ, SEQ):
    NTOK = NSEQ * SEQ
    nc = bass.Bass("TRN2", target_bir_lowering=False)
    K = Ctx()
    K.nc = nc
    K.d = LazyInputs(nc)
    K.x = nc.dram_tensor("x", [NTOK, D], F32, kind="ExternalInput").ap()
    cst = nc.dram_tensor("cst", [128, 1024], F32, kind="ExternalInput").ap()
    K.out = nc.dram_tensor("out", [NTOK, D], F32, kind="ExternalOutput").ap()
    K.xT = nc.dram_tensor("xT", [D, NTOK], F32, kind="Internal").ap()
    K.NT = NTOK // TT
    K.TPS = SEQ // TT
    K.rr = 0
    K.nb = 6
    with ExitStack() as st:
        P = Prog(nc, st)
        K.P = P
        K.ld = P.dma_sem("ld")
        K.stx = P.dma_sem("stx")
        K.wld = P.dma_sem("wld")
        K.pld = P.dma_sem("pld")
        K.rl = [P.dma_sem(f"rl{i}") for i in range(4)]
        K.rs = [P.dma_sem(f"rs{i}") for i in range(4)]
        K.cstf = st.enter_context(nc.sbuf_tensor("cstf", [128, 1024], F32))
        K.identb = st.enter_context(nc.sbuf_tensor("identb", [128, 128], BF16))
        K.onesb = st.enter_context(nc.sbuf_tensor("onesb", [128, 128], BF16))
        K.cst_b = Buf()
        K.ps = [st.enter_context(nc.psum_tensor(f"ps{i}", [128, 512], F32)) for i in range(8)]
        K.psb = [Buf() for _ in range(8)]
        P.dma([("sp", K.cstf[:, :], cst)], K.pld, writes=[K.cst_b])
        cp(K, "dve", K.identb[:, :], K.cstf[:, 0:128], [K.cst_b], [K.cst_b])
        cp(K, "dve", K.onesb[:, :], K.cstf[:, 896:1024], [K.cst_b], [K.cst_b])
        for p in phases:
            PHASES[p](K)
        for k, v in P.cnt.items():
            if v:
                P.ops["sp"].append(lambda e, sem=P.semh[k], v=v: e.wait_ge(sem, v))
        P.flush()
        K.ninstr = P.ninstr
    return nc, K


def run(phases, x, inputs, NSEQ, SEQ, ncores, trace=False):
    nc, K = build(phases, NSEQ, SEQ)
    cst = make_consts()
    in_maps = []
    for i in range(ncores):
        m = {name: np.ascontiguousarray(inputs[name], dtype=np.float32) for name in K.d.keys()}
        m["x"] = np.ascontiguousarray(x[i])
        m["cst"] = cst
        in_maps.append(m)
    res = run_bass_kernel_spmd(nc, in_maps, core_ids=list(range(ncores)), trace=trace)
    return np.stack([res.results[i]["out"] for i in range(ncores)]), res


def kernel(**inputs):
    x = np.asarray(inputs["x"], dtype=np.float32)
    B, S, _ = x.shape
    nseq = B // NCORES
    xs = x.reshape(NCORES, nseq * S, D)
    out, _ = run(ALL_PHASES, xs, inputs, nseq, S, NCORES)
    return out.reshape(B, S, D).astype(np.float32)
```

```python
import numpy as np
from contextlib import ExitStack
import concourse.bass as bass
import concourse.mybir as mybir
from concourse.bass_utils import run_bass_kernel_spmd

F32 = mybir.dt.float32
BF16 = mybir.dt.bfloat16
AF = mybir.ActivationFunctionType
ALU = mybir.AluOpType

LINV = 0
LINSTOP = 9
LINX = 0
D = 1024
TT = 512
EPS = 1e-6
FH = 2816
NCORES = 8

INPUT_SHAPES = [
    ("norm_mix", [4, 1024]), ("norm_ffn", [4, 1024]), ("norm_final", [1024]),
    ("gla_w_in", [1, 1024, 3088]), ("gla_w_g2", [1, 16, 512]), ("gla_b_g2", [1, 512]),
    ("gla_norm", [1, 256]), ("gla_w_out", [1, 1024, 1024]),
    ("cv_w_in", [1, 1024, 2048]), ("cv_b_in", [1, 2048]), ("cv_w_dw", [1, 31, 1024]),
    ("cv_b_dw", [1, 1024]), ("cv_ln_g", [1, 1024]), ("cv_ln_b", [1, 1024]),
    ("cv_w_out", [1, 1024, 1024]), ("cv_b_out", [1, 1024]),
    ("sg_w_in", [1, 1024, 2048]), ("sg_b_in", [1, 2048]), ("sg_ln_g", [1, 1024]),
    ("sg_ln_b", [1, 1024]), ("sg_w_s", [1, 8, 128, 128]), ("sg_b_s", [1, 8, 128]),
    ("sg_w_out", [1, 1024, 1024]), ("sg_b_out", [1, 1024]),
    ("hg_w_in", [1, 1024, 4096]), ("hg_lb_table", [4, 1024]), ("hg_norm", [1, 128]),
    ("hg_w_out", [1, 1024, 1024]),
    ("ffn_w_up", [4, 1024, 5632]), ("ffn_w_dw", [4, 3, 5632]), ("ffn_w_down", [4, 2816, 1024]),
]


class Buf:
    __slots__ = ("w", "r")

    def __init__(self):
        self.w = None
        self.r = {}


class Prog:
    ENG = ("pe", "dve", "act", "pool", "sp")

    def __init__(self, nc, stack):
        self.nc = nc
        self.stack = stack
        self.ops = {e: [] for e in self.ENG}
        self.semh = {}
        self.cnt = {}
        for e in ("pe", "dve", "act", "pool"):
            self.semh[e] = stack.enter_context(nc.semaphore("s_" + e))
            self.cnt[e] = 0
        self.seen = {e: {} for e in self.ENG}
        self.ninstr = 0

    def dma_sem(self, name):
        self.semh[name] = self.stack.enter_context(self.nc.semaphore(name))
        self.cnt[name] = 0
        return name

    def _waits(self, eng, reads, writes):
        deps = {}
        for b in reads:
            if b.w is not None:
                k, v = b.w
                if deps.get(k, 0) < v:
                    deps[k] = v
        for b in writes:
            if b.w is not None:
                k, v = b.w
                if deps.get(k, 0) < v:
                    deps[k] = v
            for k, v in b.r.items():
                if deps.get(k, 0) < v:
                    deps[k] = v
        seen = self.seen[eng]
        for k, v in deps.items():
            if k == eng:
                if eng == "pe":
                    continue
                if v <= self.cnt[eng] - 3:
                    continue
            if seen.get(k, 0) >= v:
                continue
            seen[k] = v
            sem = self.semh[k]
            self.ninstr += 1
            self.ops[eng].append(lambda e, sem=sem, v=v: e.wait_ge(sem, v))

    def _mark(self, ev, reads, writes):
        k, v = ev
        for b in reads:
            if b.r.get(k, 0) < v:
                b.r[k] = v
        for b in writes:
            b.w = ev
            b.r = {}

    def op(self, eng, fn, reads=(), writes=()):
        self._waits(eng, reads, writes)
        self.cnt[eng] += 1
        sem = self.semh[eng]
        self.ninstr += 1
        self.ops[eng].append(lambda e, fn=fn, sem=sem: fn(e).then_inc(sem, 1))
        self._mark((eng, self.cnt[eng]), reads, writes)

    def dma(self, pairs, sem, reads=(), writes=()):
        for q in sorted(set(p[0] for p in pairs)):
            self._waits(q, reads, writes)
            if self.cnt[sem] and self.seen[q].get(sem, 0) < self.cnt[sem]:
                self.seen[q][sem] = self.cnt[sem]
                self.ops[q].append(lambda e, s_=self.semh[sem], v=self.cnt[sem]: e.wait_ge(s_, v))
        h = self.semh[sem]
        for (q, o, i) in pairs:
            self.cnt[sem] += 16
            self.ninstr += 1
            self.ops[q].append(lambda e, o=o, i=i, h=h: e.dma_start(out=o, in_=i).then_inc(h, 16))
        self._mark((sem, self.cnt[sem]), reads, writes)

    def barrier(self):
        for eng in self.ENG:
            seen = self.seen[eng]
            for k, v in self.cnt.items():
                if v == 0 or seen.get(k, 0) >= v:
                    continue
                if k == eng and eng == "pe":
                    continue
                seen[k] = v
                sem = self.semh[k]
                self.ops[eng].append(lambda e, sem=sem, v=v: e.wait_ge(sem, v))

    def flush(self):
        ops = self.ops
        with self.nc.Block() as block:
            @block.tensor
            def _(e):
                for c in ops["pe"]:
                    c(e)

            @block.vector
            def _(e):
                for c in ops["dve"]:
                    c(e)

            @block.scalar
            def _(e):
                for c in ops["act"]:
                    c(e)

            @block.gpsimd
            def _(e):
                for c in ops["pool"]:
                    c(e)

            @block.sync
            def _(e):
                for c in ops["sp"]:
                    c(e)
        self.ops = {e: [] for e in self.ENG}


class Ctx:
    pass


class LazyInputs(dict):
    def __init__(self, nc):
        super().__init__()
        self.nc = nc
        self.shapes = dict(INPUT_SHAPES)

    def __missing__(self, name):
        ap = self.nc.dram_tensor(name, self.shapes[name], F32, kind="ExternalInput").ap()
        self[name] = ap
        return ap


def mm(K, out, lhsT, rhs, start, stop, reads, wb):
    K.P.op("pe", lambda e: e.matmul(out, lhsT=lhsT, rhs=rhs, start=start, stop=stop), reads=reads, writes=[wb])


def tr(K, out, in_, ident, reads, wb):
    K.P.op("pe", lambda e: e.transpose(out=out, in_=in_, identity=ident), reads=list(reads) + [K.cst_b], writes=[wb])


def act(K, out, in_, func, reads, writes, **kw):
    K.P.op("act", lambda e: e.activation(out=out, in_=in_, func=func, **kw), reads=reads, writes=writes)


def tt(K, eng, out, in0, in1, op, reads, writes):
    K.P.op(eng, lambda e: e.tensor_tensor(out=out, in0=in0, in1=in1, op=op), reads=reads, writes=writes)


def stt(K, out, in0, scalar, in1, op0, op1, reads, writes):
    K.P.op("dve", lambda e: e.scalar_tensor_tensor(out=out, in0=in0, scalar=scalar, in1=in1, op0=op0, op1=op1),
           reads=reads, writes=writes)


def ts(K, eng, out, in0, s1, s2, op0, op1, reads, writes):
    if s2 is None:
        K.P.op(eng, lambda e: e.tensor_scalar(out=out, in0=in0, scalar1=s1, scalar2=None, op0=op0), reads=reads, writes=writes)
    else:
        K.P.op(eng, lambda e: e.tensor_scalar(out=out, in0=in0, scalar1=s1, scalar2=s2, op0=op0, op1=op1), reads=reads, writes=writes)


def cp(K, eng, out, in_, reads, writes):
    if eng == "act":
        act(K, out, in_, AF.Copy, reads, writes)
    else:
        K.P.op(eng, lambda e: e.tensor_copy(out=out, in_=in_), reads=reads, writes=writes)


def memset(K, eng, ap, val, writes):
    K.P.op(eng, lambda e: e.memset(ap, val), writes=writes)


def bank(K, n=None):
    n = n or K.nb
    b = K.rr % n
    K.rr += 1
    return b


class Phase:
    def __init__(self, K, name):
        self.K = K
        self.name = name
        self.st = ExitStack()
        self.n = 0

    def sb(self, shape, dt, name=None):
        self.n += 1
        t = self.st.enter_context(self.K.nc.sbuf_tensor(f"{self.name}_{name or self.n}", list(shape), dt))
        return t, Buf()

    def close(self):
        K = self.K
        K.P.barrier()
        K.P.flush()
        self.st.close()


def load_cols(K, ph, specs):
    P = K.P
    tot = sum(ap.shape[0] for _, ap in specs)
    G = (tot + 127) // 128
    stg, stg_b = ph.sb([128, G, 128], F32, "colstg")
    cols, cols_b = ph.sb([128, G * 128], F32, "cols")
    off = {}
    pairs = []
    r = 0
    for key, ap in specs:
        off[key] = r
        R = ap.shape[0]
        s = 0
        while s < R:
            n = min(R - s, 128 - (r % 128))
            pairs.append(("sp", stg[r % 128:r % 128 + n, r // 128, :], ap[s:s + n, :]))
            s += n
            r += n
    P.dma(pairs, K.pld, writes=[stg_b])
    for g in range(G):
        rows = min(128, tot - g * 128)
        pb = bank(K)
        tr(K, K.ps[pb][:, 0:rows], stg[0:rows, g, :], K.cstf[0:rows, 0:rows], [stg_b], K.psb[pb])
        cp(K, "act", cols[:, g * 128:g * 128 + rows], K.ps[pb][:, 0:rows], [K.psb[pb]], [cols_b])
    return cols, cols_b, off


def load_bcast(K, ph, specs):
    P = K.P
    tot = sum(ap.shape[1] for _, ap in specs)
    row, row_b = ph.sb([1, tot], F32, "bcrow")
    bc, bc_b = ph.sb([128, tot], F32, "bc")
    off = {}
    pairs = []
    c = 0
    for key, ap in specs:
        off[key] = c
        pairs.append(("sp", row[0:1, c:c + ap.shape[1]], ap))
        c += ap.shape[1]
    P.dma(pairs, K.pld, writes=[row_b])
    for s in range(0, tot, 512):
        n = min(512, tot - s)
        pb = bank(K)
        mm(K, K.ps[pb][:, 0:n], K.cstf[0:1, 896:1024], row[0:1, s:s + n], True, True, [row_b, K.cst_b], K.psb[pb])
        cp(K, "act", bc[:, s:s + n], K.ps[pb][:, 0:n], [K.psb[pb]], [bc_b])
    return bc, bc_b, off


def load_x(K, t, xt, xt_b):
    K.P.dma([("sp", xt[:, :, :], K.xT[:, t * TT:(t + 1) * TT].rearrange("(c p) t -> p c t", p=128))],
            K.ld, writes=[xt_b])


def store_x(K, t, xt, xt_b):
    K.P.dma([("sp", K.xT[:, t * TT:(t + 1) * TT].rearrange("(c p) t -> p c t", p=128), xt[:, :, :])],
            K.stx, reads=[xt_b])


def bcast_stat(K, srcs, src_bufs, out, out_b, scale, func=None, bias=EPS):
    pb = bank(K)
    n = len(srcs)
    for i, s in enumerate(srcs):
        mm(K, K.ps[pb][:, 0:out.shape[-1]], K.onesb[:, :], s, i == 0, i == n - 1, list(src_bufs) + [K.cst_b], K.psb[pb])
    if func is None:
        act(K, out, K.ps[pb][:, 0:out.shape[-1]], AF.Ln, [K.psb[pb]], [out_b], scale=scale, bias=bias)
        act(K, out, out, AF.Exp, [out_b], [out_b], scale=-0.5)
    else:
        act(K, out, K.ps[pb][:, 0:out.shape[-1]], func, [K.psb[pb]], [out_b], scale=scale, bias=bias)
    return pb


def rmsnorm_h(K, xt, xt_b, g, g_b, sq, sq_b, rstd, rstd_b, h, h_b):
    act(K, sq[:, 0:8, :], xt[:, :, :], AF.Square, [xt_b], [sq_b])
    bcast_stat(K, [sq[:, c, :] for c in range(8)], [sq_b], rstd[:, :], rstd_b, 1.0 / D)
    for c in range(8):
        stt(K, h[:, c, :], xt[:, c, :], g[:, c:c + 1], rstd[:, :], ALU.mult, ALU.mult, [xt_b, g_b, rstd_b], [h_b])


class XPipe:
    def __init__(self, K, ph, g, g_b, xt, xt_b, h, h_b, sq, sq_b, rstd, rstd_b, nring=4):
        self.K, self.g, self.g_b = K, g, g_b
        self.xt, self.xt_b, self.h, self.h_b = xt, xt_b, h, h_b
        self.sq, self.sq_b, self.rstd, self.rstd_b = sq, sq_b, rstd, rstd_b
        self.ring = [ph.sb([128, TT], F32, f"xr{i}") for i in range(nring)]
        self.r = 0

    def load(self, t):
        load_x(self.K, t, self.xt, self.xt_b)

    def square(self):
        act(self.K, self.sq[:, 0:8, :], self.xt[:, :, :], AF.Square, [self.xt_b], [self.sq_b])

    def make_h(self):
        K = self.K
        bcast_stat(K, [self.sq[:, c, :] for c in range(8)], [self.sq_b], self.rstd[:, :], self.rstd_b, 1.0 / D)
        for c in range(8):
            stt(K, self.h[:, c, :], self.xt[:, c, :], self.g[:, c:c + 1], self.rstd[:, :], ALU.mult, ALU.mult,
                [self.xt_b, self.g_b, self.rstd_b], [self.h_b])

    def res_load(self, t, c):
        K = self.K
        i = self.r % len(self.ring)
        self.r += 1
        xr, xr_b = self.ring[i]
        K.P.dma([("sp", xr[:, :], K.xT[c * 128:(c + 1) * 128, t * TT:(t + 1) * TT])], K.rl[i], writes=[xr_b])
        return i

    def res_add_store(self, t, c, i, pb, bias=None, bias_b=None):
        K = self.K
        xr, xr_b = self.ring[i]
        if bias is None:
            tt(K, "dve", xr[:, :], xr[:, :], K.ps[pb][:, :], ALU.add, [xr_b, K.psb[pb]], [xr_b])
        else:
            stt(K, xr[:, :], K.ps[pb][:, :], bias, xr[:, :], ALU.add, ALU.add, [K.psb[pb], bias_b, xr_b], [xr_b])
        K.P.dma([("sp", K.xT[c * 128:(c + 1) * 128, t * TT:(t + 1) * TT], xr[:, :])], K.rs[i], reads=[xr_b])

    def out_proj(self, t, wout, wout_b, src, src_b, bias_fn=None, bias_b=None):
        K = self.K
        for half in range(2):
            slots = [self.res_load(t, c) for c in range(half * 4, half * 4 + 4)]
            for c in range(half * 4, half * 4 + 4):
                pb = bank(K)
                for d in range(8):
                    sb_d = src_b[d] if isinstance(src_b, list) else src_b
                    mm(K, K.ps[pb][:, :], wout[:, d, c * 128:(c + 1) * 128], src[:, d, :], d == 0, d == 7, [wout_b, sb_d], K.psb[pb])
                self.res_add_store(t, c, slots[c - half * 4], pb, None if bias_fn is None else bias_fn(c), bias_b)


def phase_tin(K):
    ph = Phase(K, "tin")
    xins = [ph.sb([128, 4, D], F32) for _ in range(2)]
    xts = [ph.sb([128, 8, TT], F32) for _ in range(2)]
    for t in range(K.NT):
        xin, xin_b = xins[t % 2]
        xt, xt_b = xts[t % 2]
        K.P.dma([("sp", xin[:, :, :], K.x[t * TT:(t + 1) * TT, :].rearrange("(g p) d -> p g d", p=128))], K.rl[t % 2], writes=[xin_b])
        for c in range(8):
            pb = bank(K)
            for g in range(4):
                tr(K, K.ps[pb][:, g * 128:(g + 1) * 128], xin[:, g, c * 128:(c + 1) * 128], K.cstf[:, 0:128], [xin_b], K.psb[pb])
            cp(K, "act" if c % 2 else "dve", xt[:, c, :], K.ps[pb][:, :], [K.psb[pb]], [xt_b])
        K.P.dma([("sp", K.xT[:, t * TT:(t + 1) * TT].rearrange("(c p) t -> p c t", p=128), xt[:, :, :])], K.rs[t % 2], reads=[xt_b])
    ph.close()


def phase_fin(K, norm):
    ph = Phase(K, "fin")
    xts = [ph.sb([128, 8, TT], F32) for _ in range(2)]
    yos = [ph.sb([128, 4, D], F32) for _ in range(2)]
    if norm:
        cols, cols_b, off = load_cols(K, ph, [("g", K.d["norm_final"].rearrange("(c p) -> c p", p=128))])
        sq, sq_b = ph.sb([128, 8, TT], BF16)
        rstd, rstd_b = ph.sb([128, TT], F32)
    for t in range(K.NT):
        xt, xt_b = xts[t % 2]
        yo, yo_b = yos[t % 2]
        K.P.dma([("sp", xt[:, :, :], K.xT[:, t * TT:(t + 1) * TT].rearrange("(c p) t -> p c t", p=128))], K.rl[t % 2], writes=[xt_b])
        if norm:
            act(K, sq[:, :, :], xt[:, :, :], AF.Square, [xt_b], [sq_b])
            bcast_stat(K, [sq[:, c, :] for c in range(8)], [sq_b], rstd[:, :], rstd_b, 1.0 / D)
            for c in range(8):
                stt(K, xt[:, c, :], xt[:, c, :], cols[:, off["g"] + c:off["g"] + c + 1], rstd[:, :], ALU.mult, ALU.mult,
                    [xt_b, cols_b, rstd_b], [xt_b])
        for g in range(4):
            for half in range(2):
                pb = bank(K)
                for c4 in range(4):
                    c = half * 4 + c4
                    tr(K, K.ps[pb][:, c4 * 128:(c4 + 1) * 128], xt[:, c, g * 128:(g + 1) * 128], K.cstf[:, 0:128], [xt_b], K.psb[pb])
                cp(K, "act" if half else "dve", yo[:, g, half * 512:(half + 1) * 512], K.ps[pb][:, :], [K.psb[pb]], [yo_b])
        K.P.dma([("sp", K.out[t * TT:(t + 1) * TT, :].rearrange("(g p) d -> p g d", p=128), yo[:, :, :])], K.rs[t % 2], reads=[yo_b])
    ph.close()


def phase_ffn(K, li):
    P = K.P
    ph = Phase(K, f"ffn{li}")
    K.nb = 4
    wup, wup_b = ph.sb([128, 8, 2 * FH], BF16, "wup")
    wdn, wdn_b = ph.sb([128, 22, D], BF16, "wdn")
    P.dma([("pool", wup[:, c, :], K.d["ffn_w_up"][li, c * 128:(c + 1) * 128, :]) for c in range(8)], K.wld, writes=[wup_b])
    P.dma([("pool", wdn[:, j, :], K.d["ffn_w_down"][li, j * 128:(j + 1) * 128, :]) for j in range(22)], K.wld, writes=[wdn_b])
    cols, cols_b, off = load_cols(K, ph, [
        ("g", K.d["norm_ffn"][li].rearrange("(c p) -> c p", p=128)),
        ("dw", K.d["ffn_w_dw"][li].rearrange("k (c p) -> (k c) p", p=128)),
    ])
    xt, xt_b = ph.sb([128, 8, TT], F32, "xt")
    h, h_b = ph.sb([128, 8, TT], BF16, "h")
    u, _ = ph.sb([128, 22, TT], BF16, "u")
    u_bs = [Buf() for _ in range(22)]
    rstd, rstd_b = ph.sb([128, TT], F32, "rstd")
    NZ = 2
    zs = [ph.sb([128, TT + 2], F32, f"zs{i}") for i in range(NZ)]
    acc = [ph.sb([128, TT], F32, f"acc{i}") for i in range(4)]
    ztail, ztail_b = ph.sb([128, 44, 2], F32, "ztail")
    NR = 4
    ring = [ph.sb([128, TT], F32, f"xr{i}") for i in range(NR)]
    g = cols[:, off["g"]:off["g"] + 8]
    dw0 = off["dw"]
    ACC = [4, 5, 6, 7]
    st = {"k": 0, "r": 0}

    def norm_a(t):
        load_x(K, t, xt, xt_b)
        act(K, h[:, :, :], xt[:, :, :], AF.Square, [xt_b], [h_b])

    def norm_b():
        bcast_stat(K, [h[:, c, :] for c in range(8)], [h_b], rstd[:, :], rstd_b, 1.0 / D)
        for c in range(8):
            stt(K, h[:, c, :], xt[:, c, :], g[:, c:c + 1], rstd[:, :], ALU.mult, ALU.mult, [xt_b, cols_b, rstd_b], [h_b])

    def res_load(t, c):
        i = st["r"] % NR
        st["r"] += 1
        xr, xr_b = ring[i]
        P.dma([("sp", xr[:, :], K.xT[c * 128:(c + 1) * 128, t * TT:(t + 1) * TT])], K.rl[i], writes=[xr_b])
        return i

    def res_add_store(t, c, i, pb):
        xr, xr_b = ring[i]
        tt(K, "dve", xr[:, :], xr[:, :], K.ps[pb][:, :], ALU.add, [xr_b, K.psb[pb]], [xr_b])
        P.dma([("sp", K.xT[c * 128:(c + 1) * 128, t * TT:(t + 1) * TT], xr[:, :])], K.rs[i], reads=[xr_b])

    norm_a(0)
    norm_b()
    for t in range(K.NT):
        if t % K.TPS == 0:
            memset(K, "pool", ztail[:, :, :], 0.0, [ztail_b])
        for j in range(22):
            a2 = []
            for which, fc in ((0, j), (1, 22 + j)):
                pb = bank(K)
                for d in range(8):
                    mm(K, K.ps[pb][:, :], wup[:, d, fc * 128:(fc + 1) * 128], h[:, d, :], d == 0, d == 7, [wup_b, h_b], K.psb[pb])
                z, z_b = zs[st["k"] % NZ]
                a, a_b = acc[(2 * (j % 2)) + which]
                st["k"] += 1
                cp(K, "pool", z[:, 0:2], ztail[:, fc, :], [ztail_b], [z_b])
                act(K, z[:, 2:TT + 2], K.ps[pb][:, :], AF.Copy, [K.psb[pb]], [z_b])
                cp(K, "pool", ztail[:, fc, :], z[:, TT:TT + 2], [z_b], [ztail_b])
                act(K, a[:, :], z[:, 0:TT], AF.Copy, [z_b, cols_b], [a_b], scale=cols[:, dw0 + fc:dw0 + fc + 1])
                stt(K, a[:, :], z[:, 1:TT + 1], cols[:, dw0 + 44 + fc:dw0 + 44 + fc + 1], a[:, :], ALU.mult, ALU.add, [z_b, cols_b, a_b], [a_b])
                stt(K, a[:, :], z[:, 2:TT + 2], cols[:, dw0 + 88 + fc:dw0 + 88 + fc + 1], a[:, :], ALU.mult, ALU.add, [z_b, cols_b, a_b], [a_b])
                a2.append((a, a_b))
            (ag, ag_b), (av, av_b) = a2
            act(K, ag[:, :], ag[:, :], AF.Silu, [ag_b], [ag_b])
            tt(K, "dve", u[:, j, :], ag[:, :], av[:, :], ALU.mult, [ag_b, av_b], [u_bs[j]])
        nxt = t + 1 < K.NT
        if nxt:
            norm_a(t + 1)
        slots = [res_load(t, c) for c in range(4)]
        for c in range(4):
            for j in range(20):
                mm(K, K.ps[ACC[c]][:, :], wdn[:, j, c * 128:(c + 1) * 128], u[:, j, :], j == 0, False, [wdn_b, u_bs[j]], K.psb[ACC[c]])
        for c in range(4):
            for j in (20, 21):
                mm(K, K.ps[ACC[c]][:, :], wdn[:, j, c * 128:(c + 1) * 128], u[:, j, :], False, j == 21, [wdn_b, u_bs[j]], K.psb[ACC[c]])
        for c in range(4):
            res_add_store(t, c, slots[c], ACC[c])
        if nxt:
            norm_b()
        slots = [res_load(t, c) for c in range(4, 8)]
        for c in range(4, 8):
            pb = bank(K)
            for j in range(22):
                mm(K, K.ps[pb][:, :], wdn[:, j, c * 128:(c + 1) * 128], u[:, j, :], j == 0, j == 21, [wdn_b, u_bs[j]], K.psb[pb])
            res_add_store(t, c, slots[c - 4], pb)
    K.nb = 6
    ph.close()


def phase_conf(K):
    P = K.P
    ph = Phase(K, "conf")
    win, win_b = ph.sb([128, 8, 2048], BF16, "win")
    wout, wout_b = ph.sb([128, 8, D], BF16, "wout")
    P.dma([("pool", win[:, c, :], K.d["cv_w_in"][0, c * 128:(c + 1) * 128, :]) for c in range(8)], K.wld, writes=[win_b])
    P.dma([("pool", wout[:, c, :], K.d["cv_w_out"][0, c * 128:(c + 1) * 128, :]) for c in range(8)], K.wld, writes=[wout_b])
    v2 = lambda a: a.rearrange("(c p) -> c p", p=128)
    cols, cols_b, off = load_cols(K, ph, [
        ("g", v2(K.d["norm_mix"][1])), ("bin", v2(K.d["cv_b_in"][0])),
        ("dw", K.d["cv_w_dw"][0].rearrange("k (c p) -> (k c) p", p=128)),
        ("bdw", v2(K.d["cv_b_dw"][0])), ("lng", v2(K.d["cv_ln_g"][0])), ("lnb", v2(K.d["cv_ln_b"][0])),
        ("bout", v2(K.d["cv_b_out"][0])),
    ])
    col = lambda key, i: cols[:, off[key] + i:off[key] + i + 1]
    xt, xt_b = ph.sb([128, 8, TT], F32, "xt")
    h, h_b = ph.sb([128, 8, TT], BF16, "h")
    csq, csq_b = ph.sb([128, 8, TT], BF16, "csq")
    sq, sq_b = csq, csq_b
    rstd, rstd_b = ph.sb([128, TT], F32, "rstd")
    glu = [ph.sb([128, TT + 30], BF16, f"glu{c}") for c in range(8)]
    cacc = [ph.sb([128, TT], F32, f"cacc{c}") for c in range(8)]
    sg = [ph.sb([128, TT], F32, f"sg{i}") for i in range(2)]
    cb, _ = ph.sb([128, 8, TT], BF16, "cb")
    cb_b = [Buf() for _ in range(8)]
    sl, sl_b = cb, cb_b
    mean, mean_b = ph.sb([128, TT], F32, "mean")
    msq, msq_b = ph.sb([128, TT], F32, "msq")
    lrs, lrs_b = ph.sb([128, TT], F32, "lrs")
    tmp = [ph.sb([128, TT], F32, f"tmp{i}") for i in range(2)]
    dg, dg_b = ph.sb([128, 31 * 8, 128], BF16, "dg")
    for i in range(31 * 8):
        ts(K, "dve", dg[:, i, :], K.cstf[:, 0:128], col("dw", i), None, ALU.mult, None, [K.cst_b, cols_b], [dg_b])
    xp = XPipe(K, ph, cols[:, off["g"]:off["g"] + 8], cols_b, xt, xt_b, h, h_b, sq, sq_b, rstd, rstd_b)
    xp.load(0)
    xp.square()
    xp.make_h()
    for t in range(K.NT):
        nxt = t + 1 < K.NT
        if nxt:
            xp.load(t + 1)
        for c in range(8):
            gl, gl_b = glu[c]
            if t % K.TPS == 0:
                memset(K, "pool", gl[:, 0:30], 0.0, [gl_b])
            else:
                cp(K, "pool", gl[:, 0:30], gl[:, TT:TT + 30], [gl_b], [gl_b])
            pa = bank(K)
            for d in range(8):
                mm(K, K.ps[pa][:, :], win[:, d, c * 128:(c + 1) * 128], h[:, d, :], d == 0, d == 7, [win_b, h_b], K.psb[pa])
            pg = bank(K)
            for d in range(8):
                mm(K, K.ps[pg][:, :], win[:, d, D + c * 128:D + (c + 1) * 128], h[:, d, :], d == 0, d == 7, [win_b, h_b], K.psb[pg])
            s, s_b = sg[c % 2]
            act(K, s[:, :], K.ps[pg][:, :], AF.Sigmoid, [K.psb[pg], cols_b], [s_b], bias=col("bin", 8 + c))
            stt(K, gl[:, 30:TT + 30], K.ps[pa][:, :], col("bin", c), s[:, :], ALU.add, ALU.mult, [K.psb[pa], cols_b, s_b], [gl_b])
        if nxt:
            xp.square()
            xp.make_h()
        for c in range(8):
            gl, gl_b = glu[c]
            pc = bank(K)
            for kk in range(31):
                mm(K, K.ps[pc][:, :], dg[:, kk * 8 + c, :], gl[:, kk:kk + TT], kk == 0, kk == 30, [dg_b, gl_b], K.psb[pc])
            ca, ca_b = cacc[c]
            act(K, ca[:, :], K.ps[pc][:, :], AF.Identity, [K.psb[pc], cols_b], [ca_b], bias=col("bdw", c))
            cp(K, "dve", cb[:, c, :], ca[:, :], [ca_b], [cb_b[c]])
            act(K, csq[:, c, :], ca[:, :], AF.Square, [ca_b], [csq_b])
        bcast_stat(K, [cb[:, c, :] for c in range(8)], cb_b, mean[:, :], mean_b, 1.0 / D, func=AF.Copy, bias=0.0)
        tt(K, "dve", msq[:, :], mean[:, :], mean[:, :], ALU.mult, [mean_b], [msq_b])
        pb = bank(K)
        for c in range(8):
            mm(K, K.ps[pb][:, :], K.onesb[:, :], csq[:, c, :], c == 0, c == 7, [csq_b, K.cst_b], K.psb[pb])
        stt(K, lrs[:, :], K.ps[pb][:, :], 1.0 / D, msq[:, :], ALU.mult, ALU.subtract, [K.psb[pb], msq_b], [lrs_b])
        act(K, lrs[:, :], lrs[:, :], AF.Ln, [lrs_b], [lrs_b], bias=EPS)
        act(K, lrs[:, :], lrs[:, :], AF.Exp, [lrs_b], [lrs_b], scale=-0.5)
        for c in range(8):
            ca, ca_b = cacc[c]
            tm, tm_b = tmp[c % 2]
            tt(K, "dve", tm[:, :], ca[:, :], mean[:, :], ALU.subtract, [ca_b, mean_b], [tm_b])
            tt(K, "dve", tm[:, :], tm[:, :], lrs[:, :], ALU.mult, [tm_b, lrs_b], [tm_b])
            act(K, sl[:, c, :], tm[:, :], AF.Silu, [tm_b, cols_b], [sl_b[c]], scale=col("lng", c), bias=col("lnb", c))
        xp.out_proj(t, wout, wout_b, sl, sl_b, lambda c: col("bout", c), cols_b)
    ph.close()


def phase_sgu(K):
    P = K.P
    ph = Phase(K, "sgu")
    win, win_b = ph.sb([128, 8, 2048], BF16, "win")
    wout, wout_b = ph.sb([128, 8, D], BF16, "wout")
    wsn, wsn_b = ph.sb([128, 8, 128], F32, "wsn")
    P.dma([("pool", win[:, c, :], K.d["sg_w_in"][0, c * 128:(c + 1) * 128, :]) for c in range(8)], K.wld, writes=[win_b])
    P.dma([("pool", wout[:, c, :], K.d["sg_w_out"][0, c * 128:(c + 1) * 128, :]) for c in range(8)], K.wld, writes=[wout_b])
    P.dma([("sp", wsn[:, g, :], K.d["sg_w_s"][0, g, :, :]) for g in range(8)], K.pld, writes=[wsn_b])
    v2 = lambda a: a.rearrange("(c p) -> c p", p=128)
    cols, cols_b, off = load_cols(K, ph, [
        ("g", v2(K.d["norm_mix"][2])), ("binu", v2(K.d["sg_b_in"][0, 0:D])), ("bout", v2(K.d["sg_b_out"][0])),
    ])
    col = lambda key, i: cols[:, off[key] + i:off[key] + i + 1]
    bc, bc_b, boff = load_bcast(K, ph, [
        ("binv", K.d["sg_b_in"][0:1, D:2 * D]), ("lng", K.d["sg_ln_g"][0:1, :]), ("lnb", K.d["sg_ln_b"][0:1, :]),
        ("bs", K.d["sg_b_s"][0:1, :, :].rearrange("o g i -> o (g i)")),
    ])
    wct, wct_b = ph.sb([128, 8, 128], BF16, "wct")
    for g in range(8):
        pb = bank(K)
        tr(K, K.ps[pb][:, 0:128], wsn[:, g, :], K.cstf[:, 0:128], [wsn_b], K.psb[pb])
        tt(K, "dve", wct[:, g, :], K.ps[pb][:, 0:128], K.cstf[:, 128:256], ALU.mult, [K.psb[pb], K.cst_b], [wct_b])
    bs4, bs4_b = ph.sb([128, 8, TT], F32, "bs4")
    for tg in range(4):
        cp(K, "pool", bs4[:, :, tg * 128:(tg + 1) * 128], bc[:, boff["bs"]:boff["bs"] + 1024].rearrange("p (g i) -> p g i", i=128), [bc_b], [bs4_b])
    xt, xt_b = ph.sb([128, 8, TT], F32, "xt")
    h, h_b = ph.sb([128, 8, TT], BF16, "h")
    sq, sq_b = ph.sb([128, 8, TT], BF16, "sq")
    rstd, rstd_b = ph.sb([128, TT], F32, "rstd")
    ug, ug_b = ph.sb([128, 8, TT], F32, "ug")
    vx = [ph.sb([128, D], F32, f"vx{i}") for i in range(2)]
    vnb, vnb_b = ph.sb([128, 4, D], BF16, "vnb")
    stats, stats_b = ph.sb([128, 2, 6], F32, "stats")
    mv, mv_b = ph.sb([128, 2], F32, "mv")
    tmp = [ph.sb([128, TT], F32, f"tmp{i}") for i in range(2)]
    gated, _ = ph.sb([128, 8, TT], BF16, "gated")
    gated_b = [Buf() for _ in range(8)]
    xp = XPipe(K, ph, cols[:, off["g"]:off["g"] + 8], cols_b, xt, xt_b, h, h_b, sq, sq_b, rstd, rstd_b)
    xp.load(0)
    xp.square()
    xp.make_h()
    for t in range(K.NT):
        nxt = t + 1 < K.NT
        if nxt:
            xp.load(t + 1)
        for tg in range(4):
            vxt, vx_b = vx[tg % 2]
            for half in range(2):
                pb = bank(K)
                for d in range(8):
                    mm(K, K.ps[pb][:, :], h[:, d, tg * 128:(tg + 1) * 128], win[:, d, D + half * 512:D + (half + 1) * 512], d == 0, d == 7,
                       [win_b, h_b], K.psb[pb])
                tt(K, "dve", vxt[:, half * 512:(half + 1) * 512], K.ps[pb][:, :], bc[:, boff["binv"] + half * 512:boff["binv"] + (half + 1) * 512],
                   ALU.add, [K.psb[pb], bc_b], [vx_b])
            act(K, vxt[:, :], vxt[:, :], AF.Gelu_apprx_tanh, [vx_b], [vx_b])
            for half in range(2):
                K.P.op("dve", lambda e, half=half, vxt=vxt: e.bn_stats(out=stats[:, half, :], in_=vxt[:, half * 512:(half + 1) * 512]),
                       reads=[vx_b], writes=[stats_b])
            K.P.op("dve", lambda e: e.bn_aggr(out=mv[:, :], in_=stats[:, :, :].rearrange("p a b -> p (a b)")), reads=[stats_b], writes=[mv_b])
            act(K, mv[:, 1:2], mv[:, 1:2], AF.Ln, [mv_b], [mv_b], bias=EPS)
            act(K, mv[:, 1:2], mv[:, 1:2], AF.Exp, [mv_b], [mv_b], scale=-0.5)
            ts(K, "dve", vxt[:, :], vxt[:, :], mv[:, 0:1], mv[:, 1:2], ALU.subtract, ALU.mult, [vx_b, mv_b], [vx_b])
            tt(K, "dve", vxt[:, :], vxt[:, :], bc[:, boff["lng"]:boff["lng"] + D], ALU.mult, [vx_b, bc_b], [vx_b])
            tt(K, "dve", vnb[:, tg, :], vxt[:, :], bc[:, boff["lnb"]:boff["lnb"] + D], ALU.add, [vx_b, bc_b], [vnb_b])
        for c in range(8):
            pb = bank(K)
            for d in range(8):
                mm(K, K.ps[pb][:, :], win[:, d, c * 128:(c + 1) * 128], h[:, d, :], d == 0, d == 7, [win_b, h_b], K.psb[pb])
            act(K, ug[:, c, :], K.ps[pb][:, :], AF.Gelu_apprx_tanh, [K.psb[pb], cols_b], [ug_b], bias=col("binu", c))
        if nxt:
            xp.square()
            xp.make_h()
        for g in range(8):
            pb = bank(K)
            for tg in range(4):
                mm(K, K.ps[pb][:, tg * 128:(tg + 1) * 128], vnb[:, tg, g * 128:(g + 1) * 128], wct[:, g, :], True, True, [vnb_b, wct_b], K.psb[pb])
            tm, tm_b = tmp[g % 2]
            tt(K, "dve", tm[:, :], K.ps[pb][:, :], bs4[:, g, :], ALU.add, [K.psb[pb], bs4_b], [tm_b])
            tt(K, "dve", gated[:, g, :], tm[:, :], ug[:, g, :], ALU.mult, [tm_b, ug_b], [gated_b[g]])
        xp.out_proj(t, wout, wout_b, gated, gated_b, lambda c: col("bout", c), cols_b)
    ph.close()


def phase_lin(K, kind):
    P = K.P
    gla = kind == "gla"
    ph = Phase(K, kind)
    NH, DV = (4, 256) if gla else (8, 128)
    DVC = DV // 128
    WIN = 3088 if gla else 4096
    wkey = "gla_w_in" if gla else "hg_w_in"
    okey = "gla_w_out" if gla else "hg_w_out"
    layer = 0 if gla else 3
    win, win_b = ph.sb([128, 8, WIN], BF16, "win")
    wout, wout_b = ph.sb([128, 8, D], BF16, "wout")
    pairs = [("pool", win[:, c, :], K.d[wkey][0, c * 128:(c + 1) * 128, :]) for c in range(8)]
    pairs += [("pool", wout[:, c, :], K.d[okey][0, c * 128:(c + 1) * 128, :]) for c in range(8)]
    wbufs = [win_b, wout_b]
    if gla:
        wg2, wg2_b = ph.sb([16, 512], BF16, "wg2")
        pairs.append(("pool", wg2[:, :], K.d["gla_w_g2"][0]))
        wbufs.append(wg2_b)
    P.dma(pairs, K.wld, writes=wbufs)
    v2 = lambda a: a.rearrange("(c p) -> c p", p=128)
    specs = [("g", v2(K.d["norm_mix"][layer]))]
    if gla:
        specs += [("bg2", v2(K.d["gla_b_g2"][0])), ("gn", v2(K.d["gla_norm"][0]))]
    else:
        specs += [("gn", v2(K.d["hg_norm"][0])), ("lbt", K.d["hg_lb_table"].rearrange("l (c p) -> (l c) p", p=128))]
    cols, cols_b, off = load_cols(K, ph, specs)
    col = lambda key, i: cols[:, off[key] + i:off[key] + i + 1]
    par, par_b = ph.sb([128, 16], F32, "par")
    if gla:
        ts(K, "dve", par[:, 0:4], cols[:, off["bg2"]:off["bg2"] + 4], -1.0, None, ALU.mult, None, [cols_b], [par_b])
        dsc = 1.0 / 16.0
        qscale = 128.0 ** -0.5
    else:
        lt = cols[:, off["lbt"]:off["lbt"] + 32].rearrange("p (l c) -> p l c", c=8)
        ex, ex_b = ph.sb([128, 4, 8], F32, "lbex")
        mx, mx_b = ph.sb([128, 8], F32, "lbmx")
        tt(K, "dve", mx[:, :], lt[:, 0, :], lt[:, 1, :], ALU.max, [cols_b], [mx_b])
        tt(K, "dve", mx[:, :], mx[:, :], lt[:, 2, :], ALU.max, [cols_b, mx_b], [mx_b])
        tt(K, "dve", mx[:, :], mx[:, :], lt[:, 3, :], ALU.max, [cols_b, mx_b], [mx_b])
        for l in range(4):
            tt(K, "dve", ex[:, l, :], lt[:, l, :], mx[:, :], ALU.subtract, [cols_b, mx_b], [ex_b])
        act(K, ex[:, :, :], ex[:, :, :], AF.Exp, [ex_b], [ex_b])
        tt(K, "dve", par[:, 8:16], ex[:, 1, :], ex[:, 2, :], ALU.add, [ex_b], [par_b])
        tt(K, "dve", par[:, 8:16], par[:, 8:16], ex[:, 3, :], ALU.add, [ex_b, par_b], [par_b])
        tt(K, "dve", mx[:, :], par[:, 8:16], ex[:, 0, :], ALU.add, [ex_b, par_b], [mx_b])
        K.P.op("dve", lambda e: e.reciprocal(out=mx[:, :], in_=mx[:, :]), reads=[mx_b], writes=[mx_b])
        tt(K, "dve", par[:, 0:8], par[:, 8:16], mx[:, :], ALU.mult, [par_b, mx_b], [par_b])
        dsc = 1.0
        qscale = 1.0
    xt, xt_b = ph.sb([128, 8, TT], F32, "xt")
    h, h_b = ph.sb([128, 8, TT], BF16, "h")
    rstd, rstd_b = ph.sb([128, TT], F32, "rstd")
    vt, vt_b = ph.sb([128, 4, D], BF16, "vt")
    gate, gate_b = ph.sb([128, 8, TT], BF16, "gate")
    sq, sq_b = gate, gate_b
    if not gla:
        qsall, qsall_b = ph.sb([128, NH, TT], BF16, "qsall")
    og, og_b = ph.sb([128, 8, TT], BF16, "og")
    f32t = lambda n: ph.sb([128, TT], F32, n)
    bft = lambda n: ph.sb([128, TT], BF16, n)
    (l1, l1_b), (l2, l2_b), (cl, cl_b) = f32t("l1"), f32t("l2"), f32t("cl")
    (dA, dA_b), (dD, dD_b) = (l1, l1_b), (l2, l2_b)
    (eA, eA_b), (eB, eB_b), (eC, eC_b), (eD, eD_b) = f32t("eA"), f32t("eB"), f32t("eC"), f32t("eD")
    (qs, qs_b), (kk, kk_b) = f32t("qs"), f32t("kk")
    qtS, ktS, qdS, kdS = [[bft(f"{n}{i}") for i in range(2)] for n in ("qt", "kt", "qd", "kd")]
    eCS = [(eC, eC_b), f32t("eC1")]
    stmS = [ph.sb([128, 4, 128], BF16, f"stm{i}") for i in range(2)]
    kdt, kdt_b = ph.sb([128, 4, 128], BF16, "kdt")
    Sall, Sall_b = ph.sb([128, 7, DV], F32, "Sall")
    Sb8S = [ph.sb([128, 8, DV], BF16, f"Sb8_{i}") for i in range(2)]
    sqo, sqo_b = ph.sb([128, DVC, TT], BF16, "sqo")
    rs, rs_b = f32t("rs")
    tmpo, tmpo_b = f32t("tmpo")
    if gla:
        glr, glr_b = ph.sb([16, TT], BF16, "glr")
    S32 = [ph.sb([128, DV], F32, f"S32_{i}") for i in range(NH)]
    NB = 2
    K.nb = NB
    PS_S, PS_T, PS_KV = 2, 3, [4, 5]
    POS = [[6, 7], [6, 7]] if gla else [[6], [7]]
    bd4 = K.cstf[:, 256:384].rearrange("p (o i) -> p o i", o=1).to_broadcast([128, 4, 128])
    msk = K.cstf[:, 384:896]
    bd = K.cstf[:, 256:384]
    c3 = lambda a: a[:, :].rearrange("p (c t) -> p c t", t=64)
    QOFF, KOFF = (0, 512) if gla else (0, 1024)
    VOFF = 1024 if gla else 2048
    GOFF = 2048 if gla else 3072

    def pre(hd):
        s_ = hd % 2
        (qt, qt_b), (kt, kt_b), (qd, qd_b), (kd, kd_b) = qtS[s_], ktS[s_], qdS[s_], kdS[s_]
        eC_, eC_b_ = eCS[s_]
        if gla:
            pq = bank(K, NB)
            for d in range(8):
                mm(K, K.ps[pq][:, :], win[:, d, QOFF + hd * 128:QOFF + (hd + 1) * 128], h[:, d, :], d == 0, d == 7, [win_b, h_b], K.psb[pq])
            cp(K, "act", qs[:, :], K.ps[pq][:, :], [K.psb[pq]], [qs_b])
            pk = bank(K, NB)
            for d in range(8):
                mm(K, K.ps[pk][:, :], win[:, d, KOFF + hd * 128:KOFF + (hd + 1) * 128], h[:, d, :], d == 0, d == 7, [win_b, h_b], K.psb[pk])
            cp(K, "act", kk[:, :], K.ps[pk][:, :], [K.psb[pk]], [kk_b])
            pg = bank(K, NB)
            mm(K, K.ps[pg][:, :], wg2[0:16, hd * 128:(hd + 1) * 128], glr[0:16, :], True, True, [wg2_b, glr_b], K.psb[pg])
            act(K, l1[:, :], K.ps[pg][:, :], AF.Exp, [K.psb[pg], par_b], [l1_b], scale=-1.0, bias=par[:, hd:hd + 1])
            act(K, l2[:, :], l1[:, :], AF.Ln, [l1_b], [l2_b], bias=1.0)
            yield
        else:
            pk = bank(K, NB)
            for d in range(8):
                mm(K, K.ps[pk][:, :], win[:, d, KOFF + hd * 128:KOFF + (hd + 1) * 128], h[:, d, :], d == 0, d == 7, [win_b, h_b], K.psb[pk])
            act(K, l1[:, :], K.ps[pk][:, :], AF.Exp, [K.psb[pk]], [l1_b], scale=-1.0)
            act(K, l2[:, :], l1[:, :], AF.Ln, [l1_b], [l2_b], bias=1.0)
            act(K, l1[:, :], l1[:, :], AF.Ln, [l1_b, par_b], [l1_b], scale=par[:, hd:hd + 1], bias=1.0)
            yield
            tt(K, "dve", l2[:, :], l2[:, :], l1[:, :], ALU.subtract, [l1_b, l2_b], [l2_b])
            act(K, kk[:, :], l2[:, :], AF.Exp, [l2_b], [kk_b], scale=-1.0)
            act(K, kk[:, :], kk[:, :], AF.Identity, [kk_b], [kk_b], scale=-1.0, bias=1.0)
        K.P.op("dve", lambda e: e.tensor_tensor_scan(out=cl[:, :], data0=msk, data1=l2[:, :], initial=0.0, op0=ALU.mult, op1=ALU.add),
               reads=[l2_b, K.cst_b], writes=[cl_b])
        tt(K, "dve", c3(dA), c3(cl), c3(cl)[:, :, 32:33].to_broadcast([128, 8, 64]), ALU.subtract, [cl_b], [dA_b])
        tt(K, "dve", c3(dD), c3(cl)[:, :, 63:64].to_broadcast([128, 8, 64]), c3(cl), ALU.subtract, [cl_b], [dD_b])
        yield
        act(K, eA[:, :], dA[:, :], AF.Exp, [dA_b], [eA_b], scale=-dsc)
        act(K, eB[:, :], dA[:, :], AF.Exp, [dA_b], [eB_b], scale=dsc)
        act(K, eC_[:, :], cl[:, :], AF.Exp, [cl_b], [eC_b_], scale=-dsc)
        act(K, eD[:, :], dD[:, :], AF.Exp, [dD_b], [eD_b], scale=-dsc)
        yield
        qsrc, qsrc_b = (qs[:, :], qs_b) if gla else (qsall[:, hd, :], qsall_b)
        stt(K, qt[:, :], qsrc, qscale, eA[:, :], ALU.mult, ALU.mult, [qsrc_b, eA_b], [qt_b])
        stt(K, qd[:, :], qsrc, qscale, eC_[:, :], ALU.mult, ALU.mult, [qsrc_b, eC_b_], [qd_b])
        tt(K, "dve", kt[:, :], kk[:, :], eB[:, :], ALU.mult, [kk_b, eB_b], [kt_b])
        tt(K, "dve", kd[:, :], kk[:, :], eD[:, :], ALU.mult, [kk_b, eD_b], [kd_b])
        yield

    def stageB(hd, nxt):
        s_ = hd % 2
        (qt, qt_b), (kt, kt_b), (kd, kd_b) = qtS[s_], ktS[s_], kdS[s_]
        eC_, eC_b_ = eCS[s_]
        stm, stm_b = stmS[s_]
        Sb8, Sb8_b = Sb8S[s_]
        S, S_b = S32[hd]
        step = lambda: next(nxt, None) if nxt is not None else None
        for tg in range(4):
            sl_ = slice(tg * 128, (tg + 1) * 128)
            mm(K, K.ps[PS_S][:, sl_], kt[:, sl_], qt[:, sl_], True, True, [kt_b, qt_b], K.psb[PS_S])
        tt(K, "dve", stm[:, :, :], K.ps[PS_S][:, :].rearrange("p (g i) -> p g i", i=128), bd4, ALU.mult, [K.psb[PS_S], K.cst_b], [stm_b])
        ptv = K.ps[PS_T][:, 0:256].bitcast(BF16)
        for tg in range(4):
            tr(K, ptv[:, tg * 128:(tg + 1) * 128], kd[:, tg * 128:(tg + 1) * 128], K.identb[:, :], [kd_b], K.psb[PS_T])
        cp(K, "act", kdt[:, :, :], ptv.rearrange("p (g d) -> p g d", d=128), [K.psb[PS_T]], [kdt_b])
        cp(K, "act", Sb8[:, 0, :], S[:, :], [S_b], [Sb8_b])
        CPB = 512 // DV
        NHALF = 8 // (2 * CPB)

        def kvloc(c):
            cc = c % (2 * CPB)
            return PS_KV[cc % 2], slice((cc // 2) * DV, (cc // 2 + 1) * DV)

        for hf in range(NHALF):
            cr = range(hf * 2 * CPB, (hf + 1) * 2 * CPB)
            for c in cr:
                tg, ci = c // 2, c % 2
                rows = slice(ci * 64, (ci + 1) * 64)
                kb, ks = kvloc(c)
                mm(K, K.ps[kb][:, ks], kdt[rows, tg, :], vt[rows, tg, hd * DV:(hd + 1) * DV], True, True, [kdt_b, vt_b], K.psb[kb])
            if hf == 0:
                step()
            for c in cr:
                kb, ks = kvloc(c)
                src, src_b = (S[:, :], S_b) if c == 0 else (Sall[:, c - 1, :], Sall_b)
                dst, dst_b = (S[:, :], S_b) if c == 7 else (Sall[:, c, :], Sall_b)
                stt(K, dst, src, eC_[:, c * 64 + 63:c * 64 + 64], K.ps[kb][:, ks], ALU.mult, ALU.add,
                    [src_b, eC_b_, K.psb[kb]], [dst_b])
        cp(K, "act", Sb8[:, 1:8, :], Sall[:, :, :], [Sall_b], [Sb8_b])
        step()
        step()

    def stageC(hd):
        s_ = hd % 2
        qd, qd_b = qdS[s_]
        stm, stm_b = stmS[s_]
        Sb8, Sb8_b = Sb8S[s_]
        PO = POS[s_]
        for tg in range(4):
            sl_ = slice(tg * 128, (tg + 1) * 128)
            for ec in range(DVC):
                mm(K, K.ps[PO[ec]][:, sl_], vt[:, tg, hd * DV + ec * 128:hd * DV + (ec + 1) * 128], stm[:, tg, :], True, False,
                   [vt_b, stm_b], K.psb[PO[ec]])
            for ci in range(2):
                c = tg * 2 + ci
                cs = slice(c * 64, (c + 1) * 64)
                for ec in range(DVC):
                    mm(K, K.ps[PO[ec]][:, cs], Sb8[:, c, ec * 128:(ec + 1) * 128], qd[:, cs], False, ci == 1, [Sb8_b, qd_b], K.psb[PO[ec]])
        for ec in range(DVC):
            act(K, sqo[:, ec, :], K.ps[PO[ec]][:, :], AF.Square, [K.psb[PO[ec]]], [sqo_b])
        bcast_stat(K, [sqo[:, ec, :] for ec in range(DVC)], [sqo_b], rs[:, :], rs_b, 1.0 / DV)
        for ec in range(DVC):
            stt(K, tmpo[:, :], K.ps[PO[ec]][:, :], col("gn", ec), rs[:, :], ALU.mult, ALU.mult, [K.psb[PO[ec]], cols_b, rs_b], [tmpo_b])
            tt(K, "dve", og[:, hd * DVC + ec, :], tmpo[:, :], gate[:, hd * DVC + ec, :], ALU.mult, [tmpo_b, gate_b], [og_b])

    xp = XPipe(K, ph, cols[:, off["g"]:off["g"] + 8], cols_b, xt, xt_b, h, h_b, sq, sq_b, rstd, rstd_b)
    xp.load(0)
    xp.square()
    xp.make_h()
    for t in range(K.NT):
        nxt = t + 1 < K.NT
        if nxt:
            xp.load(t + 1)
        if t % K.TPS == 0:
            for hd in range(NH):
                memset(K, "pool", S32[hd][0][:, :], 0.0, [S32[hd][1]])
        if gla:
            pb = bank(K, NB)
            for d in range(8):
                mm(K, K.ps[pb][0:16, :], win[:, d, 3072:3088], h[:, d, :], d == 0, d == 7, [win_b, h_b], K.psb[pb])
            cp(K, "act", glr[:, :], K.ps[pb][0:16, :], [K.psb[pb]], [glr_b])
        if not gla:
            for hd in range(NH):
                pb = bank(K, NB)
                for d in range(8):
                    mm(K, K.ps[pb][:, :], win[:, d, QOFF + hd * 128:QOFF + (hd + 1) * 128], h[:, d, :], d == 0, d == 7, [win_b, h_b], K.psb[pb])
                act(K, qsall[:, hd, :], K.ps[pb][:, :], AF.Silu, [K.psb[pb]], [qsall_b])
        for c in range(8):
            pb = bank(K, NB)
            for d in range(8):
                mm(K, K.ps[pb][:, :], win[:, d, GOFF + c * 128:GOFF + (c + 1) * 128], h[:, d, :], d == 0, d == 7, [win_b, h_b], K.psb[pb])
            act(K, gate[:, c, :], K.ps[pb][:, :], AF.Silu, [K.psb[pb]], [gate_b])
        g0 = pre(0)
        next(g0)
        for tg in range(4):
            for half in range(2):
                pb = bank(K, NB)
                for d in range(8):
                    mm(K, K.ps[pb][:, :], h[:, d, tg * 128:(tg + 1) * 128], win[:, d, VOFF + half * 512:VOFF + (half + 1) * 512], d == 0, d == 7,
                       [win_b, h_b], K.psb[pb])
                cp(K, "act" if half else "dve", vt[:, tg, half * 512:(half + 1) * 512], K.ps[pb][:, :], [K.psb[pb]], [vt_b])
            next(g0, None)
        for _ in g0:
            pass
        pend = pre(1)
        stageB(0, pend)
        for hd in range(NH):
            if hd + 1 < NH:
                for _ in pend:
                    pass
            stageC(hd)
            if hd + 1 < NH:
                pend = pre(hd + 2) if hd + 2 < NH else None
                stageB(hd + 1, pend)
        if nxt:
            xp.square()
            xp.make_h()
        xp.out_proj(t, wout, wout_b, og, og_b)
    K.nb = 6
    ph.close()


PHASES = {
    "tin": phase_tin,
    "tout": lambda K: phase_fin(K, False),
    "fin": lambda K: phase_fin(K, True),
    "gla": lambda K: phase_lin(K, "gla"),
    "hg": lambda K: phase_lin(K, "hg"),
    "conf": phase_conf,
    "sgu": phase_sgu,
    "ffn0": lambda K: phase_ffn(K, 0),
    "ffn1": lambda K: phase_ffn(K, 1),
    "ffn2": lambda K: phase_ffn(K, 2),
    "ffn3": lambda K: phase_ffn(K, 3),
}
ALL_PHASES = ["tin", "gla", "ffn0", "conf", "ffn1", "sgu", "ffn2", "hg", "ffn3", "fin"]


def make_consts():
    c = np.zeros((128, 1024), np.float32)
    j = np.arange(128)[:, None]
    i = np.arange(128)[None, :]
    c[:, 0:128] = (j == i)
    c[:, 128:256] = (j <= i)
    c[:, 256:384] = (j <= i) & ((j // 64) == (i // 64))
    t = np.arange(512)[None, :]
    c[:, 384:896] = (t % 64 != 0)
    c[:, 896:1024] = 1.0
    return c


def build(phases, NSEQ, SEQ):
    NTOK = NSEQ * SEQ
    nc = bass.Bass("TRN2", target_bir_lowering=False)
    K = Ctx()
    K.nc = nc
    K.d = LazyInputs(nc)
    K.x = nc.dram_tensor("x", [NTOK, D], F32, kind="ExternalInput").ap()
    cst = nc.dram_tensor("cst", [128, 1024], F32, kind="ExternalInput").ap()
    K.out = nc.dram_tensor("out", [NTOK, D], F32, kind="ExternalOutput").ap()
    K.xT = nc.dram_tensor("xT", [D, NTOK], F32, kind="Internal").ap()
    K.NT = NTOK // TT
    K.TPS = SEQ // TT
    K.rr = 0
    K.nb = 6
    with ExitStack() as st:
        P = Prog(nc, st)
        K.P = P
        K.ld = P.dma_sem("ld")
        K.stx = P.dma_sem("stx")
        K.wld = P.dma_sem("wld")
        K.pld = P.dma_sem("pld")
        K.rl = [P.dma_sem(f"rl{i}") for i in range(4)]
        K.rs = [P.dma_sem(f"rs{i}") for i in range(4)]
        K.cstf = st.enter_context(nc.sbuf_tensor("cstf", [128, 1024], F32))
        K.identb = st.enter_context(nc.sbuf_tensor("identb", [128, 128], BF16))
        K.onesb = st.enter_context(nc.sbuf_tensor("onesb", [128, 128], BF16))
        K.cst_b = Buf()
        K.ps = [st.enter_context(nc.psum_tensor(f"ps{i}", [128, 512], F32)) for i in range(8)]
        K.psb = [Buf() for _ in range(8)]
        P.dma([("sp", K.cstf[:, :], cst)], K.pld, writes=[K.cst_b])
        cp(K, "dve", K.identb[:, :], K.cstf[:, 0:128], [K.cst_b], [K.cst_b])
        cp(K, "dve", K.onesb[:, :], K.cstf[:, 896:1024], [K.cst_b], [K.cst_b])
        for p in phases:
            PHASES[p](K)
        for k, v in P.cnt.items():
            if v:
                P.ops["sp"].append(lambda e, sem=P.semh[k], v=v: e.wait_ge(sem, v))
        P.flush()
        K.ninstr = P.ninstr
    return nc, K


def run(phases, x, inputs, NSEQ, SEQ, ncores, trace=False):
    nc, K = build(phases, NSEQ, SEQ)
    cst = make_consts()
    in_maps = []
    for i in range(ncores):
        m = {name: np.ascontiguousarray(inputs[name], dtype=np.float32) for name in K.d.keys()}
        m["x"] = np.ascontiguousarray(x[i])
        m["cst"] = cst
        in_maps.append(m)
    res = run_bass_kernel_spmd(nc, in_maps, core_ids=list(range(ncores)), trace=trace)
    return np.stack([res.results[i]["out"] for i in range(ncores)]), res


def kernel(**inputs):
    x = np.asarray(inputs["x"], dtype=np.float32)
    B, S, _ = x.shape
    nseq = B // NCORES
    xs = x.reshape(NCORES, nseq * S, D)
    out, _ = run(ALL_PHASES, xs, inputs, nseq, S, NCORES)
    return out.reshape(B, S, D).astype(np.float32)
```

```python
import numpy as np
from contextlib import ExitStack
import concourse.bass as bass
import concourse.mybir as mybir
from concourse.bass_utils import run_bass_kernel_spmd

F32 = mybir.dt.float32
BF16 = mybir.dt.bfloat16
AF = mybir.ActivationFunctionType
ALU = mybir.AluOpType

LINV = 0
LINSTOP = 9
LINX = 0
D = 1024
TT = 512
EPS = 1e-6
FH = 2816
NCORES = 8

INPUT_SHAPES = [
    ("norm_mix", [4, 1024]), ("norm_ffn", [4, 1024]), ("norm_final", [1024]),
    ("gla_w_in", [1, 1024, 3088]), ("gla_w_g2", [1, 16, 512]), ("gla_b_g2", [1, 512]),
    ("gla_norm", [1, 256]), ("gla_w_out", [1, 1024, 1024]),
    ("cv_w_in", [1, 1024, 2048]), ("cv_b_in", [1, 2048]), ("cv_w_dw", [1, 31, 1024]),
    ("cv_b_dw", [1, 1024]), ("cv_ln_g", [1, 1024]), ("cv_ln_b", [1, 1024]),
    ("cv_w_out", [1, 1024, 1024]), ("cv_b_out", [1, 1024]),
    ("sg_w_in", [1, 1024, 2048]), ("sg_b_in", [1, 2048]), ("sg_ln_g", [1, 1024]),
    ("sg_ln_b", [1, 1024]), ("sg_w_s", [1, 8, 128, 128]), ("sg_b_s", [1, 8, 128]),
    ("sg_w_out", [1, 1024, 1024]), ("sg_b_out", [1, 1024]),
    ("hg_w_in", [1, 1024, 4096]), ("hg_lb_table", [4, 1024]), ("hg_norm", [1, 128]),
    ("hg_w_out", [1, 1024, 1024]),
    ("ffn_w_up", [4, 1024, 5632]), ("ffn_w_dw", [4, 3, 5632]), ("ffn_w_down", [4, 2816, 1024]),
]


class Buf:
    __slots__ = ("w", "r")

    def __init__(self):
        self.w = None
        self.r = {}


class Prog:
    ENG = ("pe", "dve", "act", "pool", "sp")

    def __init__(self, nc, stack):
        self.nc = nc
        self.stack = stack
        self.ops = {e: [] for e in self.ENG}
        self.semh = {}
        self.cnt = {}
        for e in ("pe", "dve", "act", "pool"):
            self.semh[e] = stack.enter_context(nc.semaphore("s_" + e))
            self.cnt[e] = 0
        self.seen = {e: {} for e in self.ENG}
        self.ninstr = 0

    def dma_sem(self, name):
        self.semh[name] = self.stack.enter_context(self.nc.semaphore(name))
        self.cnt[name] = 0
        return name

    def _waits(self, eng, reads, writes):
        deps = {}
        for b in reads:
            if b.w is not None:
                k, v = b.w
                if deps.get(k, 0) < v:
                    deps[k] = v
        for b in writes:
            if b.w is not None:
                k, v = b.w
                if deps.get(k, 0) < v:
                    deps[k] = v
            for k, v in b.r.items():
                if deps.get(k, 0) < v:
                    deps[k] = v
        seen = self.seen[eng]
        for k, v in deps.items():
            if k == eng:
                if eng == "pe":
                    continue
                if v <= self.cnt[eng] - 3:
                    continue
            if seen.get(k, 0) >= v:
                continue
            seen[k] = v
            sem = self.semh[k]
            self.ninstr += 1
            self.ops[eng].append(lambda e, sem=sem, v=v: e.wait_ge(sem, v))

    def _mark(self, ev, reads, writes):
        k, v = ev
        for b in reads:
            if b.r.get(k, 0) < v:
                b.r[k] = v
        for b in writes:
            b.w = ev
            b.r = {}

    def op(self, eng, fn, reads=(), writes=()):
        self._waits(eng, reads, writes)
        self.cnt[eng] += 1
        sem = self.semh[eng]
        self.ninstr += 1
        self.ops[eng].append(lambda e, fn=fn, sem=sem: fn(e).then_inc(sem, 1))
        self._mark((eng, self.cnt[eng]), reads, writes)

    def dma(self, pairs, sem, reads=(), writes=()):
        for q in sorted(set(p[0] for p in pairs)):
            self._waits(q, reads, writes)
            if self.cnt[sem] and self.seen[q].get(sem, 0) < self.cnt[sem]:
                self.seen[q][sem] = self.cnt[sem]
                self.ops[q].append(lambda e, s_=self.semh[sem], v=self.cnt[sem]: e.wait_ge(s_, v))
        h = self.semh[sem]
        for (q, o, i) in pairs:
            self.cnt[sem] += 16
            self.ninstr += 1
            self.ops[q].append(lambda e, o=o, i=i, h=h: e.dma_start(out=o, in_=i).then_inc(h, 16))
        self._mark((sem, self.cnt[sem]), reads, writes)

    def barrier(self):
        for eng in self.ENG:
            seen = self.seen[eng]
            for k, v in self.cnt.items():
                if v == 0 or seen.get(k, 0) >= v:
                    continue
                if k == eng and eng == "pe":
                    continue
                seen[k] = v
                sem = self.semh[k]
                self.ops[eng].append(lambda e, sem=sem, v=v: e.wait_ge(sem, v))

    def flush(self):
        ops = self.ops
        with self.nc.Block() as block:
            @block.tensor
            def _(e):
                for c in ops["pe"]:
                    c(e)

            @block.vector
            def _(e):
                for c in ops["dve"]:
                    c(e)

            @block.scalar
            def _(e):
                for c in ops["act"]:
                    c(e)

            @block.gpsimd
            def _(e):
                for c in ops["pool"]:
                    c(e)

            @block.sync
            def _(e):
                for c in ops["sp"]:
                    c(e)
        self.ops = {e: [] for e in self.ENG}


class Ctx:
    pass


class LazyInputs(dict):
    def __init__(self, nc):
        super().__init__()
        self.nc = nc
        self.shapes = dict(INPUT_SHAPES)

    def __missing__(self, name):
        ap = self.nc.dram_tensor(name, self.shapes[name], F32, kind="ExternalInput").ap()
        self[name] = ap
        return ap


def mm(K, out, lhsT, rhs, start, stop, reads, wb):
    K.P.op("pe", lambda e: e.matmul(out, lhsT=lhsT, rhs=rhs, start=start, stop=stop), reads=reads, writes=[wb])


def tr(K, out, in_, ident, reads, wb):
    K.P.op("pe", lambda e: e.transpose(out=out, in_=in_, identity=ident), reads=list(reads) + [K.cst_b], writes=[wb])


def act(K, out, in_, func, reads, writes, **kw):
    K.P.op("act", lambda e: e.activation(out=out, in_=in_, func=func, **kw), reads=reads, writes=writes)


def tt(K, eng, out, in0, in1, op, reads, writes):
    K.P.op(eng, lambda e: e.tensor_tensor(out=out, in0=in0, in1=in1, op=op), reads=reads, writes=writes)


def stt(K, out, in0, scalar, in1, op0, op1, reads, writes):
    K.P.op("dve", lambda e: e.scalar_tensor_tensor(out=out, in0=in0, scalar=scalar, in1=in1, op0=op0, op1=op1),
           reads=reads, writes=writes)


def ts(K, eng, out, in0, s1, s2, op0, op1, reads, writes):
    if s2 is None:
        K.P.op(eng, lambda e: e.tensor_scalar(out=out, in0=in0, scalar1=s1, scalar2=None, op0=op0), reads=reads, writes=writes)
    else:
        K.P.op(eng, lambda e: e.tensor_scalar(out=out, in0=in0, scalar1=s1, scalar2=s2, op0=op0, op1=op1), reads=reads, writes=writes)


def cp(K, eng, out, in_, reads, writes):
    if eng == "act":
        act(K, out, in_, AF.Copy, reads, writes)
    else:
        K.P.op(eng, lambda e: e.tensor_copy(out=out, in_=in_), reads=reads, writes=writes)


def memset(K, eng, ap, val, writes):
    K.P.op(eng, lambda e: e.memset(ap, val), writes=writes)


def bank(K, n=None):
    n = n or K.nb
    b = K.rr % n
    K.rr += 1
    return b


class Phase:
    def __init__(self, K, name):
        self.K = K
        self.name = name
        self.st = ExitStack()
        self.n = 0

    def sb(self, shape, dt, name=None):
        self.n += 1
        t = self.st.enter_context(self.K.nc.sbuf_tensor(f"{self.name}_{name or self.n}", list(shape), dt))
        return t, Buf()

    def close(self):
        K = self.K
        K.nb = 6
        K.P.barrier()
        K.P.flush()
        self.st.close()


def load_cols(K, ph, specs):
    P = K.P
    tot = sum(ap.shape[0] for _, ap in specs)
    G = (tot + 127) // 128
    stg, stg_b = ph.sb([128, G, 128], F32, "colstg")
    cols, cols_b = ph.sb([128, G * 128], F32, "cols")
    off = {}
    pairs = []
    r = 0
    for key, ap in specs:
        off[key] = r
        R = ap.shape[0]
        s = 0
        while s < R:
            n = min(R - s, 128 - (r % 128))
            pairs.append(("sp", stg[r % 128:r % 128 + n, r // 128, :], ap[s:s + n, :]))
            s += n
            r += n
    P.dma(pairs, K.pld, writes=[stg_b])
    for g in range(G):
        rows = min(128, tot - g * 128)
        pb = bank(K)
        tr(K, K.ps[pb][:, 0:rows], stg[0:rows, g, :], K.cstf[0:rows, 0:rows], [stg_b], K.psb[pb])
        cp(K, "act", cols[:, g * 128:g * 128 + rows], K.ps[pb][:, 0:rows], [K.psb[pb]], [cols_b])
    return cols, cols_b, off


def load_bcast(K, ph, specs):
    P = K.P
    tot = sum(ap.shape[1] for _, ap in specs)
    row, row_b = ph.sb([1, tot], F32, "bcrow")
    bc, bc_b = ph.sb([128, tot], F32, "bc")
    off = {}
    pairs = []
    c = 0
    for key, ap in specs:
        off[key] = c
        pairs.append(("sp", row[0:1, c:c + ap.shape[1]], ap))
        c += ap.shape[1]
    P.dma(pairs, K.pld, writes=[row_b])
    for s in range(0, tot, 512):
        n = min(512, tot - s)
        pb = bank(K)
        mm(K, K.ps[pb][:, 0:n], K.cstf[0:1, 896:1024], row[0:1, s:s + n], True, True, [row_b, K.cst_b], K.psb[pb])
        cp(K, "act", bc[:, s:s + n], K.ps[pb][:, 0:n], [K.psb[pb]], [bc_b])
    return bc, bc_b, off


def load_x(K, t, xt, xt_b):
    K.P.dma([("sp", xt[:, :, :], K.xT[:, t * TT:(t + 1) * TT].rearrange("(c p) t -> p c t", p=128))],
            K.ld, writes=[xt_b])


def store_x(K, t, xt, xt_b):
    K.P.dma([("sp", K.xT[:, t * TT:(t + 1) * TT].rearrange("(c p) t -> p c t", p=128), xt[:, :, :])],
            K.stx, reads=[xt_b])


def bcast_stat(K, srcs, src_bufs, out, out_b, scale, func=None, bias=EPS):
    pb = bank(K)
    n = len(srcs)
    for i, s in enumerate(srcs):
        mm(K, K.ps[pb][:, 0:out.shape[-1]], K.onesb[:, :], s, i == 0, i == n - 1, list(src_bufs) + [K.cst_b], K.psb[pb])
    if func is None:
        act(K, out, K.ps[pb][:, 0:out.shape[-1]], AF.Ln, [K.psb[pb]], [out_b], scale=scale, bias=bias)
        act(K, out, out, AF.Exp, [out_b], [out_b], scale=-0.5)
    else:
        act(K, out, K.ps[pb][:, 0:out.shape[-1]], func, [K.psb[pb]], [out_b], scale=scale, bias=bias)
    return pb


def rmsnorm_h(K, xt, xt_b, g, g_b, sq, sq_b, rstd, rstd_b, h, h_b):
    act(K, sq[:, 0:8, :], xt[:, :, :], AF.Square, [xt_b], [sq_b])
    bcast_stat(K, [sq[:, c, :] for c in range(8)], [sq_b], rstd[:, :], rstd_b, 1.0 / D)
    for c in range(8):
        stt(K, h[:, c, :], xt[:, c, :], g[:, c:c + 1], rstd[:, :], ALU.mult, ALU.mult, [xt_b, g_b, rstd_b], [h_b])


class XPipe:
    def __init__(self, K, ph, g, g_b, xt, xt_b, h, h_b, sq, sq_b, rstd, rstd_b, nring=4):
        self.K, self.g, self.g_b = K, g, g_b
        self.xt, self.xt_b, self.h, self.h_b = xt, xt_b, h, h_b
        self.sq, self.sq_b, self.rstd, self.rstd_b = sq, sq_b, rstd, rstd_b
        self.ring = [ph.sb([128, TT], F32, f"xr{i}") for i in range(nring)]
        self.r = 0

    def load(self, t):
        load_x(self.K, t, self.xt, self.xt_b)

    def square(self):
        act(self.K, self.sq[:, 0:8, :], self.xt[:, :, :], AF.Square, [self.xt_b], [self.sq_b])

    def make_h(self):
        K = self.K
        bcast_stat(K, [self.sq[:, c, :] for c in range(8)], [self.sq_b], self.rstd[:, :], self.rstd_b, 1.0 / D)
        for c in range(8):
            stt(K, self.h[:, c, :], self.xt[:, c, :], self.g[:, c:c + 1], self.rstd[:, :], ALU.mult, ALU.mult,
                [self.xt_b, self.g_b, self.rstd_b], [self.h_b])

    def res_load(self, t, c):
        K = self.K
        i = self.r % len(self.ring)
        self.r += 1
        xr, xr_b = self.ring[i]
        K.P.dma([("sp", xr[:, :], K.xT[c * 128:(c + 1) * 128, t * TT:(t + 1) * TT])], K.rl[i], writes=[xr_b])
        return i

    def res_add_store(self, t, c, i, pb, bias=None, bias_b=None):
        K = self.K
        xr, xr_b = self.ring[i]
        if bias is None:
            tt(K, "dve", xr[:, :], xr[:, :], K.ps[pb][:, :], ALU.add, [xr_b, K.psb[pb]], [xr_b])
        else:
            stt(K, xr[:, :], K.ps[pb][:, :], bias, xr[:, :], ALU.add, ALU.add, [K.psb[pb], bias_b, xr_b], [xr_b])
        K.P.dma([("sp", K.xT[c * 128:(c + 1) * 128, t * TT:(t + 1) * TT], xr[:, :])], K.rs[i], reads=[xr_b])

    def out_proj(self, t, wout, wout_b, src, src_b, bias_fn=None, bias_b=None):
        K = self.K
        for half in range(2):
            slots = [self.res_load(t, c) for c in range(half * 4, half * 4 + 4)]
            for c in range(half * 4, half * 4 + 4):
                pb = bank(K)
                for d in range(8):
                    sb_d = src_b[d] if isinstance(src_b, list) else src_b
                    mm(K, K.ps[pb][:, :], wout[:, d, c * 128:(c + 1) * 128], src[:, d, :], d == 0, d == 7, [wout_b, sb_d], K.psb[pb])
                self.res_add_store(t, c, slots[c - half * 4], pb, None if bias_fn is None else bias_fn(c), bias_b)


def phase_tin(K):
    ph = Phase(K, "tin")
    K.nb = 8
    xins = [ph.sb([128, 4, D], F32) for _ in range(2)]
    xts = [ph.sb([128, 8, TT], F32) for _ in range(2)]
    for t in range(K.NT):
        xin, xin_b = xins[t % 2]
        xt, xt_b = xts[t % 2]
        K.P.dma([("sp", xin[:, :, :], K.x[t * TT:(t + 1) * TT, :].rearrange("(g p) d -> p g d", p=128))], K.rl[t % 2], writes=[xin_b])
        for c in range(8):
            pb = bank(K)
            for g in range(4):
                tr(K, K.ps[pb][:, g * 128:(g + 1) * 128], xin[:, g, c * 128:(c + 1) * 128], K.cstf[:, 0:128], [xin_b], K.psb[pb])
            cp(K, "act" if c % 2 else "dve", xt[:, c, :], K.ps[pb][:, :], [K.psb[pb]], [xt_b])
        K.P.dma([("sp", K.xT[:, t * TT:(t + 1) * TT].rearrange("(c p) t -> p c t", p=128), xt[:, :, :])], K.rs[t % 2], reads=[xt_b])
    ph.close()


def phase_fin(K, norm):
    ph = Phase(K, "fin")
    K.nb = 8
    xts = [ph.sb([128, 8, TT], F32) for _ in range(2)]
    yos = [ph.sb([128, 4, D], F32) for _ in range(2)]
    if norm:
        cols, cols_b, off = load_cols(K, ph, [("g", K.d["norm_final"].rearrange("(c p) -> c p", p=128))])
        sq, sq_b = ph.sb([128, 8, TT], BF16)
        rstd, rstd_b = ph.sb([128, TT], F32)
    for t in range(K.NT):
        xt, xt_b = xts[t % 2]
        yo, yo_b = yos[t % 2]
        K.P.dma([("sp", xt[:, :, :], K.xT[:, t * TT:(t + 1) * TT].rearrange("(c p) t -> p c t", p=128))], K.rl[t % 2], writes=[xt_b])
        if norm:
            act(K, sq[:, :, :], xt[:, :, :], AF.Square, [xt_b], [sq_b])
            bcast_stat(K, [sq[:, c, :] for c in range(8)], [sq_b], rstd[:, :], rstd_b, 1.0 / D)
            for c in range(8):
                stt(K, xt[:, c, :], xt[:, c, :], cols[:, off["g"] + c:off["g"] + c + 1], rstd[:, :], ALU.mult, ALU.mult,
                    [xt_b, cols_b, rstd_b], [xt_b])
        for g in range(4):
            for half in range(2):
                pb = bank(K)
                for c4 in range(4):
                    c = half * 4 + c4
                    tr(K, K.ps[pb][:, c4 * 128:(c4 + 1) * 128], xt[:, c, g * 128:(g + 1) * 128], K.cstf[:, 0:128], [xt_b], K.psb[pb])
                cp(K, "act" if half else "dve", yo[:, g, half * 512:(half + 1) * 512], K.ps[pb][:, :], [K.psb[pb]], [yo_b])
        K.P.dma([("sp", K.out[t * TT:(t + 1) * TT, :].rearrange("(g p) d -> p g d", p=128), yo[:, :, :])], K.rs[t % 2], reads=[yo_b])
    ph.close()


def phase_ffn(K, li):
    P = K.P
    ph = Phase(K, f"ffn{li}")
    K.nb = 4
    wup, wup_b = ph.sb([128, 8, 2 * FH], BF16, "wup")
    wdn, wdn_b = ph.sb([128, 22, D], BF16, "wdn")
    P.dma([("pool", wup[:, c, :], K.d["ffn_w_up"][li, c * 128:(c + 1) * 128, :]) for c in range(8)], K.wld, writes=[wup_b])
    P.dma([("pool", wdn[:, j, :], K.d["ffn_w_down"][li, j * 128:(j + 1) * 128, :]) for j in range(22)], K.wld, writes=[wdn_b])
    cols, cols_b, off = load_cols(K, ph, [
        ("g", K.d["norm_ffn"][li].rearrange("(c p) -> c p", p=128)),
        ("dw", K.d["ffn_w_dw"][li].rearrange("k (c p) -> (k c) p", p=128)),
    ])
    xt, xt_b = ph.sb([128, 8, TT], F32, "xt")
    h, h_b = ph.sb([128, 8, TT], BF16, "h")
    u, _ = ph.sb([128, 22, TT], BF16, "u")
    u_bs = [Buf() for _ in range(22)]
    rstd, rstd_b = ph.sb([128, TT], F32, "rstd")
    NZ = 2
    zs = [ph.sb([128, TT + 2], F32, f"zs{i}") for i in range(NZ)]
    acc = [ph.sb([128, TT], F32, f"acc{i}") for i in range(4)]
    ztail, ztail_b = ph.sb([128, 44, 2], F32, "ztail")
    NR = 4
    ring = [ph.sb([128, TT], F32, f"xr{i}") for i in range(NR)]
    g = cols[:, off["g"]:off["g"] + 8]
    dw0 = off["dw"]
    ACC = [4, 5, 6, 7]
    st = {"k": 0, "r": 0}

    def norm_a(t):
        load_x(K, t, xt, xt_b)
        act(K, h[:, :, :], xt[:, :, :], AF.Square, [xt_b], [h_b])

    def norm_b():
        bcast_stat(K, [h[:, c, :] for c in range(8)], [h_b], rstd[:, :], rstd_b, 1.0 / D)
        for c in range(8):
            stt(K, h[:, c, :], xt[:, c, :], g[:, c:c + 1], rstd[:, :], ALU.mult, ALU.mult, [xt_b, cols_b, rstd_b], [h_b])

    def res_load(t, c):
        i = st["r"] % NR
        st["r"] += 1
        xr, xr_b = ring[i]
        P.dma([("sp", xr[:, :], K.xT[c * 128:(c + 1) * 128, t * TT:(t + 1) * TT])], K.rl[i], writes=[xr_b])
        return i

    def res_add_store(t, c, i, pb):
        xr, xr_b = ring[i]
        tt(K, "dve", xr[:, :], xr[:, :], K.ps[pb][:, :], ALU.add, [xr_b, K.psb[pb]], [xr_b])
        P.dma([("sp", K.xT[c * 128:(c + 1) * 128, t * TT:(t + 1) * TT], xr[:, :])], K.rs[i], reads=[xr_b])

    norm_a(0)
    norm_b()
    for t in range(K.NT):
        if t % K.TPS == 0:
            memset(K, "pool", ztail[:, :, :], 0.0, [ztail_b])
        for j in range(22):
            a2 = []
            for which, fc in ((0, j), (1, 22 + j)):
                pb = bank(K)
                for d in range(8):
                    mm(K, K.ps[pb][:, :], wup[:, d, fc * 128:(fc + 1) * 128], h[:, d, :], d == 0, d == 7, [wup_b, h_b], K.psb[pb])
                z, z_b = zs[st["k"] % NZ]
                a, a_b = acc[(2 * (j % 2)) + which]
                st["k"] += 1
                cp(K, "pool", z[:, 0:2], ztail[:, fc, :], [ztail_b], [z_b])
                act(K, z[:, 2:TT + 2], K.ps[pb][:, :], AF.Copy, [K.psb[pb]], [z_b])
                cp(K, "pool", ztail[:, fc, :], z[:, TT:TT + 2], [z_b], [ztail_b])
                act(K, a[:, :], z[:, 0:TT], AF.Copy, [z_b, cols_b], [a_b], scale=cols[:, dw0 + fc:dw0 + fc + 1])
                stt(K, a[:, :], z[:, 1:TT + 1], cols[:, dw0 + 44 + fc:dw0 + 44 + fc + 1], a[:, :], ALU.mult, ALU.add, [z_b, cols_b, a_b], [a_b])
                stt(K, a[:, :], z[:, 2:TT + 2], cols[:, dw0 + 88 + fc:dw0 + 88 + fc + 1], a[:, :], ALU.mult, ALU.add, [z_b, cols_b, a_b], [a_b])
                a2.append((a, a_b))
            (ag, ag_b), (av, av_b) = a2
            act(K, ag[:, :], ag[:, :], AF.Silu, [ag_b], [ag_b])
            tt(K, "dve", u[:, j, :], ag[:, :], av[:, :], ALU.mult, [ag_b, av_b], [u_bs[j]])
        nxt = t + 1 < K.NT
        if nxt:
            norm_a(t + 1)
        slots = [res_load(t, c) for c in range(4)]
        for c in range(4):
            for j in range(20):
                mm(K, K.ps[ACC[c]][:, :], wdn[:, j, c * 128:(c + 1) * 128], u[:, j, :], j == 0, False, [wdn_b, u_bs[j]], K.psb[ACC[c]])
        for c in range(4):
            for j in (20, 21):
                mm(K, K.ps[ACC[c]][:, :], wdn[:, j, c * 128:(c + 1) * 128], u[:, j, :], False, j == 21, [wdn_b, u_bs[j]], K.psb[ACC[c]])
        for c in range(4):
            res_add_store(t, c, slots[c], ACC[c])
        if nxt:
            norm_b()
        slots = [res_load(t, c) for c in range(4, 8)]
        for c in range(4, 8):
            pb = bank(K)
            for j in range(22):
                mm(K, K.ps[pb][:, :], wdn[:, j, c * 128:(c + 1) * 128], u[:, j, :], j == 0, j == 21, [wdn_b, u_bs[j]], K.psb[pb])
            res_add_store(t, c, slots[c - 4], pb)
    K.nb = 6
    ph.close()


def phase_conf(K):
    P = K.P
    ph = Phase(K, "conf")
    K.nb = 8
    win, win_b = ph.sb([128, 8, 2048], BF16, "win")
    wout, wout_b = ph.sb([128, 8, D], BF16, "wout")
    P.dma([("pool", win[:, c, :], K.d["cv_w_in"][0, c * 128:(c + 1) * 128, :]) for c in range(8)], K.wld, writes=[win_b])
    P.dma([("pool", wout[:, c, :], K.d["cv_w_out"][0, c * 128:(c + 1) * 128, :]) for c in range(8)], K.wld, writes=[wout_b])
    v2 = lambda a: a.rearrange("(c p) -> c p", p=128)
    cols, cols_b, off = load_cols(K, ph, [
        ("g", v2(K.d["norm_mix"][1])), ("bin", v2(K.d["cv_b_in"][0])),
        ("dw", K.d["cv_w_dw"][0].rearrange("k (c p) -> (k c) p", p=128)),
        ("bdw", v2(K.d["cv_b_dw"][0])), ("lng", v2(K.d["cv_ln_g"][0])), ("lnb", v2(K.d["cv_ln_b"][0])),
        ("bout", v2(K.d["cv_b_out"][0])),
    ])
    col = lambda key, i: cols[:, off[key] + i:off[key] + i + 1]
    xt, xt_b = ph.sb([128, 8, TT], F32, "xt")
    h, h_b = ph.sb([128, 8, TT], BF16, "h")
    csq, csq_b = ph.sb([128, 8, TT], BF16, "csq")
    sq, sq_b = csq, csq_b
    rstd, rstd_b = ph.sb([128, TT], F32, "rstd")
    glu = [ph.sb([128, TT + 30], BF16, f"glu{c}") for c in range(8)]
    cacc = [ph.sb([128, TT], F32, f"cacc{c}") for c in range(8)]
    sg = [ph.sb([128, TT], F32, f"sg{i}") for i in range(2)]
    cb, _ = ph.sb([128, 8, TT], BF16, "cb")
    cb_b = [Buf() for _ in range(8)]
    sl, sl_b = cb, cb_b
    mean, mean_b = ph.sb([128, TT], F32, "mean")
    msq, msq_b = ph.sb([128, TT], F32, "msq")
    lrs, lrs_b = ph.sb([128, TT], F32, "lrs")
    tmp = [ph.sb([128, TT], F32, f"tmp{i}") for i in range(2)]
    dg, dg_b = ph.sb([128, 31 * 8, 128], BF16, "dg")
    for i in range(31 * 8):
        ts(K, "dve", dg[:, i, :], K.cstf[:, 0:128], col("dw", i), None, ALU.mult, None, [K.cst_b, cols_b], [dg_b])
    xp = XPipe(K, ph, cols[:, off["g"]:off["g"] + 8], cols_b, xt, xt_b, h, h_b, sq, sq_b, rstd, rstd_b)
    xp.load(0)
    xp.square()
    xp.make_h()
    for t in range(K.NT):
        nxt = t + 1 < K.NT
        if nxt:
            xp.load(t + 1)
        for c in range(8):
            gl, gl_b = glu[c]
            if t % K.TPS == 0:
                memset(K, "pool", gl[:, 0:30], 0.0, [gl_b])
            else:
                cp(K, "pool", gl[:, 0:30], gl[:, TT:TT + 30], [gl_b], [gl_b])
            pa = bank(K)
            for d in range(8):
                mm(K, K.ps[pa][:, :], win[:, d, c * 128:(c + 1) * 128], h[:, d, :], d == 0, d == 7, [win_b, h_b], K.psb[pa])
            pg = bank(K)
            for d in range(8):
                mm(K, K.ps[pg][:, :], win[:, d, D + c * 128:D + (c + 1) * 128], h[:, d, :], d == 0, d == 7, [win_b, h_b], K.psb[pg])
            s, s_b = sg[c % 2]
            act(K, s[:, :], K.ps[pg][:, :], AF.Sigmoid, [K.psb[pg], cols_b], [s_b], bias=col("bin", 8 + c))
            stt(K, gl[:, 30:TT + 30], K.ps[pa][:, :], col("bin", c), s[:, :], ALU.add, ALU.mult, [K.psb[pa], cols_b, s_b], [gl_b])
        if nxt:
            xp.square()
            xp.make_h()
        for c in range(8):
            gl, gl_b = glu[c]
            pc = bank(K)
            for kk in range(31):
                mm(K, K.ps[pc][:, :], dg[:, kk * 8 + c, :], gl[:, kk:kk + TT], kk == 0, kk == 30, [dg_b, gl_b], K.psb[pc])
            ca, ca_b = cacc[c]
            act(K, ca[:, :], K.ps[pc][:, :], AF.Identity, [K.psb[pc], cols_b], [ca_b], bias=col("bdw", c))
            cp(K, "dve", cb[:, c, :], ca[:, :], [ca_b], [cb_b[c]])
            act(K, csq[:, c, :], ca[:, :], AF.Square, [ca_b], [csq_b])
        bcast_stat(K, [cb[:, c, :] for c in range(8)], cb_b, mean[:, :], mean_b, 1.0 / D, func=AF.Copy, bias=0.0)
        tt(K, "dve", msq[:, :], mean[:, :], mean[:, :], ALU.mult, [mean_b], [msq_b])
        pb = bank(K)
        for c in range(8):
            mm(K, K.ps[pb][:, :], K.onesb[:, :], csq[:, c, :], c == 0, c == 7, [csq_b, K.cst_b], K.psb[pb])
        stt(K, lrs[:, :], K.ps[pb][:, :], 1.0 / D, msq[:, :], ALU.mult, ALU.subtract, [K.psb[pb], msq_b], [lrs_b])
        act(K, lrs[:, :], lrs[:, :], AF.Ln, [lrs_b], [lrs_b], bias=EPS)
        act(K, lrs[:, :], lrs[:, :], AF.Exp, [lrs_b], [lrs_b], scale=-0.5)
        for c in range(8):
            ca, ca_b = cacc[c]
            tm, tm_b = tmp[c % 2]
            tt(K, "dve", tm[:, :], ca[:, :], mean[:, :], ALU.subtract, [ca_b, mean_b], [tm_b])
            tt(K, "dve", tm[:, :], tm[:, :], lrs[:, :], ALU.mult, [tm_b, lrs_b], [tm_b])
            act(K, sl[:, c, :], tm[:, :], AF.Silu, [tm_b, cols_b], [sl_b[c]], scale=col("lng", c), bias=col("lnb", c))
        xp.out_proj(t, wout, wout_b, sl, sl_b, lambda c: col("bout", c), cols_b)
    ph.close()


def phase_sgu(K):
    P = K.P
    ph = Phase(K, "sgu")
    K.nb = 8
    win, win_b = ph.sb([128, 8, 2048], BF16, "win")
    wout, wout_b = ph.sb([128, 8, D], BF16, "wout")
    wsn, wsn_b = ph.sb([128, 8, 128], F32, "wsn")
    P.dma([("pool", win[:, c, :], K.d["sg_w_in"][0, c * 128:(c + 1) * 128, :]) for c in range(8)], K.wld, writes=[win_b])
    P.dma([("pool", wout[:, c, :], K.d["sg_w_out"][0, c * 128:(c + 1) * 128, :]) for c in range(8)], K.wld, writes=[wout_b])
    P.dma([("sp", wsn[:, g, :], K.d["sg_w_s"][0, g, :, :]) for g in range(8)], K.pld, writes=[wsn_b])
    v2 = lambda a: a.rearrange("(c p) -> c p", p=128)
    cols, cols_b, off = load_cols(K, ph, [
        ("g", v2(K.d["norm_mix"][2])), ("binu", v2(K.d["sg_b_in"][0, 0:D])), ("bout", v2(K.d["sg_b_out"][0])),
    ])
    col = lambda key, i: cols[:, off[key] + i:off[key] + i + 1]
    bc, bc_b, boff = load_bcast(K, ph, [
        ("binv", K.d["sg_b_in"][0:1, D:2 * D]), ("lng", K.d["sg_ln_g"][0:1, :]), ("lnb", K.d["sg_ln_b"][0:1, :]),
        ("bs", K.d["sg_b_s"][0:1, :, :].rearrange("o g i -> o (g i)")),
    ])
    wct, wct_b = ph.sb([128, 8, 128], BF16, "wct")
    for g in range(8):
        pb = bank(K)
        tr(K, K.ps[pb][:, 0:128], wsn[:, g, :], K.cstf[:, 0:128], [wsn_b], K.psb[pb])
        tt(K, "dve", wct[:, g, :], K.ps[pb][:, 0:128], K.cstf[:, 128:256], ALU.mult, [K.psb[pb], K.cst_b], [wct_b])
    bs4, bs4_b = ph.sb([128, 8, TT], F32, "bs4")
    for tg in range(4):
        cp(K, "pool", bs4[:, :, tg * 128:(tg + 1) * 128], bc[:, boff["bs"]:boff["bs"] + 1024].rearrange("p (g i) -> p g i", i=128), [bc_b], [bs4_b])
    xt, xt_b = ph.sb([128, 8, TT], F32, "xt")
    h, h_b = ph.sb([128, 8, TT], BF16, "h")
    sq, sq_b = ph.sb([128, 8, TT], BF16, "sq")
    rstd, rstd_b = ph.sb([128, TT], F32, "rstd")
    ug, ug_b = ph.sb([128, 8, TT], F32, "ug")
    vx = [ph.sb([128, D], F32, f"vx{i}") for i in range(2)]
    vnb, vnb_b = ph.sb([128, 4, D], BF16, "vnb")
    stats, stats_b = ph.sb([128, 2, 6], F32, "stats")
    mv, mv_b = ph.sb([128, 2], F32, "mv")
    tmp = [ph.sb([128, TT], F32, f"tmp{i}") for i in range(2)]
    gated, _ = ph.sb([128, 8, TT], BF16, "gated")
    gated_b = [Buf() for _ in range(8)]
    xp = XPipe(K, ph, cols[:, off["g"]:off["g"] + 8], cols_b, xt, xt_b, h, h_b, sq, sq_b, rstd, rstd_b)
    xp.load(0)
    xp.square()
    xp.make_h()
    for t in range(K.NT):
        nxt = t + 1 < K.NT
        if nxt:
            xp.load(t + 1)
        for tg in range(4):
            vxt, vx_b = vx[tg % 2]
            for half in range(2):
                pb = bank(K)
                for d in range(8):
                    mm(K, K.ps[pb][:, :], h[:, d, tg * 128:(tg + 1) * 128], win[:, d, D + half * 512:D + (half + 1) * 512], d == 0, d == 7,
                       [win_b, h_b], K.psb[pb])
                tt(K, "dve", vxt[:, half * 512:(half + 1) * 512], K.ps[pb][:, :], bc[:, boff["binv"] + half * 512:boff["binv"] + (half + 1) * 512],
                   ALU.add, [K.psb[pb], bc_b], [vx_b])
            act(K, vxt[:, :], vxt[:, :], AF.Gelu_apprx_tanh, [vx_b], [vx_b])
            for half in range(2):
                K.P.op("dve", lambda e, half=half, vxt=vxt: e.bn_stats(out=stats[:, half, :], in_=vxt[:, half * 512:(half + 1) * 512]),
                       reads=[vx_b], writes=[stats_b])
            K.P.op("dve", lambda e: e.bn_aggr(out=mv[:, :], in_=stats[:, :, :].rearrange("p a b -> p (a b)")), reads=[stats_b], writes=[mv_b])
            act(K, mv[:, 1:2], mv[:, 1:2], AF.Ln, [mv_b], [mv_b], bias=EPS)
            act(K, mv[:, 1:2], mv[:, 1:2], AF.Exp, [mv_b], [mv_b], scale=-0.5)
            ts(K, "dve", vxt[:, :], vxt[:, :], mv[:, 0:1], mv[:, 1:2], ALU.subtract, ALU.mult, [vx_b, mv_b], [vx_b])
            tt(K, "dve", vxt[:, :], vxt[:, :], bc[:, boff["lng"]:boff["lng"] + D], ALU.mult, [vx_b, bc_b], [vx_b])
            tt(K, "dve", vnb[:, tg, :], vxt[:, :], bc[:, boff["lnb"]:boff["lnb"] + D], ALU.add, [vx_b, bc_b], [vnb_b])
        for c in range(8):
            pb = bank(K)
            for d in range(8):
                mm(K, K.ps[pb][:, :], win[:, d, c * 128:(c + 1) * 128], h[:, d, :], d == 0, d == 7, [win_b, h_b], K.psb[pb])
            act(K, ug[:, c, :], K.ps[pb][:, :], AF.Gelu_apprx_tanh, [K.psb[pb], cols_b], [ug_b], bias=col("binu", c))
        if nxt:
            xp.square()
            xp.make_h()
        for g in range(8):
            pb = bank(K)
            for tg in range(4):
                mm(K, K.ps[pb][:, tg * 128:(tg + 1) * 128], vnb[:, tg, g * 128:(g + 1) * 128], wct[:, g, :], True, True, [vnb_b, wct_b], K.psb[pb])
            tm, tm_b = tmp[g % 2]
            tt(K, "dve", tm[:, :], K.ps[pb][:, :], bs4[:, g, :], ALU.add, [K.psb[pb], bs4_b], [tm_b])
            tt(K, "dve", gated[:, g, :], tm[:, :], ug[:, g, :], ALU.mult, [tm_b, ug_b], [gated_b[g]])
        xp.out_proj(t, wout, wout_b, gated, gated_b, lambda c: col("bout", c), cols_b)
    ph.close()


def phase_lin(K, kind):
    P = K.P
    gla = kind == "gla"
    ph = Phase(K, kind)
    NH, DV = (4, 256) if gla else (8, 128)
    DVC = DV // 128
    WIN = 3088 if gla else 4096
    wkey = "gla_w_in" if gla else "hg_w_in"
    okey = "gla_w_out" if gla else "hg_w_out"
    layer = 0 if gla else 3
    win, win_b = ph.sb([128, 8, WIN], BF16, "win")
    wout, wout_b = ph.sb([128, 8, D], BF16, "wout")
    pairs = [("pool", win[:, c, :], K.d[wkey][0, c * 128:(c + 1) * 128, :]) for c in range(8)]
    pairs += [("pool", wout[:, c, :], K.d[okey][0, c * 128:(c + 1) * 128, :]) for c in range(8)]
    wbufs = [win_b, wout_b]
    if gla:
        wg2, wg2_b = ph.sb([16, 512], BF16, "wg2")
        pairs.append(("pool", wg2[:, :], K.d["gla_w_g2"][0]))
        wbufs.append(wg2_b)
    P.dma(pairs, K.wld, writes=wbufs)
    v2 = lambda a: a.rearrange("(c p) -> c p", p=128)
    specs = [("g", v2(K.d["norm_mix"][layer]))]
    if gla:
        specs += [("bg2", v2(K.d["gla_b_g2"][0])), ("gn", v2(K.d["gla_norm"][0]))]
    else:
        specs += [("gn", v2(K.d["hg_norm"][0])), ("lbt", K.d["hg_lb_table"].rearrange("l (c p) -> (l c) p", p=128))]
    cols, cols_b, off = load_cols(K, ph, specs)
    col = lambda key, i: cols[:, off[key] + i:off[key] + i + 1]
    par, par_b = ph.sb([128, 16], F32, "par")
    if gla:
        ts(K, "dve", par[:, 0:4], cols[:, off["bg2"]:off["bg2"] + 4], -1.0, None, ALU.mult, None, [cols_b], [par_b])
        dsc = 1.0 / 16.0
        qscale = 128.0 ** -0.5
    else:
        lt = cols[:, off["lbt"]:off["lbt"] + 32].rearrange("p (l c) -> p l c", c=8)
        ex, ex_b = ph.sb([128, 4, 8], F32, "lbex")
        mx, mx_b = ph.sb([128, 8], F32, "lbmx")
        tt(K, "dve", mx[:, :], lt[:, 0, :], lt[:, 1, :], ALU.max, [cols_b], [mx_b])
        tt(K, "dve", mx[:, :], mx[:, :], lt[:, 2, :], ALU.max, [cols_b, mx_b], [mx_b])
        tt(K, "dve", mx[:, :], mx[:, :], lt[:, 3, :], ALU.max, [cols_b, mx_b], [mx_b])
        for l in range(4):
            tt(K, "dve", ex[:, l, :], lt[:, l, :], mx[:, :], ALU.subtract, [cols_b, mx_b], [ex_b])
        act(K, ex[:, :, :], ex[:, :, :], AF.Exp, [ex_b], [ex_b])
        tt(K, "dve", par[:, 8:16], ex[:, 1, :], ex[:, 2, :], ALU.add, [ex_b], [par_b])
        tt(K, "dve", par[:, 8:16], par[:, 8:16], ex[:, 3, :], ALU.add, [ex_b, par_b], [par_b])
        tt(K, "dve", mx[:, :], par[:, 8:16], ex[:, 0, :], ALU.add, [ex_b, par_b], [mx_b])
        K.P.op("dve", lambda e: e.reciprocal(out=mx[:, :], in_=mx[:, :]), reads=[mx_b], writes=[mx_b])
        tt(K, "dve", par[:, 0:8], par[:, 8:16], mx[:, :], ALU.mult, [par_b, mx_b], [par_b])
        dsc = 1.0
        qscale = 1.0
    xt, xt_b = ph.sb([128, 8, TT], F32, "xt")
    h, h_b = ph.sb([128, 8, TT], BF16, "h")
    rstd, rstd_b = ph.sb([128, TT], F32, "rstd")
    vt, vt_b = ph.sb([128, 4, D], BF16, "vt")
    gate, gate_b = ph.sb([128, 8, TT], BF16, "gate")
    sq, sq_b = gate, gate_b
    if not gla:
        qsall, qsall_b = ph.sb([128, NH, TT], BF16, "qsall")
    og, og_b = ph.sb([128, 8, TT], BF16, "og")
    f32t = lambda n: ph.sb([128, TT], F32, n)
    bft = lambda n: ph.sb([128, TT], BF16, n)
    (l1, l1_b), (l2, l2_b), (cl, cl_b) = f32t("l1"), f32t("l2"), f32t("cl")
    (dA, dA_b), (dD, dD_b) = (l1, l1_b), (l2, l2_b)
    (eA, eA_b), (eB, eB_b), (eC, eC_b), (eD, eD_b) = f32t("eA"), f32t("eB"), f32t("eC"), f32t("eD")
    (qs, qs_b), (kk, kk_b) = f32t("qs"), f32t("kk")
    qtS, ktS, qdS, kdS = [[bft(f"{n}{i}") for i in range(2)] for n in ("qt", "kt", "qd", "kd")]
    eCS = [(eC, eC_b), f32t("eC1")]
    stmS = [ph.sb([128, 4, 128], BF16, f"stm{i}") for i in range(2)]
    kdt, kdt_b = ph.sb([128, 4, 128], BF16, "kdt")
    Sall, Sall_b = ph.sb([128, 7, DV], F32, "Sall")
    Sb8S = [ph.sb([128, 8, DV], BF16, f"Sb8_{i}") for i in range(2)]
    sqo, sqo_b = ph.sb([128, DVC, TT], BF16, "sqo")
    rs, rs_b = f32t("rs")
    tmpo, tmpo_b = f32t("tmpo")
    if gla:
        glr, glr_b = ph.sb([16, TT], BF16, "glr")
    S32 = [ph.sb([128, DV], F32, f"S32_{i}") for i in range(NH)]
    NB = 2
    K.nb = NB
    PS_S, PS_T, PS_KV = 2, 3, [4, 5]
    POS = [[6, 7], [6, 7]] if gla else [[6], [7]]
    bd4 = K.cstf[:, 256:384].rearrange("p (o i) -> p o i", o=1).to_broadcast([128, 4, 128])
    msk = K.cstf[:, 384:896]
    bd = K.cstf[:, 256:384]
    c3 = lambda a: a[:, :].rearrange("p (c t) -> p c t", t=64)
    QOFF, KOFF = (0, 512) if gla else (0, 1024)
    VOFF = 1024 if gla else 2048
    GOFF = 2048 if gla else 3072

    def pre(hd):
        s_ = hd % 2
        (qt, qt_b), (kt, kt_b), (qd, qd_b), (kd, kd_b) = qtS[s_], ktS[s_], qdS[s_], kdS[s_]
        eC_, eC_b_ = eCS[s_]
        if gla:
            pq = bank(K, NB)
            for d in range(8):
                mm(K, K.ps[pq][:, :], win[:, d, QOFF + hd * 128:QOFF + (hd + 1) * 128], h[:, d, :], d == 0, d == 7, [win_b, h_b], K.psb[pq])
            cp(K, "act", qs[:, :], K.ps[pq][:, :], [K.psb[pq]], [qs_b])
            pk = bank(K, NB)
            for d in range(8):
                mm(K, K.ps[pk][:, :], win[:, d, KOFF + hd * 128:KOFF + (hd + 1) * 128], h[:, d, :], d == 0, d == 7, [win_b, h_b], K.psb[pk])
            cp(K, "act", kk[:, :], K.ps[pk][:, :], [K.psb[pk]], [kk_b])
            pg = bank(K, NB)
            mm(K, K.ps[pg][:, :], wg2[0:16, hd * 128:(hd + 1) * 128], glr[0:16, :], True, True, [wg2_b, glr_b], K.psb[pg])
            act(K, l1[:, :], K.ps[pg][:, :], AF.Exp, [K.psb[pg], par_b], [l1_b], scale=-1.0, bias=par[:, hd:hd + 1])
            act(K, l2[:, :], l1[:, :], AF.Ln, [l1_b], [l2_b], bias=1.0)
            yield
        else:
            pk = bank(K, NB)
            for d in range(8):
                mm(K, K.ps[pk][:, :], win[:, d, KOFF + hd * 128:KOFF + (hd + 1) * 128], h[:, d, :], d == 0, d == 7, [win_b, h_b], K.psb[pk])
            act(K, l1[:, :], K.ps[pk][:, :], AF.Exp, [K.psb[pk]], [l1_b], scale=-1.0)
            act(K, l2[:, :], l1[:, :], AF.Ln, [l1_b], [l2_b], bias=1.0)
            act(K, l1[:, :], l1[:, :], AF.Ln, [l1_b, par_b], [l1_b], scale=par[:, hd:hd + 1], bias=1.0)
            yield
            tt(K, "dve", l2[:, :], l2[:, :], l1[:, :], ALU.subtract, [l1_b, l2_b], [l2_b])
            act(K, kk[:, :], l2[:, :], AF.Exp, [l2_b], [kk_b], scale=-1.0)
            act(K, kk[:, :], kk[:, :], AF.Identity, [kk_b], [kk_b], scale=-1.0, bias=1.0)
        K.P.op("dve", lambda e: e.tensor_tensor_scan(out=cl[:, :], data0=msk, data1=l2[:, :], initial=0.0, op0=ALU.mult, op1=ALU.add),
               reads=[l2_b, K.cst_b], writes=[cl_b])
        tt(K, "dve", c3(dA), c3(cl), c3(cl)[:, :, 32:33].to_broadcast([128, 8, 64]), ALU.subtract, [cl_b], [dA_b])
        tt(K, "dve", c3(dD), c3(cl)[:, :, 63:64].to_broadcast([128, 8, 64]), c3(cl), ALU.subtract, [cl_b], [dD_b])
        yield
        act(K, eA[:, :], dA[:, :], AF.Exp, [dA_b], [eA_b], scale=-dsc)
        act(K, eB[:, :], dA[:, :], AF.Exp, [dA_b], [eB_b], scale=dsc)
        act(K, eC_[:, :], cl[:, :], AF.Exp, [cl_b], [eC_b_], scale=-dsc)
        act(K, eD[:, :], dD[:, :], AF.Exp, [dD_b], [eD_b], scale=-dsc)
        yield
        qsrc, qsrc_b = (qs[:, :], qs_b) if gla else (qsall[:, hd, :], qsall_b)
        stt(K, qt[:, :], qsrc, qscale, eA[:, :], ALU.mult, ALU.mult, [qsrc_b, eA_b], [qt_b])
        stt(K, qd[:, :], qsrc, qscale, eC_[:, :], ALU.mult, ALU.mult, [qsrc_b, eC_b_], [qd_b])
        tt(K, "dve", kt[:, :], kk[:, :], eB[:, :], ALU.mult, [kk_b, eB_b], [kt_b])
        tt(K, "dve", kd[:, :], kk[:, :], eD[:, :], ALU.mult, [kk_b, eD_b], [kd_b])
        yield

    def stageB(hd, nxt):
        s_ = hd % 2
        (qt, qt_b), (kt, kt_b), (kd, kd_b) = qtS[s_], ktS[s_], kdS[s_]
        eC_, eC_b_ = eCS[s_]
        stm, stm_b = stmS[s_]
        Sb8, Sb8_b = Sb8S[s_]
        S, S_b = S32[hd]
        step = lambda: next(nxt, None) if nxt is not None else None
        for tg in range(4):
            sl_ = slice(tg * 128, (tg + 1) * 128)
            mm(K, K.ps[PS_S][:, sl_], kt[:, sl_], qt[:, sl_], True, True, [kt_b, qt_b], K.psb[PS_S])
        tt(K, "dve", stm[:, :, :], K.ps[PS_S][:, :].rearrange("p (g i) -> p g i", i=128), bd4, ALU.mult, [K.psb[PS_S], K.cst_b], [stm_b])
        ptv = K.ps[PS_T][:, 0:256].bitcast(BF16)
        for tg in range(4):
            tr(K, ptv[:, tg * 128:(tg + 1) * 128], kd[:, tg * 128:(tg + 1) * 128], K.identb[:, :], [kd_b], K.psb[PS_T])
        cp(K, "act", kdt[:, :, :], ptv.rearrange("p (g d) -> p g d", d=128), [K.psb[PS_T]], [kdt_b])
        cp(K, "act", Sb8[:, 0, :], S[:, :], [S_b], [Sb8_b])
        CPB = 512 // DV
        NHALF = 8 // (2 * CPB)

        def kvloc(c):
            cc = c % (2 * CPB)
            return PS_KV[cc % 2], slice((cc // 2) * DV, (cc // 2 + 1) * DV)

        for hf in range(NHALF):
            cr = range(hf * 2 * CPB, (hf + 1) * 2 * CPB)
            for c in cr:
                tg, ci = c // 2, c % 2
                rows = slice(ci * 64, (ci + 1) * 64)
                kb, ks = kvloc(c)
                mm(K, K.ps[kb][:, ks], kdt[rows, tg, :], vt[rows, tg, hd * DV:(hd + 1) * DV], True, True, [kdt_b, vt_b], K.psb[kb])
            if hf == 0:
                step()
            for c in cr:
                kb, ks = kvloc(c)
                src, src_b = (S[:, :], S_b) if c == 0 else (Sall[:, c - 1, :], Sall_b)
                dst, dst_b = (S[:, :], S_b) if c == 7 else (Sall[:, c, :], Sall_b)
                stt(K, dst, src, eC_[:, c * 64 + 63:c * 64 + 64], K.ps[kb][:, ks], ALU.mult, ALU.add,
                    [src_b, eC_b_, K.psb[kb]], [dst_b])
        cp(K, "act", Sb8[:, 1:8, :], Sall[:, :, :], [Sall_b], [Sb8_b])
        step()
        step()

    def stageC(hd):
        s_ = hd % 2
        qd, qd_b = qdS[s_]
        stm, stm_b = stmS[s_]
        Sb8, Sb8_b = Sb8S[s_]
        PO = POS[s_]
        for tg in range(4):
            sl_ = slice(tg * 128, (tg + 1) * 128)
            for ec in range(DVC):
                mm(K, K.ps[PO[ec]][:, sl_], vt[:, tg, hd * DV + ec * 128:hd * DV + (ec + 1) * 128], stm[:, tg, :], True, False,
                   [vt_b, stm_b], K.psb[PO[ec]])
            for ci in range(2):
                c = tg * 2 + ci
                cs = slice(c * 64, (c + 1) * 64)
                for ec in range(DVC):
                    mm(K, K.ps[PO[ec]][:, cs], Sb8[:, c, ec * 128:(ec + 1) * 128], qd[:, cs], False, ci == 1, [Sb8_b, qd_b], K.psb[PO[ec]])
        for ec in range(DVC):
            act(K, sqo[:, ec, :], K.ps[PO[ec]][:, :], AF.Square, [K.psb[PO[ec]]], [sqo_b])
        bcast_stat(K, [sqo[:, ec, :] for ec in range(DVC)], [sqo_b], rs[:, :], rs_b, 1.0 / DV)
        for ec in range(DVC):
            stt(K, tmpo[:, :], K.ps[PO[ec]][:, :], col("gn", ec), rs[:, :], ALU.mult, ALU.mult, [K.psb[PO[ec]], cols_b, rs_b], [tmpo_b])
            tt(K, "dve", og[:, hd * DVC + ec, :], tmpo[:, :], gate[:, hd * DVC + ec, :], ALU.mult, [tmpo_b, gate_b], [og_b])

    xp = XPipe(K, ph, cols[:, off["g"]:off["g"] + 8], cols_b, xt, xt_b, h, h_b, sq, sq_b, rstd, rstd_b)
    xp.load(0)
    xp.square()
    xp.make_h()
    for t in range(K.NT):
        nxt = t + 1 < K.NT
        if nxt:
            xp.load(t + 1)
        if t % K.TPS == 0:
            for hd in range(NH):
                memset(K, "pool", S32[hd][0][:, :], 0.0, [S32[hd][1]])
        if gla:
            pb = bank(K, NB)
            for d in range(8):
                mm(K, K.ps[pb][0:16, :], win[:, d, 3072:3088], h[:, d, :], d == 0, d == 7, [win_b, h_b], K.psb[pb])
            cp(K, "act", glr[:, :], K.ps[pb][0:16, :], [K.psb[pb]], [glr_b])
        if not gla:
            for hd in range(NH):
                pb = bank(K, NB)
                for d in range(8):
                    mm(K, K.ps[pb][:, :], win[:, d, QOFF + hd * 128:QOFF + (hd + 1) * 128], h[:, d, :], d == 0, d == 7, [win_b, h_b], K.psb[pb])
                act(K, qsall[:, hd, :], K.ps[pb][:, :], AF.Silu, [K.psb[pb]], [qsall_b])
        for c in range(8):
            pb = bank(K, NB)
            for d in range(8):
                mm(K, K.ps[pb][:, :], win[:, d, GOFF + c * 128:GOFF + (c + 1) * 128], h[:, d, :], d == 0, d == 7, [win_b, h_b], K.psb[pb])
            act(K, gate[:, c, :], K.ps[pb][:, :], AF.Silu, [K.psb[pb]], [gate_b])
        g0 = pre(0)
        next(g0)
        for tg in range(4):
            for half in range(2):
                pb = bank(K, NB)
                for d in range(8):
                    mm(K, K.ps[pb][:, :], h[:, d, tg * 128:(tg + 1) * 128], win[:, d, VOFF + half * 512:VOFF + (half + 1) * 512], d == 0, d == 7,
                       [win_b, h_b], K.psb[pb])
                cp(K, "act" if half else "dve", vt[:, tg, half * 512:(half + 1) * 512], K.ps[pb][:, :], [K.psb[pb]], [vt_b])
            next(g0, None)
        for _ in g0:
            pass
        pend = pre(1)
        stageB(0, pend)
        for hd in range(NH):
            if hd + 1 < NH:
                for _ in pend:
                    pass
            stageC(hd)
            if hd + 1 < NH:
                pend = pre(hd + 2) if hd + 2 < NH else None
                stageB(hd + 1, pend)
        if nxt:
            xp.square()
            xp.make_h()
        xp.out_proj(t, wout, wout_b, og, og_b)
    K.nb = 6
    ph.close()


PHASES = {
    "tin": phase_tin,
    "tout": lambda K: phase_fin(K, False),
    "fin": lambda K: phase_fin(K, True),
    "gla": lambda K: phase_lin(K, "gla"),
    "hg": lambda K: phase_lin(K, "hg"),
    "conf": phase_conf,
    "sgu": phase_sgu,
    "ffn0": lambda K: phase_ffn(K, 0),
    "ffn1": lambda K: phase_ffn(K, 1),
    "ffn2": lambda K: phase_ffn(K, 2),
    "ffn3": lambda K: phase_ffn(K, 3),
}
ALL_PHASES = ["tin", "gla", "ffn0", "conf", "ffn1", "sgu", "ffn2", "hg", "ffn3", "fin"]


def make_consts():
    c = np.zeros((128, 1024), np.float32)
    j = np.arange(128)[:, None]
    i = np.arange(128)[None, :]
    c[:, 0:128] = (j == i)
    c[:, 128:256] = (j <= i)
    c[:, 256:384] = (j <= i) & ((j // 64) == (i // 64))
    t = np.arange(512)[None, :]
    c[:, 384:896] = (t % 64 != 0)
    c[:, 896:1024] = 1.0
    return c


def build(phases, NSEQ, SEQ):
    NTOK = NSEQ * SEQ
    nc = bass.Bass("TRN2", target_bir_lowering=False)
    K = Ctx()
    K.nc = nc
    K.d = LazyInputs(nc)
    K.x = nc.dram_tensor("x", [NTOK, D], F32, kind="ExternalInput").ap()
    cst = nc.dram_tensor("cst", [128, 1024], F32, kind="ExternalInput").ap()
    K.out = nc.dram_tensor("out", [NTOK, D], F32, kind="ExternalOutput").ap()
    K.xT = nc.dram_tensor("xT", [D, NTOK], F32, kind="Internal").ap()
    K.NT = NTOK // TT
    K.TPS = SEQ // TT
    K.rr = 0
    K.nb = 6
    with ExitStack() as st:
        P = Prog(nc, st)
        K.P = P
        K.ld = P.dma_sem("ld")
        K.stx = P.dma_sem("stx")
        K.wld = P.dma_sem("wld")
        K.pld = P.dma_sem("pld")
        K.rl = [P.dma_sem(f"rl{i}") for i in range(4)]
        K.rs = [P.dma_sem(f"rs{i}") for i in range(4)]
        K.cstf = st.enter_context(nc.sbuf_tensor("cstf", [128, 1024], F32))
        K.identb = st.enter_context(nc.sbuf_tensor("identb", [128, 128], BF16))
        K.onesb = st.enter_context(nc.sbuf_tensor("onesb", [128, 128], BF16))
        K.cst_b = Buf()
        K.ps = [st.enter_context(nc.psum_tensor(f"ps{i}", [128, 512], F32)) for i in range(8)]
        K.psb = [Buf() for _ in range(8)]
        P.dma([("sp", K.cstf[:, :], cst)], K.pld, writes=[K.cst_b])
        cp(K, "dve", K.identb[:, :], K.cstf[:, 0:128], [K.cst_b], [K.cst_b])
        cp(K, "dve", K.onesb[:, :], K.cstf[:, 896:1024], [K.cst_b], [K.cst_b])
        for p in phases:
            PHASES[p](K)
        for k, v in P.cnt.items():
            if v:
                P.ops["sp"].append(lambda e, sem=P.semh[k], v=v: e.wait_ge(sem, v))
        P.flush()
        K.ninstr = P.ninstr
    return nc, K


def run(phases, x, inputs, NSEQ, SEQ, ncores, trace=False):
    nc, K = build(phases, NSEQ, SEQ)
    cst = make_consts()
    in_maps = []
    for i in range(ncores):
        m = {name: np.ascontiguousarray(inputs[name], dtype=np.float32) for name in K.d.keys()}
        m["x"] = np.ascontiguousarray(x[i])
        m["cst"] = cst
        in_maps.append(m)
    res = run_bass_kernel_spmd(nc, in_maps, core_ids=list(range(ncores)), trace=trace)
    return np.stack([res.results[i]["out"] for i in range(ncores)]), res


def kernel(**inputs):
    x = np.asarray(inputs["x"], dtype=np.float32)
    B, S, _ = x.shape
    nseq = B // NCORES
    xs = x.reshape(NCORES, nseq * S, D)
    out, _ = run(ALL_PHASES, xs, inputs, nseq, S, NCORES)
    return out.reshape(B, S, D).astype(np.float32)
```
